# Optimizing a Trainium2 kernel written in Bass

```python
import math
import jax, jax.numpy as jnp
from jax import lax
import numpy as np

D_MODEL = 1024
BATCH = 8
SEQ = 8192
DEPTH = 2
DEC_BATCH = 8
DEC_SEQ = 2048
PAST_LEN = 128

N_DIFF_HEADS = 4
DIFF_QK_DIM = 64
DIFF_V_DIM = 128
N_MLA_HEADS = 4
MLA_Q_RANK = 256
MLA_KV_RANK = 256
MLA_NOPE_DIM = 64
MLA_ROPE_DIM = 32
MLA_QK_DIM = MLA_NOPE_DIM + MLA_ROPE_DIM
MLA_V_DIM = 128
ROPE_THETA = 10000.0
DIFF_Q_W = N_DIFF_HEADS * 2 * DIFF_QK_DIM
DIFF_V_W = N_DIFF_HEADS * DIFF_V_DIM
MIX_WIDTH = DIFF_V_W + N_MLA_HEADS * MLA_V_DIM
IN_SPLITS = [DIFF_Q_W, 2 * DIFF_Q_W, 2 * DIFF_Q_W + DIFF_V_W,
             2 * DIFF_Q_W + DIFF_V_W + MLA_Q_RANK,
             2 * DIFF_Q_W + DIFF_V_W + MLA_Q_RANK + MLA_KV_RANK]
IN_WIDTH = IN_SPLITS[-1] + MLA_ROPE_DIM
PEER_HEADS = 8
PEER_N_KEYS = 128
PEER_N_EXPERTS = PEER_N_KEYS * PEER_N_KEYS
PEER_KEY_DIM = 256
PEER_HALF = PEER_KEY_DIM // 2
PEER_TOPK = 16
TOKEN_CHUNK = 128
Q_BLOCK = 128
EPS = 1e-6

kernel_name = "hybrid_diffattn_mla_peer_encoder"


def rmsnorm(x, g):
    x32 = x.astype(jnp.float32)
    y = x32 * lax.rsqrt(jnp.mean(x32 * x32, axis=-1, keepdims=True) + EPS)
    return (y * g.astype(jnp.float32)).astype(x.dtype)


def rope(x, pos):
    half = x.shape[-1] // 2
    inv = ROPE_THETA ** (-jnp.arange(half, dtype=jnp.float32) / half)
    ang = pos.astype(jnp.float32)[:, None] * inv[None, :]
    cos = jnp.cos(ang)[None, :, None, :]
    sin = jnp.sin(ang)[None, :, None, :]
    x1 = x[..., :half].astype(jnp.float32)
    x2 = x[..., half:].astype(jnp.float32)
    return jnp.concatenate([x1 * cos - x2 * sin, x2 * cos + x1 * sin], axis=-1).astype(x.dtype)


def to_blocks(q):
    b, s = q.shape[:2]
    return jnp.moveaxis(q.reshape((b, s // Q_BLOCK, Q_BLOCK) + q.shape[2:]), 1, 0)


def from_blocks(o):
    nb, b, qb = o.shape[:3]
    return jnp.moveaxis(o, 0, 1).reshape((b, nb * qb) + o.shape[3:])


def alibi_slopes(n):
    return 2.0 ** (-8.0 * jnp.arange(1, n + 1, dtype=jnp.float32) / n)


def diff_attention(q, k, v, lam, slopes):
    s_len = k.shape[1]
    pos = jnp.arange(s_len)
    scale = DIFF_QK_DIM ** -0.5

    def block(args):
        qb, pq = args
        s = jnp.einsum('bqhjd,bshjd->bhjqs', qb, k).astype(jnp.float32) * scale
        dist = jnp.abs(pq[:, None] - pos[None, :]).astype(jnp.float32)
        s = s - slopes[None, :, None, None, None] * dist[None, None, None]
        p = jax.nn.softmax(s, axis=-1)
        a = p[:, :, 0] - lam * p[:, :, 1]
        return jnp.einsum('bhqs,bshe->bqhe', a.astype(v.dtype), v)

    o = lax.map(block, (to_blocks(q), pos.reshape(-1, Q_BLOCK)))
    return from_blocks(o)


def mla_attention(q, k, v):
    scale = MLA_QK_DIM ** -0.5

    def block(qb):
        s = jnp.einsum('bqhd,bshd->bhqs', qb, k).astype(jnp.float32) * scale
        p = jax.nn.softmax(s, axis=-1)
        return jnp.einsum('bhqs,bshe->bqhe', p.astype(v.dtype), v)

    return from_blocks(lax.map(block, to_blocks(q)))


def peer(x, w_q, key1, key2, u, v):
    b, s, d = x.shape
    xt = x.reshape(-1, TOKEN_CHUNK, d)

    def chunk(xc):
        c = xc.shape[0]
        q = (xc @ w_q).reshape(c, PEER_HEADS, 2, PEER_HALF)
        s1 = jnp.einsum('chd,hkd->chk', q[:, :, 0], key1).astype(jnp.float32)
        s2 = jnp.einsum('chd,hkd->chk', q[:, :, 1], key2).astype(jnp.float32)
        v1, i1 = lax.top_k(s1, PEER_TOPK)
        v2, i2 = lax.top_k(s2, PEER_TOPK)
        cand = (v1[..., :, None] + v2[..., None, :]).reshape(c, PEER_HEADS, PEER_TOPK * PEER_TOPK)
        cidx = (i1[..., :, None] * PEER_N_KEYS + i2[..., None, :]).reshape(c, PEER_HEADS, PEER_TOPK * PEER_TOPK)
        sc, sel = lax.top_k(cand, PEER_TOPK)
        eidx = jnp.take_along_axis(cidx, sel, axis=-1)
        g = jax.nn.softmax(sc, axis=-1)
        ue = u[eidx]
        h = jax.nn.gelu(jnp.einsum('chkd,cd->chk', ue, xc).astype(jnp.float32), approximate=False)
        ve = v[eidx]
        return jnp.einsum('chk,chkd->cd', (g * h).astype(xc.dtype), ve)

    return lax.map(chunk, xt).reshape(b, s, d)


def layer(x, l, norm_mix_g, w_in, diff_q_norm_g, diff_k_norm_g, lam_q1, lam_k1, lam_q2, lam_k2,
          diff_subln_g, mla_q_latent_g, mla_w_uq, mla_kv_latent_g, mla_w_ukv, mla_q_norm_g,
          mla_k_norm_g, w_out, norm_ffn_g, peer_w_q, peer_key1, peer_key2, peer_u, peer_v):
    b, s, _ = x.shape
    pos = jnp.arange(s)
    h = rmsnorm(x, norm_mix_g) @ w_in
    dq, dk, dv, cq, ckv, kr = jnp.split(h, IN_SPLITS, axis=-1)

    dq = rmsnorm(dq.reshape(b, s, N_DIFF_HEADS, 2, DIFF_QK_DIM), diff_q_norm_g)
    dk = rmsnorm(dk.reshape(b, s, N_DIFF_HEADS, 2, DIFF_QK_DIM), diff_k_norm_g)
    dv = dv.reshape(b, s, N_DIFF_HEADS, DIFF_V_DIM)
    lam_init = 0.8 - 0.6 * math.exp(-0.3 * l)
    lam = (jnp.exp(jnp.sum(lam_q1.astype(jnp.float32) * lam_k1.astype(jnp.float32)))
           - jnp.exp(jnp.sum(lam_q2.astype(jnp.float32) * lam_k2.astype(jnp.float32))) + lam_init)
    od = diff_attention(dq, dk, dv, lam, alibi_slopes(N_DIFF_HEADS))
    od = (rmsnorm(od, diff_subln_g) * (1.0 - lam_init)).reshape(b, s, DIFF_V_W)

    mq = (rmsnorm(cq, mla_q_latent_g) @ mla_w_uq).reshape(b, s, N_MLA_HEADS, MLA_QK_DIM)
    kv = (rmsnorm(ckv, mla_kv_latent_g) @ mla_w_ukv).reshape(b, s, N_MLA_HEADS, MLA_NOPE_DIM + MLA_V_DIM)
    k_nope, mv = kv[..., :MLA_NOPE_DIM], kv[..., MLA_NOPE_DIM:]
    k_rope = jnp.broadcast_to(kr[:, :, None, :], (b, s, N_MLA_HEADS, MLA_ROPE_DIM))
    mk = jnp.concatenate([k_nope, k_rope], axis=-1)
    mq = rmsnorm(mq, mla_q_norm_g)
    mk = rmsnorm(mk, mla_k_norm_g)
    mq = jnp.concatenate([mq[..., :MLA_NOPE_DIM], rope(mq[..., MLA_NOPE_DIM:], pos)], axis=-1)
    mk = jnp.concatenate([mk[..., :MLA_NOPE_DIM], rope(mk[..., MLA_NOPE_DIM:], pos)], axis=-1)
    om = mla_attention(mq, mk, mv).reshape(b, s, N_MLA_HEADS * MLA_V_DIM)

    x = x + jnp.concatenate([od, om], axis=-1) @ w_out
    x = x + peer(rmsnorm(x, norm_ffn_g), peer_w_q, peer_key1, peer_key2, peer_u, peer_v)
    return x


def setup_inputs(seed: int = 0) -> dict:
    key = jax.random.key(seed)
    ks = jax.random.split(key, 26)
    f = jnp.float32

    def nrm(k, shape, scale):
        return jax.random.normal(k, shape, f) * scale

    def gain(k, shape):
        return 1.0 + 0.02 * jax.random.normal(k, shape, f)

    L, D = DEPTH, D_MODEL
    return {
        "x_prompt": nrm(ks[0], (BATCH, SEQ, D), 1.0),
        "x_sample": nrm(ks[1], (DEC_BATCH, DEC_SEQ, D), 1.0),
        "norm_mix_g": gain(ks[2], (L, D)),
        "w_in": nrm(ks[3], (L, D, IN_WIDTH), D ** -0.5),
        "diff_q_norm_g": gain(ks[4], (L, DIFF_QK_DIM)),
        "diff_k_norm_g": gain(ks[5], (L, DIFF_QK_DIM)),
        "lam_q1": nrm(ks[6], (L, DIFF_QK_DIM), 0.1),
        "lam_k1": nrm(ks[7], (L, DIFF_QK_DIM), 0.1),
        "lam_q2": nrm(ks[8], (L, DIFF_QK_DIM), 0.1),
        "lam_k2": nrm(ks[9], (L, DIFF_QK_DIM), 0.1),
        "diff_subln_g": gain(ks[10], (L, DIFF_V_DIM)),
        "mla_q_latent_g": gain(ks[11], (L, MLA_Q_RANK)),
        "mla_w_uq": nrm(ks[12], (L, MLA_Q_RANK, N_MLA_HEADS * MLA_QK_DIM), MLA_Q_RANK ** -0.5),
        "mla_kv_latent_g": gain(ks[13], (L, MLA_KV_RANK)),
        "mla_w_ukv": nrm(ks[14], (L, MLA_KV_RANK, N_MLA_HEADS * (MLA_NOPE_DIM + MLA_V_DIM)), MLA_KV_RANK ** -0.5),
        "mla_q_norm_g": gain(ks[15], (L, MLA_QK_DIM)),
        "mla_k_norm_g": gain(ks[16], (L, MLA_QK_DIM)),
        "w_out": nrm(ks[17], (L, MIX_WIDTH, D), MIX_WIDTH ** -0.5),
        "norm_ffn_g": gain(ks[18], (L, D)),
        "peer_w_q": nrm(ks[19], (L, D, PEER_HEADS * PEER_KEY_DIM), D ** -0.5),
        "peer_key1": nrm(ks[20], (L, PEER_HEADS, PEER_N_KEYS, PEER_HALF), PEER_HALF ** -0.5),
        "peer_key2": nrm(ks[21], (L, PEER_HEADS, PEER_N_KEYS, PEER_HALF), PEER_HALF ** -0.5),
        "peer_u": nrm(ks[22], (L, PEER_N_EXPERTS, D), D ** -0.5),
        "peer_v": nrm(ks[23], (L, PEER_N_EXPERTS, D), D ** -0.5),
    }


def reference(x_prompt, x_sample, norm_mix_g, w_in, diff_q_norm_g, diff_k_norm_g, lam_q1, lam_k1,
              lam_q2, lam_k2, diff_subln_g, mla_q_latent_g, mla_w_uq, mla_kv_latent_g, mla_w_ukv,
              mla_q_norm_g, mla_k_norm_g, w_out, norm_ffn_g, peer_w_q, peer_key1, peer_key2,
              peer_u, peer_v):
    y_prompt = x_prompt
    y_sample = x_sample
    for l in range(DEPTH):
        p = (norm_mix_g[l], w_in[l], diff_q_norm_g[l], diff_k_norm_g[l], lam_q1[l], lam_k1[l],
             lam_q2[l], lam_k2[l], diff_subln_g[l], mla_q_latent_g[l], mla_w_uq[l],
             mla_kv_latent_g[l], mla_w_ukv[l], mla_q_norm_g[l], mla_k_norm_g[l], w_out[l],
             norm_ffn_g[l], peer_w_q[l], peer_key1[l], peer_key2[l], peer_u[l], peer_v[l])
        y_prompt = layer(y_prompt, l, *p)
        y_sample = layer(y_sample, l, *p)
    return (y_prompt, y_sample)
```

```python
import contextlib
import math

import ml_dtypes
import numpy as np

import concourse.bass as bass
import concourse.mybir as mybir
from concourse.bass_utils import run_bass_kernel_spmd

F32 = mybir.dt.float32
BF16 = mybir.dt.bfloat16
U32 = mybir.dt.uint32
I32 = mybir.dt.int32
AF = mybir.ActivationFunctionType
ALU = mybir.AluOpType
AX = mybir.AxisListType

D = 1024
DEPTH = 2
SP_FULL = 8192
SS_FULL = 2048
INW = 2080
EPS = 1e-6
NEG = -3.0e38
PEER_DENSE = True
KPAD = 96
BAND = 70.0
FRONT_PER_CHUNK = 2
MULT_ENG = "pool"
GAP = "gap"
ONEHOT_B_ENG = "dve"
PHASES = "abc"


class Buf:
    __slots__ = ("name", "w", "r", "dsem", "dcount", "slot", "gen")

    def __init__(self, name):
        self.name = name
        self.w = None
        self.r = {}
        self.dsem = None
        self.dcount = 0
        self.slot = None
        self.gen = -1


class Eng:
    def __init__(self, name, obj, sem):
        self.name = name
        self.obj = obj
        self.sem = sem
        self.count = 0
        self.seen = {}


class Prog:
    def __init__(self, nc, es):
        self.nc = nc
        self.es = es
        self.eng = {}
        for n, attr in (("pe", "tensor"), ("act", "scalar"), ("dve", "vector"),
                        ("pool", "gpsimd"), ("sp", "sync")):
            sem = es.enter_context(nc.semaphore("sem_" + n))
            self.eng[n] = Eng(n, getattr(nc, attr), sem)
        self.dma_events = {}
        self.nsem = 5
        self.ninstr = 0
        self.free_slots = []
        self.phase_slots = []
        self.gen = 0

    def buf(self, name):
        return Buf(name)

    def _wait(self, e, ev):
        sem, val = ev
        k = id(sem)
        if e.seen.get(k, 0) >= val:
            return
        e.obj.wait_ge(sem, val)
        e.seen[k] = val
        self.ninstr += 1

    def _deps(self, e, reads, writes, skip_sem=None):
        best = {}

        def add(ev):
            k = id(ev[0])
            if k not in best or best[k][1] < ev[1]:
                best[k] = ev

        for b in reads:
            if b.w is not None:
                add(b.w)
        for b in writes:
            if b.w is not None and not (skip_sem is not None and b.w[0] is skip_sem):
                add(b.w)
            for ev in b.r.values():
                add(ev)
        for k, ev in best.items():
            if e.name == "pe" and ev[0] is e.sem:
                continue
            self._wait(e, ev)

    def _record(self, ev, reads, writes):
        k = id(ev[0])
        for b in reads:
            b.r[k] = ev
        for b in writes:
            b.w = ev
            b.r = {}

    def op(self, en, emit, reads=(), writes=()):
        e = self.eng[en]
        self._deps(e, reads, writes)
        ins = emit(e.obj)
        e.count += 1
        ins.then_inc(e.sem, 1)
        self.ninstr += 1
        ev = (e.sem, e.count)
        self._record(ev, reads, writes)
        return ev

    def dma(self, qn, emit, sem_buf, reads=(), writes=()):
        e = self.eng[qn]
        if sem_buf.dsem is None or sem_buf.gen != self.gen:
            sem_buf.gen = self.gen
            if self.free_slots:
                slot = self.free_slots.pop()
            else:
                slot = [self.es.enter_context(self.nc.semaphore("dsem_%d" % self.nsem)), 0]
                self.nsem += 1
            self.phase_slots.append(slot)
            sem_buf.slot = slot
            sem_buf.dsem = slot[0]
            sem_buf.dcount = slot[1]
        self._deps(e, reads, writes, skip_sem=sem_buf.dsem)
        ins = emit(e.obj)
        sem_buf.dcount += 16
        ins.then_inc(sem_buf.dsem, 16)
        sem_buf.slot[1] = sem_buf.dcount
        self.ninstr += 1
        ev = (sem_buf.dsem, sem_buf.dcount)
        self.dma_events[id(sem_buf.dsem)] = ev
        self._record(ev, reads, writes)
        return ev

    def barrier(self, release=True):
        engs = list(self.eng.values())
        for e in engs:
            for f in engs:
                if f is not e and f.count > 0:
                    self._wait(e, (f.sem, f.count))
            for ev in self.dma_events.values():
                self._wait(e, ev)
        self.dma_events = {}
        self.free_slots.extend(self.phase_slots)
        self.phase_slots = []
        self.gen += 1


def build_program(SP, SS, depth=DEPTH, debug=False):
    nc = bass.Bass("TRN2", target_bir_lowering=False)
    SM = max(SP, SS)

    def din(name, shape, dt=F32):
        return nc.dram_tensor(name, list(shape), dt, kind="ExternalInput").ap()

    def dscr(name, shape, dt):
        kind = "ExternalOutput" if debug else "Internal"
        return nc.dram_tensor(name, list(shape), dt, kind=kind).ap()

    I = {}
    I["xp"] = din("xp", [SP, D])
    I["xs"] = din("xs", [SS, D])
    I["norm_mix_g"] = din("norm_mix_g", [depth, D])
    I["w_in"] = din("w_in", [depth, D, INW])
    I["diff_q_norm_g"] = din("diff_q_norm_g", [depth, 64])
    I["diff_k_norm_g"] = din("diff_k_norm_g", [depth, 64])
    for n in ("lam_q1", "lam_k1", "lam_q2", "lam_k2"):
        I[n] = din(n, [depth, 64])
    I["diff_subln_g"] = din("diff_subln_g", [depth, 128])
    I["mla_q_latent_g"] = din("mla_q_latent_g", [depth, 256])
    I["mla_w_uq"] = din("mla_w_uq", [depth, 256, 384])
    I["mla_kv_latent_g"] = din("mla_kv_latent_g", [depth, 256])
    I["mla_w_ukv"] = din("mla_w_ukv", [depth, 256, 768])
    I["mla_q_norm_g"] = din("mla_q_norm_g", [depth, 96])
    I["mla_k_norm_g"] = din("mla_k_norm_g", [depth, 96])
    I["w_out"] = din("w_out", [depth, D, D])
    I["norm_ffn_g"] = din("norm_ffn_g", [depth, D])
    I["peer_w_q"] = din("peer_w_q", [depth, D, 2048])
    I["peer_key1T"] = din("peer_key1T", [depth, 8, 128, 128])
    I["peer_key2T"] = din("peer_key2T", [depth, 8, 128, 128])
    I["peer_u"] = din("peer_u", [depth, 16384, D])
    I["peer_v"] = din("peer_v", [depth, 16384, D])
    I["ident"] = din("ident", [128, 128], BF16)
    I["identf"] = din("identf", [128, 128], F32)
    I["iota16"] = din("iota16", [128, 16], F32)
    I["ropecs"] = din("ropecs", [SM, 32], F32)
    I["augq"] = din("augq", [4, 8, SM], BF16)
    I["augk"] = din("augk", [4, 4, SM], BF16)
    I["dbias"] = din("dbias", [4, 128, 4, 512], F32)
    I["iota128"] = din("iota128", [128, 128], F32)
    UT = dscr("peerUT", [128, 128, D], BF16)
    VB = dscr("peerVB", [128, 128, D], BF16)

    yp = nc.dram_tensor("yp", [SP, D], F32, kind="ExternalOutput").ap()
    ys = nc.dram_tensor("ys", [SS, D], F32, kind="ExternalOutput").ap()

    seqs = []
    for nm, S, xin, yout in (("p", SP, I["xp"], yp), ("s", SS, I["xs"], ys)):
        seqs.append(dict(
            nm=nm, S=S, xin=xin, yout=yout,
            QdT=dscr("QdT" + nm, [4, 2, 64, S], BF16),
            KdT=dscr("KdT" + nm, [4, 2, 64, S], BF16),
            Vd=dscr("Vd" + nm, [S, 512], BF16),
            QmT=dscr("QmT" + nm, [384, S], BF16),
            KmT=dscr("KmT" + nm, [384, S], BF16),
            Vm=dscr("Vm" + nm, [S, 512], BF16),
            mix=dscr("mix" + nm, [S, D], BF16),
            xmid=dscr("xmid" + nm, [S, D], F32),
        ))

    es = contextlib.ExitStack()
    with es:
        P = Prog(nc, es)

        def sbp(name, shape, dt):
            return es.enter_context(nc.sbuf_tensor(name, list(shape), dt))

        ident = sbp("ident_sb", [128, 128], BF16)
        identf = sbp("identf_sb", [128, 128], F32)
        iota16 = sbp("iota16_sb", [128, 16], F32)
        iota128 = sbp("iota128_sb", [128, 128], F32)
        neghalf = sbp("neghalf", [128, 16], F32)
        B_const = P.buf("const")
        P.dma("sp", lambda q: q.dma_start(out=ident[:], in_=I["ident"][:, :]), B_const, writes=[B_const])
        P.dma("sp", lambda q: q.dma_start(out=identf[:], in_=I["identf"][:, :]), B_const, writes=[B_const])
        P.dma("sp", lambda q: q.dma_start(out=iota16[:], in_=I["iota16"][:, :]), B_const, writes=[B_const])
        P.dma("sp", lambda q: q.dma_start(out=iota128[:], in_=I["iota128"][:, :]), B_const, writes=[B_const])
        B_nh = P.buf("neghalf")
        P.op("pool", lambda g: g.memset(neghalf[:], -0.5), writes=[B_nh])

        def rsqrt(out_ap, in_ap, n, Bout, Bin):
            P.op("pool", lambda g: g.tensor_tensor(out=out_ap, in0=in_ap, in1=neghalf[:, 0:n], op=ALU.pow),
                 reads=[Bin, B_nh], writes=[Bout])

        for l in range(depth):
            if "c" in PHASES and PEER_DENSE:
                prepass_tables(nc, P, I, l, UT, VB, ident, B_const)
                P.barrier(release=True)
            for sq in seqs:
                x_src = sq["xin"] if l == 0 else sq["xmid"]
                x_dst = sq["yout"] if l == depth - 1 else sq["xmid"]
                if "a" in PHASES:
                    phase_a(nc, P, I, sq, l, x_src, ident, B_const, rsqrt)
                    P.barrier(release=True)
                if "b" in PHASES:
                    phase_b(nc, P, I, sq, l, rsqrt, neghalf, B_nh)
                    P.barrier(release=True)
                if "c" in PHASES:
                    if PEER_DENSE:
                        phase_c_dense(nc, P, I, sq, l, x_src, x_dst, ident, identf, iota16, iota128, B_const, rsqrt, UT, VB)
                    else:
                        phase_c(nc, P, I, sq, l, x_src, x_dst, ident, identf, iota16, B_const, rsqrt)
                    P.barrier(release=True)
        build_program.last_ninstr = P.ninstr
    return nc


def phase_a(nc, P, I, sq, l, x_src, ident, B_const, rsqrt):
    S = sq["S"]
    NT = S // 128
    with contextlib.ExitStack() as es:
        def sb(name, shape, dt=F32):
            return es.enter_context(nc.sbuf_tensor("a" + str(l) + sq["nm"] + "_" + name, list(shape), dt))

        def ps(name, shape, dt=F32):
            return es.enter_context(nc.psum_tensor("a" + str(l) + sq["nm"] + "_" + name, list(shape), dt))

        win = sb("win", [128, 8, INW], BF16)
        wuq = sb("wuq", [128, 2, 384], BF16)
        wukv = sb("wukv", [128, 2, 768], BF16)
        gmix = sb("gmix", [128, 8], F32)
        gql = sb("gql", [128, 2], F32)
        gkvl = sb("gkvl", [128, 2], F32)
        gain_qk = sb("gain_qk", [128, 2, 64], F32)
        gain_m = sb("gain_m", [128, 2, 96], F32)
        es_w = contextlib.ExitStack()
        stg = [es_w.enter_context(nc.sbuf_tensor("a" + str(l) + sq["nm"] + "_stg" + str(i), [128, INW], F32)) for i in range(2)]
        B_win, B_wuq, B_wukv = P.buf("win"), P.buf("wuq"), P.buf("wukv")
        B_stg = [P.buf("stg0"), P.buf("stg1")]
        B_g = P.buf("gains")

        with nc.allow_non_contiguous_dma(reason="tiny gain vectors"):
            P.dma("sp", lambda q: q.dma_start(out=gmix[:], in_=I["norm_mix_g"][l].rearrange("(c p) -> p c", p=128)), B_g, writes=[B_g])
            P.dma("sp", lambda q: q.dma_start(out=gql[:], in_=I["mla_q_latent_g"][l].rearrange("(c p) -> p c", p=128)), B_g, writes=[B_g])
            P.dma("sp", lambda q: q.dma_start(out=gkvl[:], in_=I["mla_kv_latent_g"][l].rearrange("(c p) -> p c", p=128)), B_g, writes=[B_g])
        P.dma("sp", lambda q: q.dma_start(out=gain_qk[:, 0, :], in_=I["diff_q_norm_g"][l].partition_broadcast(128)), B_g, writes=[B_g])
        P.dma("sp", lambda q: q.dma_start(out=gain_qk[:, 1, :], in_=I["diff_k_norm_g"][l].partition_broadcast(128)), B_g, writes=[B_g])
        P.dma("sp", lambda q: q.dma_start(out=gain_m[:, 0, :], in_=I["mla_q_norm_g"][l].partition_broadcast(128)), B_g, writes=[B_g])
        P.dma("sp", lambda q: q.dma_start(out=gain_m[:, 1, :], in_=I["mla_k_norm_g"][l].partition_broadcast(128)), B_g, writes=[B_g])

        k = 0
        for c in range(8):
            b = k % 2
            P.dma("sp", lambda q, c=c, b=b: q.dma_start(out=stg[b][:, :], in_=I["w_in"][l, c * 128:(c + 1) * 128, :]),
                  B_stg[b], writes=[B_stg[b]])
            P.op("dve", lambda v, c=c, b=b: v.tensor_scalar(out=win[:, c, :], in0=stg[b][:, :], scalar1=gmix[:, c:c + 1],
                                                            scalar2=None, op0=ALU.mult),
                 reads=[B_stg[b], B_g], writes=[B_win])
            k += 1
        for c in range(2):
            b = k % 2
            P.dma("sp", lambda q, c=c, b=b: q.dma_start(out=stg[b][:, 0:384], in_=I["mla_w_uq"][l, c * 128:(c + 1) * 128, :]),
                  B_stg[b], writes=[B_stg[b]])
            P.op("dve", lambda v, c=c, b=b: v.tensor_scalar(out=wuq[:, c, :], in0=stg[b][:, 0:384], scalar1=gql[:, c:c + 1],
                                                            scalar2=None, op0=ALU.mult),
                 reads=[B_stg[b], B_g], writes=[B_wuq])
            k += 1
        for c in range(2):
            b = k % 2
            P.dma("sp", lambda q, c=c, b=b: q.dma_start(out=stg[b][:, 0:768], in_=I["mla_w_ukv"][l, c * 128:(c + 1) * 128, :]),
                  B_stg[b], writes=[B_stg[b]])
            P.op("dve", lambda v, c=c, b=b: v.tensor_scalar(out=wukv[:, c, :], in0=stg[b][:, 0:768], scalar1=gkvl[:, c:c + 1],
                                                            scalar2=None, op0=ALU.mult),
                 reads=[B_stg[b], B_g], writes=[B_wukv])
            k += 1

        P.barrier()
        es_w.close()
        xt = [sb("xt%d" % i, [128, D]) for i in range(2)]
        B_xt = [P.buf("xt0"), P.buf("xt1")]
        cs = [sb("cs%d" % i, [128, 32]) for i in range(3)]
        B_cs = [P.buf("cs0"), P.buf("cs1"), P.buf("cs2")]
        junk = sb("junk", [128, D]); B_junk = P.buf("junk")
        st = sb("st", [128, 64]); B_st = P.buf("st")
        xb = sb("xb", [128, D], BF16); B_xb = P.buf("xb")
        xT = sb("xT", [128, 8, 128], BF16); B_xT = P.buf("xT")
        sqt = sb("sqt", [128, D]); B_sqt = P.buf("sqt")
        tqk = sb("tqk", [128, 16, 64]); B_tqk = P.buf("tqk")
        qkn = sb("qkn", [128, D], BF16); B_qkn = P.buf("qkn")
        qkT = sb("qkT", [128, 8, 128], BF16); B_qkT = P.buf("qkT")
        vdb = sb("vdb", [128, 512], BF16); B_vdb = P.buf("vdb")
        latb = sb("latb", [128, 512], BF16); B_latb = P.buf("latb")
        latT = sb("latT", [128, 4, 128], BF16); B_latT = P.buf("latT")
        kvs = sb("kvs", [128, 4, 192]); B_kvs = P.buf("kvs")
        krs = sb("krs", [128, 32]); B_krs = P.buf("krs")
        qkms = [sb("qkm%d" % i, [128, 8, 96]) for i in range(2)]; B_qkms = [P.buf("qkm0"), P.buf("qkm1")]
        junk2 = sb("junk2", [128, 768]); B_junk2 = P.buf("junk2")
        sm2 = sb("sm2", [128, 32]); B_sm2 = P.buf("sm2")
        qkm2 = sb("qkm2", [128, 8, 96]); B_qkm2 = P.buf("qkm2")
        rt = sb("rt", [128, 4, 8, 16]); B_rt = P.buf("rt")
        qkmb = sb("qkmb", [128, 8, 96], BF16); B_qkmb = P.buf("qkmb")
        qkmT = sb("qkmT", [128, 6, 128], BF16); B_qkmT = P.buf("qkmT")
        vmb = sb("vmb", [128, 4, 128], BF16); B_vmb = P.buf("vmb")

        tp = ps("tp", [128, 1024], BF16); B_tp = P.buf("tp")
        hps = [ps("h%d" % g, [128, 512]) for g in range(5)]
        B_h = [P.buf("h%d" % g) for g in range(5)]
        mqp = ps("mqp", [128, 512]); B_mqp = P.buf("mqp")
        kvp = ps("kvp", [128, 512]); B_kvp = P.buf("kvp")

        SSQ, T0, RX, RX2A, RX2B, RX2C = 0, 1, 2, 3, 4, 5
        SSG, TG, RG, SC = 8, 24, 40, 8
        sm = sb("sm", [128, 64]); B_sm = P.buf("sm")

        def load(t):
            b = t % 2
            P.dma("sp", lambda q: q.dma_start(out=xt[b][:, :], in_=x_src[t * 128:(t + 1) * 128, :]), B_xt[b], writes=[B_xt[b]])
            P.dma("sp", lambda q: q.dma_start(out=cs[t % 3][:, :], in_=I["ropecs"][t * 128:(t + 1) * 128, :]), B_cs[t % 3], writes=[B_cs[t % 3]])

        tail_gen = [None]

        def pull():
            if tail_gen[0] is not None:
                next(tail_gen[0], None)

        def bop(en, emit, reads=(), writes=()):
            r = P.op(en, emit, reads=reads, writes=writes)
            if not (en == "pe" and any(w is B_tp for w in writes)):
                pull()
            return r

        load(0)
        for t in range(NT):
            b = t % 2
            if t + 1 < NT:
                load(t + 1)
            tok = slice(t * 128, (t + 1) * 128)
            qkm, B_qkm = qkms[b], B_qkms[b]
            X, BX = xt[b], B_xt[b]
            bop("act", lambda a: a.activation(out=junk[:, :], in_=X[:, :], func=AF.Square, accum_out=st[:, SSQ:SSQ + 1]),
                 reads=[BX], writes=[B_junk, B_st])
            bop("dve", lambda v: v.tensor_scalar(out=st[:, T0:T0 + 1], in0=st[:, SSQ:SSQ + 1], scalar1=1.0 / D, scalar2=EPS,
                                                  op0=ALU.mult, op1=ALU.add), reads=[B_st], writes=[B_st])
            rsqrt(st[:, RX:RX + 1], st[:, T0:T0 + 1], 1, B_st, B_st)
            bop("dve", lambda v: v.tensor_scalar(out=st[:, RX2A:RX2A + 1], in0=st[:, RX:RX + 1], scalar1=st[:, RX:RX + 1],
                                                  scalar2=1.0 / 64, op0=ALU.mult, op1=ALU.mult), reads=[B_st], writes=[B_st])
            bop("dve", lambda v: v.tensor_scalar(out=st[:, RX2B:RX2B + 1], in0=st[:, RX:RX + 1], scalar1=st[:, RX:RX + 1],
                                                  scalar2=1.0 / 256, op0=ALU.mult, op1=ALU.mult), reads=[B_st], writes=[B_st])
            rx = st[:, RX:RX + 1]
            bop("dve", lambda v: v.tensor_copy(out=xb[:, :], in_=X[:, :]), reads=[BX], writes=[B_xb])

            def tr8(pe, src=xb, n=8):
                ins = None
                for c in range(n):
                    ins = pe.transpose(out=tp[:, c * 128:(c + 1) * 128], in_=src[:, c * 128:(c + 1) * 128], identity=ident[:, :])
                return ins
            bop("pe", tr8, reads=[B_xb, B_const], writes=[B_tp])
            bop("act", lambda a: a.copy(out=xT[:, :, :].rearrange("p c t -> p (c t)"), in_=tp[:, :]), reads=[B_tp], writes=[B_xT])
            for g in range(5):
                n = 512 if g < 4 else 32

                def mm(pe, g=g, n=n):
                    ins = None
                    for c in range(8):
                        ins = pe.matmul(hps[g][:, 0:n], lhsT=xT[:, c, :], rhs=win[:, c, g * 512:g * 512 + n],
                                        start=(c == 0), stop=(c == 7))
                    return ins
                bop("pe", mm, reads=[B_xT, B_win], writes=[B_h[g]])
            bop("act", lambda a: a.activation(out=sqt[:, 0:512], in_=hps[0][:, :], func=AF.Square), reads=[B_h[0]], writes=[B_sqt])
            bop("act", lambda a: a.activation(out=sqt[:, 512:1024], in_=hps[1][:, :], func=AF.Square), reads=[B_h[1]], writes=[B_sqt])
            bop("dve", lambda v: v.tensor_reduce(out=st[:, SSG:SSG + 16], in_=sqt[:, :].rearrange("p (g d) -> p g d", d=64),
                                                  axis=AX.X, op=ALU.add), reads=[B_sqt], writes=[B_st])
            bop("dve", lambda v: v.tensor_scalar(out=st[:, TG:TG + 16], in0=st[:, SSG:SSG + 16], scalar1=st[:, RX2A:RX2A + 1],
                                                  scalar2=EPS, op0=ALU.mult, op1=ALU.add), reads=[B_st], writes=[B_st])
            rsqrt(st[:, RG:RG + 16], st[:, TG:TG + 16], 16, B_st, B_st)
            bop("dve", lambda v: v.tensor_scalar(out=st[:, SC:SC + 8], in0=st[:, RG:RG + 8], scalar1=rx, scalar2=0.125,
                                                  op0=ALU.mult, op1=ALU.mult), reads=[B_st], writes=[B_st])
            bop("dve", lambda v: v.tensor_scalar(out=st[:, SC + 8:SC + 16], in0=st[:, RG + 8:RG + 16], scalar1=rx, scalar2=None,
                                                  op0=ALU.mult), reads=[B_st], writes=[B_st])
            for half in range(2):
                bop("dve", lambda v, half=half: v.tensor_tensor(
                    out=tqk[:, half * 8:(half + 1) * 8, :], in0=hps[half][:, :].rearrange("p (g d) -> p g d", d=64),
                    in1=st[:, SC + half * 8:SC + half * 8 + 8].unsqueeze(2).to_broadcast([128, 8, 64]), op=ALU.mult),
                    reads=[B_h[half], B_st], writes=[B_tqk])
                bop("dve", lambda v, half=half: v.tensor_tensor(
                    out=qkn[:, half * 512:(half + 1) * 512].rearrange("p (g d) -> p g d", d=64),
                    in0=tqk[:, half * 8:(half + 1) * 8, :],
                    in1=gain_qk[:, half:half + 1, :].to_broadcast([128, 8, 64]), op=ALU.mult),
                    reads=[B_tqk, B_g], writes=[B_qkn])
            bop("pe", lambda pe: tr8(pe, src=qkn), reads=[B_qkn, B_const], writes=[B_tp])
            bop("act", lambda a: a.copy(out=qkT[:, :, :].rearrange("p c t -> p (c t)"), in_=tp[:, :]), reads=[B_tp], writes=[B_qkT])
            P.dma("sp", lambda q: q.dma_start(out=sq["QdT"].rearrange("h j d s -> (j d) h s")[:, :, tok], in_=qkT[:, 0:4, :]),
                  B_qkT, reads=[B_qkT])
            P.dma("sp", lambda q: q.dma_start(out=sq["KdT"].rearrange("h j d s -> (j d) h s")[:, :, tok], in_=qkT[:, 4:8, :]),
                  B_qkT, reads=[B_qkT])
            bop("act", lambda a: a.activation(out=vdb[:, :], in_=hps[2][:, :], func=AF.Copy, scale=rx), reads=[B_h[2], B_st], writes=[B_vdb])
            P.dma("sp", lambda q: q.dma_start(out=sq["Vd"][tok, :], in_=vdb[:, :]), B_vdb, reads=[B_vdb])
            bop("act", lambda a: a.activation(out=junk[:, 0:256], in_=hps[3][:, 0:256], func=AF.Square, accum_out=sm[:, 0:1]),
                 reads=[B_h[3]], writes=[B_junk, B_sm])
            bop("act", lambda a: a.activation(out=junk[:, 256:512], in_=hps[3][:, 256:512], func=AF.Square, accum_out=sm[:, 1:2]),
                 reads=[B_h[3]], writes=[B_junk, B_sm])
            bop("act", lambda a: a.copy(out=latb[:, :], in_=hps[3][:, :]), reads=[B_h[3]], writes=[B_latb])
            bop("pe", lambda pe: tr8(pe, src=latb, n=4), reads=[B_latb, B_const], writes=[B_tp])
            bop("act", lambda a: a.copy(out=latT[:, :, :].rearrange("p c t -> p (c t)"), in_=tp[:, 0:512]), reads=[B_tp], writes=[B_latT])

            def mm_q(pe):
                ins = None
                for c in range(2):
                    ins = pe.matmul(mqp[:, 0:384], lhsT=latT[:, c, :], rhs=wuq[:, c, :], start=(c == 0), stop=(c == 1))
                return ins
            bop("pe", mm_q, reads=[B_latT, B_wuq], writes=[B_mqp])

            def mm_kv(pe, lo, n, dst):
                ins = None
                for c in range(2):
                    ins = pe.matmul(dst[:, 0:n], lhsT=latT[:, 2 + c, :], rhs=wukv[:, c, lo:lo + n], start=(c == 0), stop=(c == 1))
                return ins
            bop("pe", lambda pe: mm_kv(pe, 0, 512, kvp), reads=[B_latT, B_wukv], writes=[B_kvp])
            bop("dve", lambda v: v.tensor_scalar(out=sm[:, 2:4], in0=sm[:, 0:2], scalar1=st[:, RX2B:RX2B + 1], scalar2=EPS,
                                                  op0=ALU.mult, op1=ALU.add), reads=[B_sm, B_st], writes=[B_sm])
            rsqrt(sm[:, 4:6], sm[:, 2:4], 2, B_sm, B_sm)
            bop("dve", lambda v: v.tensor_scalar(out=sm[:, 6:8], in0=sm[:, 4:6], scalar1=rx, scalar2=None, op0=ALU.mult),
                 reads=[B_sm, B_st], writes=[B_sm])
            aq = sm[:, 6:7]
            akv = sm[:, 7:8]
            bop("act", lambda a: a.activation(out=qkm[:, 0:4, :].rearrange("p h d -> p (h d)"), in_=mqp[:, 0:384], func=AF.Copy, scale=aq),
                 reads=[B_mqp, B_sm], writes=[B_qkm])
            bop("act", lambda a: a.activation(out=kvs[:, :, :].rearrange("p h d -> p (h d)")[:, 0:512], in_=kvp[:, 0:512], func=AF.Copy, scale=akv),
                 reads=[B_kvp, B_sm], writes=[B_kvs])
            bop("pe", lambda pe: mm_kv(pe, 512, 256, kvp), reads=[B_latT, B_wukv], writes=[B_kvp])
            bop("act", lambda a: a.activation(out=kvs[:, :, :].rearrange("p h d -> p (h d)")[:, 512:768], in_=kvp[:, 0:256], func=AF.Copy, scale=akv),
                 reads=[B_kvp, B_sm], writes=[B_kvs])
            bop("act", lambda a: a.activation(out=krs[:, :], in_=hps[4][:, 0:32], func=AF.Copy, scale=rx), reads=[B_h[4], B_st], writes=[B_krs])
            bop("dve", lambda v: v.tensor_copy(out=qkm[:, 4:8, 0:64], in_=kvs[:, :, 0:64]), reads=[B_kvs], writes=[B_qkm])
            bop("dve", lambda v: v.tensor_copy(out=qkm[:, 4:8, 64:96], in_=krs[:, :].unsqueeze(1).to_broadcast([128, 4, 32])),
                 reads=[B_krs], writes=[B_qkm])
            bop("dve", lambda v: v.tensor_copy(out=vmb[:, :, :], in_=kvs[:, :, 64:192]), reads=[B_kvs], writes=[B_vmb])
            P.dma("sp", lambda q: q.dma_start(out=sq["Vm"][tok, :], in_=vmb[:, :, :].rearrange("p h e -> p (h e)")), B_vmb, reads=[B_vmb])
            def tail(t=t, b=b, tok=tok, qkm=qkm, B_qkm=B_qkm):
                P.op("act", lambda a: a.activation(out=junk2[:, 0:768], in_=qkm[:, :, :].rearrange("p h d -> p (h d)"), func=AF.Square),
                     reads=[B_qkm], writes=[B_junk2])
                yield
                P.op("dve", lambda v: v.tensor_reduce(out=sm2[:, 8:16], in_=junk2[:, 0:768].rearrange("p (h d) -> p h d", d=96), axis=AX.X, op=ALU.add),
                     reads=[B_junk2], writes=[B_sm2])
                yield
                P.op("dve", lambda v: v.tensor_scalar(out=sm2[:, 16:24], in0=sm2[:, 8:16], scalar1=1.0 / 96, scalar2=EPS, op0=ALU.mult, op1=ALU.add),
                     reads=[B_sm2], writes=[B_sm2])
                yield
                rsqrt(sm2[:, 24:32], sm2[:, 16:24], 8, B_sm2, B_sm2)
                yield
                P.op("dve", lambda v: v.tensor_scalar(out=sm2[:, 24:28], in0=sm2[:, 24:28], scalar1=96.0 ** -0.5, scalar2=None, op0=ALU.mult),
                     reads=[B_sm2], writes=[B_sm2])
                yield
                P.op("dve", lambda v: v.tensor_tensor(out=qkm2[:, :, :], in0=qkm[:, :, :], in1=sm2[:, 24:32].unsqueeze(2).to_broadcast([128, 8, 96]), op=ALU.mult),
                     reads=[B_qkm, B_sm2], writes=[B_qkm2])
                yield
                for half in range(2):
                    P.op("dve", lambda v, half=half: v.tensor_tensor(
                        out=qkm[:, half * 4:(half + 1) * 4, :], in0=qkm2[:, half * 4:(half + 1) * 4, :],
                        in1=gain_m[:, half:half + 1, :].to_broadcast([128, 4, 96]), op=ALU.mult),
                        reads=[B_qkm2, B_g], writes=[B_qkm])
                C = cs[t % 3]
                cosb = C[:, 0:16].unsqueeze(1).to_broadcast([128, 8, 16])
                sinb = C[:, 16:32].unsqueeze(1).to_broadcast([128, 8, 16])
                x1 = qkm[:, :, 64:80]
                x2 = qkm[:, :, 80:96]
                P.op("dve", lambda v: v.tensor_tensor(out=rt[:, 0, :, :], in0=x1, in1=cosb, op=ALU.mult), reads=[B_qkm, B_cs[t % 3]], writes=[B_rt])
                yield
                P.op("dve", lambda v: v.tensor_tensor(out=rt[:, 1, :, :], in0=x2, in1=sinb, op=ALU.mult), reads=[B_qkm, B_cs[t % 3]], writes=[B_rt])
                yield
                P.op("dve", lambda v: v.tensor_tensor(out=rt[:, 2, :, :], in0=x2, in1=cosb, op=ALU.mult), reads=[B_qkm, B_cs[t % 3]], writes=[B_rt])
                yield
                P.op("dve", lambda v: v.tensor_tensor(out=rt[:, 3, :, :], in0=x1, in1=sinb, op=ALU.mult), reads=[B_qkm, B_cs[t % 3]], writes=[B_rt])
                yield
                P.op("dve", lambda v: v.tensor_copy(out=qkmb[:, :, 0:64], in_=qkm[:, :, 0:64]), reads=[B_qkm], writes=[B_qkmb])
                yield
                P.op("dve", lambda v: v.tensor_tensor(out=qkmb[:, :, 64:80], in0=rt[:, 0, :, :], in1=rt[:, 1, :, :], op=ALU.subtract),
                     reads=[B_rt], writes=[B_qkmb])
                yield
                P.op("dve", lambda v: v.tensor_tensor(out=qkmb[:, :, 80:96], in0=rt[:, 2, :, :], in1=rt[:, 3, :, :], op=ALU.add),
                     reads=[B_rt], writes=[B_qkmb])
                yield

                def tr6(pe):
                    ins = None
                    src = qkmb[:, :, :].rearrange("p h d -> p (h d)")
                    for c in range(6):
                        ins = pe.transpose(out=tp[:, c * 128:(c + 1) * 128], in_=src[:, c * 128:(c + 1) * 128], identity=ident[:, :])
                    return ins
                P.op("pe", tr6, reads=[B_qkmb, B_const], writes=[B_tp])
                P.op("act", lambda a: a.copy(out=qkmT[:, :, :].rearrange("p c t -> p (c t)"), in_=tp[:, 0:768]), reads=[B_tp], writes=[B_qkmT])
                yield
                P.dma("sp", lambda q: q.dma_start(out=sq["QmT"].rearrange("(c p) s -> p c s", p=128)[:, :, tok], in_=qkmT[:, 0:3, :]),
                      B_qkmT, reads=[B_qkmT])
                yield
                P.dma("sp", lambda q: q.dma_start(out=sq["KmT"].rearrange("(c p) s -> p c s", p=128)[:, :, tok], in_=qkmT[:, 3:6, :]),
                      B_qkmT, reads=[B_qkmT])
                yield

            if tail_gen[0] is not None:
                for _ in tail_gen[0]:
                    pass
            tail_gen[0] = tail()
        if tail_gen[0] is not None:
            for _ in tail_gen[0]:
                pass


def phase_b(nc, P, I, sq, l, rsqrt, neghalf, B_nh):
    S = sq["S"]
    NB = S // 128
    NQ = S // 512
    lam_init = 0.8 - 0.6 * math.exp(-0.3 * l)
    with contextlib.ExitStack() as es:
        def sb(name, shape, dt=F32):
            return es.enter_context(nc.sbuf_tensor("b" + str(l) + sq["nm"] + "_" + name, list(shape), dt))

        def ps(name, shape, dt=F32):
            return es.enter_context(nc.psum_tensor("b" + str(l) + sq["nm"] + "_" + name, list(shape), dt))

        NSETS = 2 if S <= 2048 else 1
        KTs = [sb("KT%d" % j, [96, S], BF16) for j in range(2 * NSETS)]
        QAs = [sb("QA%d" % j, [96, S], BF16) for j in range(2 * NSETS)]
        QBs = [sb("QB%d" % j, [96, S], BF16) for j in range(2 * NSETS)]
        Vs = [sb("V%d" % j, [128, NB, 132], BF16) for j in range(NSETS)]
        dbiass = [sb("dbias%d" % j, [128, 4, 512], F32) for j in range(NSETS)]
        B_KTs = [P.buf("KT") for j in range(2 * NSETS)]
        B_QAs = [P.buf("QA") for j in range(2 * NSETS)]
        B_QBs = [P.buf("QB") for j in range(2 * NSETS)]
        B_Vs = [P.buf("V") for j in range(NSETS)]
        B_dbs = [P.buf("dbias") for j in range(NSETS)]
        PT = [sb("PT%d" % i, [128, 512], BF16) for i in range(3)]
        B_PT = [P.buf("PT%d" % i) for i in range(3)]
        Sf = sb("Sf", [128, 512], F32); B_Sf = P.buf("Sf")
        on = [sb("on%d" % j, [128, 4, 128], F32) for j in range(2)]
        B_on = [P.buf("on0"), P.buf("on1")]
        rz = sb("rz", [128, 8], F32); B_rz = P.buf("rz")
        od = sb("od", [128, 4, 128], F32); B_od = P.buf("od")
        od2 = sb("od2", [128, 4, 128], F32); B_od2 = P.buf("od2")
        jk = sb("jk", [128, 512], F32); B_jk = P.buf("jk")
        ob = [sb("ob%d" % i, [128, 4, 128], BF16) for i in range(2)]
        B_ob = [P.buf("ob0"), P.buf("ob1")]
        st_ = sb("st", [128, 16], F32); B_st = P.buf("bst")
        lamv = sb("lamv", [128, 4, 64], F32)
        lamt = sb("lamt", [128, 8], F32)
        subg = sb("subg", [128, 128], F32)
        B_lam = P.buf("lam")
        Sps = [ps("S%d" % i, [128, 512]) for i in range(3)]
        B_S = [P.buf("S0"), P.buf("S1"), P.buf("S2")]
        O4 = ps("O4", [128, 4, 512])
        Ops = [O4[:, i, :] for i in range(4)]
        B_O = [P.buf("O%d" % i) for i in range(4)]

        for i, n in enumerate(("lam_q1", "lam_k1", "lam_q2", "lam_k2")):
            P.dma("sp", lambda q, i=i, n=n: q.dma_start(out=lamv[:, i, :], in_=I[n][l].partition_broadcast(128)), B_lam, writes=[B_lam])
        P.dma("sp", lambda q: q.dma_start(out=subg[:, :], in_=I["diff_subln_g"][l].partition_broadcast(128)), B_lam, writes=[B_lam])
        P.op("dve", lambda v: v.tensor_tensor(out=lamv[:, 0, :], in0=lamv[:, 0, :], in1=lamv[:, 1, :], op=ALU.mult), reads=[B_lam], writes=[B_lam])
        P.op("dve", lambda v: v.tensor_tensor(out=lamv[:, 2, :], in0=lamv[:, 2, :], in1=lamv[:, 3, :], op=ALU.mult), reads=[B_lam], writes=[B_lam])
        P.op("dve", lambda v: v.tensor_reduce(out=lamt[:, 0:1], in_=lamv[:, 0, :], axis=AX.X, op=ALU.add), reads=[B_lam], writes=[B_lam])
        P.op("dve", lambda v: v.tensor_reduce(out=lamt[:, 1:2], in_=lamv[:, 2, :], axis=AX.X, op=ALU.add), reads=[B_lam], writes=[B_lam])
        P.op("act", lambda a: a.activation(out=lamt[:, 2:4], in_=lamt[:, 0:2], func=AF.Exp), reads=[B_lam], writes=[B_lam])
        P.op("dve", lambda v: v.tensor_tensor(out=lamt[:, 4:5], in0=lamt[:, 3:4], in1=lamt[:, 2:3], op=ALU.subtract), reads=[B_lam], writes=[B_lam])
        P.op("dve", lambda v: v.tensor_scalar(out=lamt[:, 5:6], in0=lamt[:, 4:5], scalar1=-lam_init, scalar2=None, op0=ALU.add), reads=[B_lam], writes=[B_lam])
        neglam = lamt[:, 5:6]
        for V_, B_V_ in zip(Vs, B_Vs):
            P.op("pool", lambda g, V_=V_: g.memset(V_[:, :, 128:132], 1.0), writes=[B_V_])
        for j in range(2 * NSETS):
            P.op("pool", lambda g, j=j: g.memset(KTs[j][64:96, :], 0.0), writes=[B_KTs[j]])
            P.op("pool", lambda g, j=j: g.memset(QAs[j][64:96, :], 0.0), writes=[B_QAs[j]])
            P.op("pool", lambda g, j=j: g.memset(QBs[j][64:96, :], 0.0), writes=[B_QBs[j]])

        pt_i = [0]
        s_i = [0]

        def attention(maps, finish, V, B_V, dbias, B_db):
            units = []
            for qt in range(NQ):
                for j, m in enumerate(maps):
                    blks = []
                    for blk in range(NB):
                        if m["alibi"]:
                            q0, s0 = qt * 512, blk * 128
                            dmin = max(0, s0 - (q0 + 511), q0 - (s0 + 127))
                            if m["slope"] * dmin > BAND:
                                continue
                        blks.append(blk)
                    for n_, blk in enumerate(blks):
                        units.append((qt, j, blk, n_ == 0, n_ == len(blks) - 1))

            def qk(u):
                qt, j, blk, first, last = units[u]
                m = maps[j]
                sbuf_i = u % 3
                rel = blk - 4 * qt
                bs = slice(blk * 128, (blk + 1) * 128)
                qs = slice(qt * 512, (qt + 1) * 512)
                if m["alibi"]:
                    if rel < 0:
                        Kr, Q, BQ = KPAD, m["QA"], m["B_QA"]
                    elif rel >= 4:
                        Kr, Q, BQ = KPAD, m["QB"], m["B_QB"]
                    else:
                        Kr, Q, BQ = 64, m["QA"], m["B_QA"]
                else:
                    Kr, Q, BQ = 96, m["QA"], m["B_QA"]
                P.op("pe", lambda pe: pe.matmul(Sps[sbuf_i][:, :], lhsT=m["KT"][0:Kr, bs], rhs=Q[0:Kr, qs], start=True, stop=True),
                     reads=[m["B_KT"], BQ], writes=[B_S[sbuf_i]])

            def ex(u):
                qt, j, blk, first, last = units[u]
                m = maps[j]
                sbuf_i = u % 3
                pi = u % 3
                rel = blk - 4 * qt
                if m["alibi"] and 0 <= rel < 4:
                    P.op("dve", lambda v: v.tensor_tensor(out=Sf[:, :], in0=Sps[sbuf_i][:, :], in1=dbias[:, rel, :], op=ALU.add),
                         reads=[B_S[sbuf_i], B_db], writes=[B_Sf])
                    P.op("act", lambda a: a.activation(out=PT[pi][:, :], in_=Sf[:, :], func=AF.Exp), reads=[B_Sf], writes=[B_PT[pi]])
                else:
                    P.op("act", lambda a: a.activation(out=PT[pi][:, :], in_=Sps[sbuf_i][:, :], func=AF.Exp),
                         reads=[B_S[sbuf_i]], writes=[B_PT[pi]])

            def av(u):
                qt, j, blk, first, last = units[u]
                pi = u % 3

                def f(pe):
                    ins = None
                    for i in range(4):
                        ins = pe.matmul(Ops[i][:, 0:129], lhsT=PT[pi][:, i * 128:(i + 1) * 128], rhs=V[:, blk, 0:129],
                                        start=first, stop=last)
                    return ins
                P.op("pe", f, reads=[B_PT[pi], B_V], writes=B_O)
                if last:
                    P.op("dve", lambda v: v.reciprocal(out=rz[:, 0:4].unsqueeze(2), in_=O4[:, :, 128:129]), reads=B_O, writes=[B_rz])
                    P.op("dve", lambda v: v.tensor_tensor(out=on[j][:, :, :], in0=O4[:, :, 0:128],
                                                          in1=rz[:, 0:4].unsqueeze(2).to_broadcast([128, 4, 128]), op=ALU.mult),
                         reads=B_O + [B_rz], writes=[B_on[j]])
                    if j == len(maps) - 1:
                        finish(qt)

            n = len(units)
            qk(0)
            if n > 1:
                qk(1)
            for u in range(n):
                if u + 2 < n:
                    qk(u + 2)
                ex(u)
                av(u)

        ob_i = [0]

        def load_diff(h, st):
            KT, QA, QB = KTs[2 * st:2 * st + 2], QAs[2 * st:2 * st + 2], QBs[2 * st:2 * st + 2]
            B_KT, B_QA, B_QB = B_KTs[2 * st:2 * st + 2], B_QAs[2 * st:2 * st + 2], B_QBs[2 * st:2 * st + 2]
            V, B_V, dbias, B_db = Vs[st], B_Vs[st], dbiass[st], B_dbs[st]
            for j in range(2):
                P.dma("sp", lambda q, j=j: q.dma_start(out=KT[j][0:64, :], in_=sq["KdT"][h, j, :, :]), B_KT[j], writes=[B_KT[j]])
                P.dma("sp", lambda q, j=j: q.dma_start(out=KT[j][64:68, :], in_=I["augk"][h, :, 0:S]), B_KT[j], writes=[B_KT[j]])
                P.dma("sp", lambda q, j=j: q.dma_start(out=QA[j][0:64, :], in_=sq["QdT"][h, j, :, :]), B_QA[j], writes=[B_QA[j]])
                P.dma("sp", lambda q, j=j: q.dma_start(out=QA[j][64:68, :], in_=I["augq"][h, 0:4, 0:S]), B_QA[j], writes=[B_QA[j]])
                P.dma("sp", lambda q, j=j: q.dma_start(out=QB[j][0:64, :], in_=sq["QdT"][h, j, :, :]), B_QB[j], writes=[B_QB[j]])
                P.dma("sp", lambda q, j=j: q.dma_start(out=QB[j][64:68, :], in_=I["augq"][h, 4:8, 0:S]), B_QB[j], writes=[B_QB[j]])
            P.dma("sp", lambda q: q.dma_start(out=V[:, :, 0:128], in_=sq["Vd"].rearrange("(b p) e -> p b e", p=128)[:, :, h * 128:(h + 1) * 128]),
                  B_V, writes=[B_V])
            P.dma("sp", lambda q: q.dma_start(out=dbias[:, :, :], in_=I["dbias"][h]), B_db, writes=[B_db])

        def run_diff(h, st):
            KT, QA, QB = KTs[2 * st:2 * st + 2], QAs[2 * st:2 * st + 2], QBs[2 * st:2 * st + 2]
            B_KT, B_QA, B_QB = B_KTs[2 * st:2 * st + 2], B_QAs[2 * st:2 * st + 2], B_QBs[2 * st:2 * st + 2]

            def finish_diff(qt, h=h):
                oi = ob_i[0] % 2
                ob_i[0] += 1
                flat = lambda t: t[:, :, :].rearrange("p i e -> p (i e)")
                P.op("dve", lambda v: v.scalar_tensor_tensor(out=flat(od), in0=flat(on[1]), scalar=neglam, in1=flat(on[0]),
                                                             op0=ALU.mult, op1=ALU.add),
                     reads=[B_on[0], B_on[1], B_lam], writes=[B_od])
                P.op("dve", lambda v: v.tensor_tensor(out=jk[:, :], in0=flat(od), in1=flat(od), op=ALU.mult), reads=[B_od], writes=[B_jk])
                P.op("dve", lambda v: v.tensor_reduce(out=st_[:, 0:4], in_=jk[:, :].rearrange("p (i e) -> p i e", e=128), axis=AX.X, op=ALU.add),
                     reads=[B_jk], writes=[B_st])
                P.op("dve", lambda v: v.tensor_scalar(out=st_[:, 4:8], in0=st_[:, 0:4], scalar1=1.0 / 128, scalar2=EPS, op0=ALU.mult, op1=ALU.add),
                     reads=[B_st], writes=[B_st])
                rsqrt(st_[:, 8:12], st_[:, 4:8], 4, B_st, B_st)
                P.op("dve", lambda v: v.tensor_scalar(out=st_[:, 12:16], in0=st_[:, 8:12], scalar1=1.0 - lam_init, scalar2=None, op0=ALU.mult),
                     reads=[B_st], writes=[B_st])
                P.op("dve", lambda v: v.tensor_tensor(out=od2[:, :, :], in0=od[:, :, :], in1=st_[:, 12:16].unsqueeze(2).to_broadcast([128, 4, 128]), op=ALU.mult),
                     reads=[B_od, B_st], writes=[B_od2])
                P.op("dve", lambda v: v.tensor_tensor(out=ob[oi][:, :, :], in0=od2[:, :, :], in1=subg[:, :].unsqueeze(1).to_broadcast([128, 4, 128]), op=ALU.mult),
                     reads=[B_od2, B_lam], writes=[B_ob[oi]])
                P.dma("sp", lambda q: q.dma_start(
                    out=sq["mix"][qt * 512:(qt + 1) * 512, h * 128:(h + 1) * 128].rearrange("(i p) e -> p i e", p=128),
                    in_=ob[oi][:, :, :]), B_ob[oi], reads=[B_ob[oi]])

            maps = [dict(KT=KT[j], QA=QA[j], QB=QB[j], B_KT=B_KT[j], B_QA=B_QA[j], B_QB=B_QB[j], alibi=True,
                         slope=2.0 ** (-8.0 * (h + 1) / 4)) for j in range(2)]
            attention(maps, finish_diff, Vs[st], B_Vs[st], dbiass[st], B_dbs[st])

        def load_mla(h, st):
            P.dma("sp", lambda q: q.dma_start(out=KTs[2 * st][0:96, :], in_=sq["KmT"][h * 96:(h + 1) * 96, :]), B_KTs[2 * st], writes=[B_KTs[2 * st]])
            P.dma("sp", lambda q: q.dma_start(out=QAs[2 * st][0:96, :], in_=sq["QmT"][h * 96:(h + 1) * 96, :]), B_QAs[2 * st], writes=[B_QAs[2 * st]])
            P.dma("sp", lambda q: q.dma_start(out=Vs[st][:, :, 0:128], in_=sq["Vm"].rearrange("(b p) e -> p b e", p=128)[:, :, h * 128:(h + 1) * 128]),
                  B_Vs[st], writes=[B_Vs[st]])

        def run_mla(h, st):
            def finish_mla(qt, h=h):
                oi = ob_i[0] % 2
                ob_i[0] += 1
                P.op("dve", lambda v: v.tensor_copy(out=ob[oi][:, :, :], in_=on[0][:, :, :]), reads=[B_on[0]], writes=[B_ob[oi]])
                P.dma("sp", lambda q: q.dma_start(
                    out=sq["mix"][qt * 512:(qt + 1) * 512, 512 + h * 128:512 + (h + 1) * 128].rearrange("(i p) e -> p i e", p=128),
                    in_=ob[oi][:, :, :]), B_ob[oi], reads=[B_ob[oi]])

            maps = [dict(KT=KTs[2 * st], QA=QAs[2 * st], QB=None, B_KT=B_KTs[2 * st], B_QA=B_QAs[2 * st], B_QB=None, alibi=False)]
            attention(maps, finish_mla, Vs[st], B_Vs[st], dbiass[st], B_dbs[st])

        jobs = [(load_diff, run_diff, h) for h in range(4)] + [(load_mla, run_mla, h) for h in range(4)]
        if NSETS == 2:
            jobs[0][0](jobs[0][2], 0)
            for i, (ld, rn, h) in enumerate(jobs):
                if i + 1 < len(jobs):
                    jobs[i + 1][0](jobs[i + 1][2], (i + 1) % 2)
                rn(h, i % 2)
        else:
            for ld, rn, h in jobs:
                ld(h, 0)
                rn(h, 0)


def phase_c(nc, P, I, sq, l, x_src, x_dst, ident, identf, iota16, B_const, rsqrt):
    S = sq["S"]
    NT = S // 128
    NSLOT = 4
    with contextlib.ExitStack() as es:
        def sb(name, shape, dt=F32):
            return es.enter_context(nc.sbuf_tensor("c" + str(l) + sq["nm"] + "_" + name, list(shape), dt))

        def ps(name, shape, dt=F32):
            return es.enter_context(nc.psum_tensor("c" + str(l) + sq["nm"] + "_" + name, list(shape), dt))

        wout = sb("wout", [128, 8, D], BF16); B_wout = P.buf("wout")
        wq = sb("wq", [128, 8, 2048], BF16); B_wq = P.buf("wq")
        keyT = sb("keyT", [128, 8, 2, 128], BF16); B_key = P.buf("keyT")
        gffn = sb("gffn", [128, 8], F32)
        gffnb = sb("gffnb", [128, D], F32)
        B_g = P.buf("cg")
        es_w = contextlib.ExitStack()
        stg = [es_w.enter_context(nc.sbuf_tensor("c" + str(l) + sq["nm"] + "_stg" + str(i), [128, 2048], F32)) for i in range(2)]
        B_stg = [P.buf("cstg0"), P.buf("cstg1")]
        with nc.allow_non_contiguous_dma(reason="tiny gain vector"):
            P.dma("sp", lambda q: q.dma_start(out=gffn[:], in_=I["norm_ffn_g"][l].rearrange("(c p) -> p c", p=128)), B_g, writes=[B_g])
        P.dma("sp", lambda q: q.dma_start(out=gffnb[:, :], in_=I["norm_ffn_g"][l].partition_broadcast(128)), B_g, writes=[B_g])
        k = 0
        for c in range(8):
            b = k % 2
            P.dma("sp", lambda q, c=c, b=b: q.dma_start(out=stg[b][:, 0:D], in_=I["w_out"][l, c * 128:(c + 1) * 128, :]), B_stg[b], writes=[B_stg[b]])
            P.op("dve", lambda v, c=c, b=b: v.tensor_copy(out=wout[:, c, :], in_=stg[b][:, 0:D]), reads=[B_stg[b]], writes=[B_wout])
            k += 1
        for c in range(8):
            b = k % 2
            P.dma("sp", lambda q, c=c, b=b: q.dma_start(out=stg[b][:, :], in_=I["peer_w_q"][l, c * 128:(c + 1) * 128, :]), B_stg[b], writes=[B_stg[b]])
            P.op("dve", lambda v, c=c, b=b: v.tensor_scalar(out=wq[:, c, :], in0=stg[b][:, :], scalar1=gffn[:, c:c + 1], scalar2=None, op0=ALU.mult),
                 reads=[B_stg[b], B_g], writes=[B_wq])
            k += 1
        for j, nm in enumerate(("peer_key1T", "peer_key2T")):
            b = k % 2
            P.dma("sp", lambda q, nm=nm, b=b: q.dma_start(out=stg[b][:, 0:1024].rearrange("p (h i) -> p h i", i=128),
                                                         in_=I[nm][l].rearrange("h d i -> d h i")), B_stg[b], writes=[B_stg[b]])
            P.op("dve", lambda v, j=j, b=b: v.tensor_copy(out=keyT[:, :, j, :], in_=stg[b][:, 0:1024].rearrange("p (h i) -> p h i", i=128)),
                 reads=[B_stg[b]], writes=[B_key])
            k += 1

        P.barrier()
        es_w.close()
        xt = [sb("xt%d" % i, [128, D]) for i in range(2)]
        B_xt = [P.buf("cxt0"), P.buf("cxt1")]
        mx = [sb("mx%d" % i, [128, D], BF16) for i in range(2)]
        B_mx = [P.buf("mx0"), P.buf("mx1")]
        mixT = sb("mixT", [128, 8, 128], BF16); B_mixT = P.buf("mixT")
        x1 = sb("x1", [128, D]); B_x1 = P.buf("x1")
        junk = sb("junk", [128, D]); B_junk = P.buf("cjunk")
        st = sb("st", [128, 16]); B_st = P.buf("cst")
        x1b = sb("x1b", [128, D], BF16); B_x1b = P.buf("x1b")
        x1T = sb("x1T", [128, 8, 128], BF16); B_x1T = P.buf("x1T")
        xn = sb("xn", [128, D]); B_xn = P.buf("xn")
        qT = sb("qT", [128, 16, 128], BF16); B_qT = P.buf("qT")
        sc = sb("sc", [128, 16, 128]); B_sc = P.buf("sc")
        sc2 = sb("sc2", [128, 16, 128]); B_sc2 = P.buf("sc2")
        v16 = sb("v16", [128, 16, 16]); B_v16 = P.buf("v16")
        i16 = sb("i16", [128, 16, 16], U32); B_i16 = P.buf("i16")
        i16f = sb("i16f", [128, 16, 16]); B_i16f = P.buf("i16f")
        cand = sb("cand", [128, 8, 256]); B_cand = P.buf("cand")
        cand2 = sb("cand2", [128, 8, 256]); B_cand2 = P.buf("cand2")
        tv = sb("tv", [128, 8, 16]); B_tv = P.buf("tv")
        tpos = sb("tpos", [128, 8, 16], U32); B_tpos = P.buf("tpos")
        ta = sb("ta", [128, 2, 128], U32); B_ta = P.buf("ta")
        taf = sb("taf", [128, 2, 128]); B_taf = P.buf("taf")
        oh = sc2[:, :, :].rearrange("p g (x a) -> p (g x) a", a=16); B_oh = B_sc2
        oh2 = cand2[:, :, :].rearrange("p h (k a) -> p (h k) a", a=16); B_oh2 = B_cand2
        isel = sb("isel", [128, 2, 128]); B_isel = P.buf("isel")
        eidf = sb("eidf", [128, 128]); B_eidf = P.buf("eidf")
        eid = sb("eid", [128, 128], I32); B_eid = P.buf("eid")
        gt = sb("gt", [128, 8, 16]); B_gt = P.buf("gt")
        gs = sb("gs", [128, 16]); B_gs = P.buf("gs")
        hraw = sb("hraw", [128, 128]); B_hraw = P.buf("hraw")
        hg = sb("hg", [128, 128]); B_hg = P.buf("hg")
        wgt = sb("wgt", [128, 128]); B_wgt = P.buf("wgt")
        ug = [sb("ug%d" % i, [128, D]) for i in range(NSLOT)]
        B_ug = [P.buf("ug%d" % i) for i in range(NSLOT)]
        vg = ug; B_vg = B_ug
        vgb = [sb("vgb%d" % i, [128, D], BF16) for i in range(2)]
        B_vgb = [P.buf("vgb0"), P.buf("vgb1")]
        dg = [sb("dg%d" % i, [128, 128], BF16) for i in range(2)]
        B_dg = [P.buf("dg0"), P.buf("dg1")]
        prod = junk; B_prod = B_junk
        xo = [sb("xo%d" % i, [128, D]) for i in range(2)]
        B_xo = [P.buf("xo0"), P.buf("xo1")]

        tp = ps("tp", [128, 1024], BF16); B_tp = P.buf("ctp")
        yps = [ps("y%d" % i, [128, 512]) for i in range(2)]; B_y = [P.buf("y0"), P.buf("y1")]
        qps = [ps("q%d" % i, [128, 512]) for i in range(2)]; B_q = [P.buf("q0"), P.buf("q1")]
        aps = [ps("acc%d" % i, [128, 512]) for i in range(2)]; B_acc = [P.buf("acc0"), P.buf("acc1")]

        def load(t):
            b = t % 2
            P.dma("sp", lambda q: q.dma_start(out=xt[b][:, :], in_=x_src[t * 128:(t + 1) * 128, :]), B_xt[b], writes=[B_xt[b]])
            P.dma("sp", lambda q: q.dma_start(out=mx[b][:, :], in_=sq["mix"][t * 128:(t + 1) * 128, :]), B_mx[b], writes=[B_mx[b]])

        def tr8(pe, src):
            ins = None
            for c in range(8):
                ins = pe.transpose(out=tp[:, c * 128:(c + 1) * 128], in_=src[:, c * 128:(c + 1) * 128], identity=ident[:, :])
            return ins

        load(0)
        for t in range(NT):
            b = t % 2
            if t + 1 < NT:
                load(t + 1)
            tok = slice(t * 128, (t + 1) * 128)
            P.op("pe", lambda pe: tr8(pe, mx[b]), reads=[B_mx[b], B_const], writes=[B_tp])
            P.op("act", lambda a: a.copy(out=mixT[:, :, :].rearrange("p c t -> p (c t)"), in_=tp[:, :]), reads=[B_tp], writes=[B_mixT])
            for g in range(2):
                def mm(pe, g=g):
                    ins = None
                    for c in range(8):
                        ins = pe.matmul(yps[g][:, :], lhsT=mixT[:, c, :], rhs=wout[:, c, g * 512:(g + 1) * 512], start=(c == 0), stop=(c == 7))
                    return ins
                P.op("pe", mm, reads=[B_mixT, B_wout], writes=[B_y[g]])
                P.op("dve", lambda v, g=g: v.tensor_tensor(out=x1[:, g * 512:(g + 1) * 512], in0=yps[g][:, :], in1=xt[b][:, g * 512:(g + 1) * 512], op=ALU.add),
                     reads=[B_y[g], B_xt[b]], writes=[B_x1])
            P.op("act", lambda a: a.activation(out=junk[:, 0:D], in_=x1[:, :], func=AF.Square, accum_out=st[:, 0:1]), reads=[B_x1], writes=[B_junk, B_st])
            P.op("dve", lambda v: v.tensor_scalar(out=st[:, 1:2], in0=st[:, 0:1], scalar1=1.0 / D, scalar2=EPS, op0=ALU.mult, op1=ALU.add),
                 reads=[B_st], writes=[B_st])
            rsqrt(st[:, 2:3], st[:, 1:2], 1, B_st, B_st)
            r1 = st[:, 2:3]
            P.op("dve", lambda v: v.tensor_copy(out=x1b[:, :], in_=x1[:, :]), reads=[B_x1], writes=[B_x1b])
            P.op("pe", lambda pe: tr8(pe, x1b), reads=[B_x1b, B_const], writes=[B_tp])
            P.op("act", lambda a: a.copy(out=x1T[:, :, :].rearrange("p c t -> p (c t)"), in_=tp[:, :]), reads=[B_tp], writes=[B_x1T])
            P.op("dve", lambda v: v.scalar_tensor_tensor(out=xn[:, :], in0=x1[:, :], scalar=r1, in1=gffnb[:, :], op0=ALU.mult, op1=ALU.mult),
                 reads=[B_x1, B_st, B_g], writes=[B_xn])
            for half in range(4):
                def mmq(pe, half=half):
                    ins = None
                    for cc in range(4):
                        col = half * 4 + cc
                        for c in range(8):
                            ins = pe.matmul(qps[half % 2][:, cc * 128:(cc + 1) * 128], lhsT=wq[:, c, col * 128:(col + 1) * 128], rhs=x1T[:, c, :],
                                            start=(c == 0), stop=(c == 7))
                    return ins
                P.op("pe", mmq, reads=[B_x1T, B_wq], writes=[B_q[half % 2]])
                P.op("act", lambda a, half=half: a.copy(out=qT[:, half * 4:(half + 1) * 4, :].rearrange("p c t -> p (c t)"), in_=qps[half % 2][:, :]),
                     reads=[B_q[half % 2]], writes=[B_qT])
            for half in range(4):
                def mms(pe, half=half):
                    ins = None
                    for cc in range(4):
                        col = half * 4 + cc
                        ins = pe.matmul(qps[half % 2][:, cc * 128:(cc + 1) * 128], lhsT=qT[:, col, :], rhs=keyT[:, col // 2, col % 2, :],
                                        start=True, stop=True)
                    return ins
                P.op("pe", mms, reads=[B_qT, B_key], writes=[B_q[half % 2]])
                P.op("act", lambda a, half=half: a.activation(out=sc[:, half * 4:(half + 1) * 4, :].rearrange("p c i -> p (c i)"), in_=qps[half % 2][:, :],
                                                              func=AF.Copy, scale=r1),
                     reads=[B_q[half % 2], B_st], writes=[B_sc])
            for g in range(16):
                P.op("dve", lambda v, g=g: v.max(out=v16[:, g, 0:8], in_=sc[:, g, :]), reads=[B_sc], writes=[B_v16])
                P.op("dve", lambda v, g=g: v.max_index(out=i16[:, g, 0:8], in_max=v16[:, g, 0:8], in_values=sc[:, g, :]), reads=[B_sc, B_v16], writes=[B_i16])
                P.op("dve", lambda v, g=g: v.match_replace(out=sc2[:, g, :], in_to_replace=v16[:, g, 0:8], in_values=sc[:, g, :], imm_value=NEG),
                     reads=[B_sc, B_v16], writes=[B_sc2])
                P.op("dve", lambda v, g=g: v.max(out=v16[:, g, 8:16], in_=sc2[:, g, :]), reads=[B_sc2], writes=[B_v16])
                P.op("dve", lambda v, g=g: v.max_index(out=i16[:, g, 8:16], in_max=v16[:, g, 8:16], in_values=sc2[:, g, :]), reads=[B_sc2, B_v16], writes=[B_i16])
            P.op("dve", lambda v: v.tensor_copy(out=i16f[:, :, :], in_=i16[:, :, :]), reads=[B_i16], writes=[B_i16f])
            v16v = v16[:, :, :].rearrange("p (h j) k -> p h j k", j=2)
            P.op("dve", lambda v: v.tensor_tensor(out=cand[:, :, :].rearrange("p h (a b) -> p h a b", b=16),
                                                  in0=v16v[:, :, 0, :].unsqueeze(3).to_broadcast([128, 8, 16, 16]),
                                                  in1=v16v[:, :, 1, :].unsqueeze(2).to_broadcast([128, 8, 16, 16]), op=ALU.add),
                 reads=[B_v16], writes=[B_cand])
            for h in range(8):
                P.op("dve", lambda v, h=h: v.max(out=tv[:, h, 0:8], in_=cand[:, h, :]), reads=[B_cand], writes=[B_tv])
                P.op("dve", lambda v, h=h: v.max_index(out=tpos[:, h, 0:8], in_max=tv[:, h, 0:8], in_values=cand[:, h, :]), reads=[B_cand, B_tv], writes=[B_tpos])
                P.op("dve", lambda v, h=h: v.match_replace(out=cand2[:, h, :], in_to_replace=tv[:, h, 0:8], in_values=cand[:, h, :], imm_value=NEG),
                     reads=[B_cand, B_tv], writes=[B_cand2])
                P.op("dve", lambda v, h=h: v.max(out=tv[:, h, 8:16], in_=cand2[:, h, :]), reads=[B_cand2], writes=[B_tv])
                P.op("dve", lambda v, h=h: v.max_index(out=tpos[:, h, 8:16], in_max=tv[:, h, 8:16], in_values=cand2[:, h, :]), reads=[B_cand2, B_tv], writes=[B_tpos])
            tposf = tpos[:, :, :].rearrange("p h k -> p (h k)")
            P.op("dve", lambda v: v.tensor_scalar(out=ta[:, 0, :], in0=tposf, scalar1=4, scalar2=None, op0=ALU.logical_shift_right), reads=[B_tpos], writes=[B_ta])
            P.op("dve", lambda v: v.tensor_scalar(out=ta[:, 1, :], in0=tposf, scalar1=15, scalar2=None, op0=ALU.bitwise_and), reads=[B_tpos], writes=[B_ta])
            P.op("dve", lambda v: v.tensor_copy(out=taf[:, :, :], in_=ta[:, :, :]), reads=[B_ta], writes=[B_taf])
            i16v = i16f[:, :, :].rearrange("p (h j) k -> p h j k", j=2)
            for j in range(2):
                P.op("dve", lambda v, j=j: v.tensor_tensor(out=oh[:, :, :], in0=taf[:, j, :].unsqueeze(2).to_broadcast([128, 128, 16]),
                                                           in1=iota16[:, :].unsqueeze(1).to_broadcast([128, 128, 16]), op=ALU.is_equal),
                     reads=[B_taf, B_const], writes=[B_oh])
                P.op("dve", lambda v, j=j: v.tensor_tensor(out=oh2[:, :, :].rearrange("p (h k) a -> p h k a", k=16),
                                                           in0=oh[:, :, :].rearrange("p (h k) a -> p h k a", k=16),
                                                           in1=i16v[:, :, j, :].unsqueeze(2).to_broadcast([128, 8, 16, 16]), op=ALU.mult),
                     reads=[B_oh, B_i16f], writes=[B_oh2])
                P.op("dve", lambda v, j=j: v.tensor_reduce(out=isel[:, j, :], in_=oh2[:, :, :], axis=AX.X, op=ALU.add), reads=[B_oh2], writes=[B_isel])
            P.op("dve", lambda v: v.scalar_tensor_tensor(out=eidf[:, :], in0=isel[:, 0, :], scalar=128.0, in1=isel[:, 1, :], op0=ALU.mult, op1=ALU.add),
                 reads=[B_isel], writes=[B_eidf])
            P.op("dve", lambda v: v.tensor_scalar(out=eid[:, :], in0=eidf[:, :], scalar1=float(l * 16384), scalar2=None, op0=ALU.add),
                 reads=[B_eidf], writes=[B_eid])
            P.op("dve", lambda v: v.tensor_tensor(out=gt[:, :, :], in0=tv[:, :, :], in1=tv[:, :, 0:1].to_broadcast([128, 8, 16]), op=ALU.subtract),
                 reads=[B_tv], writes=[B_gt])
            P.op("act", lambda a: a.activation(out=gt[:, :, :].rearrange("p h k -> p (h k)"), in_=gt[:, :, :].rearrange("p h k -> p (h k)"), func=AF.Exp),
                 reads=[B_gt], writes=[B_gt])
            P.op("dve", lambda v: v.tensor_reduce(out=gs[:, 0:8], in_=gt[:, :, :], axis=AX.X, op=ALU.add), reads=[B_gt], writes=[B_gs])
            P.op("dve", lambda v: v.reciprocal(out=gs[:, 8:16], in_=gs[:, 0:8]), reads=[B_gs], writes=[B_gs])
            P.op("dve", lambda v: v.tensor_tensor(out=gt[:, :, :], in0=gt[:, :, :], in1=gs[:, 8:16].unsqueeze(2).to_broadcast([128, 8, 16]), op=ALU.mult),
                 reads=[B_gt, B_gs], writes=[B_gt])
            for hk in range(128):
                s_ = hk % NSLOT
                P.dma("pool", lambda g, hk=hk, s_=s_: g.indirect_dma_start(
                    out=ug[s_][:, :], out_offset=None, in_=I["peer_u"].rearrange("l e d -> (l e) d"),
                    in_offset=bass.IndirectOffsetOnAxis(ap=eid[:, hk:hk + 1], axis=0)),
                    B_ug[s_], reads=[B_eid], writes=[B_ug[s_]])
                P.op("dve", lambda v, hk=hk, s_=s_: v.scalar_tensor_tensor(
                    out=prod[:, :], in0=ug[s_][:, :], scalar=1.0, in1=xn[:, :], op0=ALU.mult, op1=ALU.mult,
                    accum_out=hraw[:, hk:hk + 1]),
                    reads=[B_ug[s_], B_xn], writes=[B_prod, B_hraw])
            P.op("act", lambda a: a.activation(out=hg[:, :], in_=hraw[:, :], func=AF.Gelu), reads=[B_hraw], writes=[B_hg])
            P.op("dve", lambda v: v.tensor_tensor(out=wgt[:, :], in0=hg[:, :], in1=gt[:, :, :].rearrange("p h k -> p (h k)"), op=ALU.mult),
                 reads=[B_hg, B_gt], writes=[B_wgt])
            for hk in range(128):
                s_ = hk % NSLOT
                d_ = hk % 2
                P.dma("pool", lambda g, hk=hk, s_=s_: g.indirect_dma_start(
                    out=vg[s_][:, :], out_offset=None, in_=I["peer_v"].rearrange("l e d -> (l e) d"),
                    in_offset=bass.IndirectOffsetOnAxis(ap=eid[:, hk:hk + 1], axis=0)),
                    B_vg[s_], reads=[B_eid], writes=[B_vg[s_]])
                P.op("act", lambda a, s_=s_, d_=d_: a.copy(out=vgb[d_][:, :], in_=vg[s_][:, :]), reads=[B_vg[s_]], writes=[B_vgb[d_]])
                P.op("dve", lambda v, hk=hk, d_=d_: v.tensor_scalar(out=dg[d_][:, :], in0=identf[:, :], scalar1=wgt[:, hk:hk + 1], scalar2=None, op0=ALU.mult),
                     reads=[B_wgt, B_const], writes=[B_dg[d_]])

                def mmv(pe, hk=hk, d_=d_):
                    ins = None
                    for g in range(2):
                        ins = pe.matmul(aps[g][:, :], lhsT=dg[d_][:, :], rhs=vgb[d_][:, g * 512:(g + 1) * 512], start=(hk == 0), stop=(hk == 127))
                    return ins
                P.op("pe", mmv, reads=[B_dg[d_], B_vgb[d_]], writes=B_acc)
            o = t % 2
            for g in range(2):
                P.op("dve", lambda v, g=g: v.tensor_tensor(out=xo[o][:, g * 512:(g + 1) * 512], in0=aps[g][:, :], in1=x1[:, g * 512:(g + 1) * 512], op=ALU.add),
                     reads=[B_acc[g], B_x1], writes=[B_xo[o]])
            P.dma("sp", lambda q: q.dma_start(out=x_dst[tok, :], in_=xo[o][:, :]), B_xo[o], reads=[B_xo[o]])


def prepass_tables(nc, P, I, l, UT, VB, ident, B_const):
    with contextlib.ExitStack() as es:
        def sb(name, shape, dt=F32):
            return es.enter_context(nc.sbuf_tensor("t" + str(l) + "_" + name, list(shape), dt))

        def ps(name, shape, dt=F32):
            return es.enter_context(nc.psum_tensor("t" + str(l) + "_" + name, list(shape), dt))

        ru = [sb("ru%d" % i, [128, D]) for i in range(3)]; B_ru = [P.buf("ru%d" % i) for i in range(3)]
        rv = [sb("rv%d" % i, [128, D]) for i in range(3)]; B_rv = [P.buf("rv%d" % i) for i in range(3)]
        rub = [sb("rub%d" % i, [128, D], BF16) for i in range(2)]; B_rub = [P.buf("rub%d" % i) for i in range(2)]
        utb = [sb("utb%d" % i, [128, D], BF16) for i in range(2)]; B_utb = [P.buf("utb%d" % i) for i in range(2)]
        vbb = [sb("vbb%d" % i, [128, D], BF16) for i in range(2)]; B_vbb = [P.buf("vbb%d" % i) for i in range(2)]
        tp = [ps("tp%d" % i, [128, 1024], BF16) for i in range(2)]; B_tp = [P.buf("ttp%d" % i) for i in range(2)]
        Usrc = I["peer_u"][l].rearrange("(i1 i2) d -> i2 i1 d", i2=128)
        Vsrc = I["peer_v"][l].rearrange("(i1 i2) d -> i2 i1 d", i2=128)

        def load(i2):
            b = i2 % 3
            P.dma("sp", lambda q: q.dma_start(out=ru[b][:, :], in_=Usrc[i2]), B_ru[b], writes=[B_ru[b]])
            P.dma("sp", lambda q: q.dma_start(out=rv[b][:, :], in_=Vsrc[i2]), B_rv[b], writes=[B_rv[b]])

        load(0)
        load(1)
        for i2 in range(128):
            if i2 + 2 < 128:
                load(i2 + 2)
            b3 = i2 % 3
            b2 = i2 % 2
            e1, e2 = ("act", "dve") if i2 % 2 == 0 else ("dve", "act")

            def cast(eng, out, in_):
                if eng == "act":
                    return lambda a: a.copy(out=out, in_=in_)
                return lambda v: v.tensor_copy(out=out, in_=in_)
            P.op(e1, cast(e1, rub[b2][:, :], ru[b3][:, :]), reads=[B_ru[b3]], writes=[B_rub[b2]])

            def tr(pe):
                ins = None
                for c in range(8):
                    ins = pe.transpose(out=tp[b2][:, c * 128:(c + 1) * 128], in_=rub[b2][:, c * 128:(c + 1) * 128], identity=ident[:, :])
                return ins
            P.op("pe", tr, reads=[B_rub[b2], B_const], writes=[B_tp[b2]])
            P.op(e2, cast(e2, utb[b2][:, :], tp[b2][:, :]), reads=[B_tp[b2]], writes=[B_utb[b2]])
            P.dma("sp", lambda q: q.dma_start(out=UT[i2], in_=utb[b2][:, :]), B_utb[b2], reads=[B_utb[b2]])
            P.op(e1, cast(e1, vbb[b2][:, :], rv[b3][:, :]), reads=[B_rv[b3]], writes=[B_vbb[b2]])
            P.dma("sp", lambda q: q.dma_start(out=VB[i2], in_=vbb[b2][:, :]), B_vbb[b2], reads=[B_vbb[b2]])


def phase_c_dense(nc, P, I, sq, l, x_src, x_dst, ident, identf, iota16, iota128, B_const, rsqrt, UT, VB):
    S = sq["S"]
    NT = S // 128
    G = 256
    NG = NT // 2
    pfx = "c" + str(l) + sq["nm"] + "_"
    with contextlib.ExitStack() as es:
        def sb(name, shape, dt=F32):
            return es.enter_context(nc.sbuf_tensor(pfx + name, list(shape), dt))

        def ps(name, shape, dt=F32):
            return es.enter_context(nc.psum_tensor(pfx + name, list(shape), dt))

        wout = sb("wout", [128, 8, D], BF16); B_wout = P.buf("wout")
        wq = sb("wq", [128, 8, 2048], BF16); B_wq = P.buf("wq")
        keyT = sb("keyT", [128, 8, 2, 128], BF16); B_key = P.buf("keyT")
        gffnb = sb("gffnb", [128, D], F32)
        B_g = P.buf("cg")
        Wg = sb("Wg", [128, G, 128], BF16); B_Wg = P.buf("Wg")
        xnT = [sb("xnT%d" % i, [128, 8, G], BF16) for i in range(2)]; B_xnT = [P.buf("xnT0"), P.buf("xnT1")]
        x1g = [sb("x1g%d" % i, [128, 2, D]) for i in range(2)]; B_x1g = [P.buf("x1g0"), P.buf("x1g1")]
        selg = sb("selg", [128, 2, 3, 128]); B_selg = P.buf("selg")

        es_w = contextlib.ExitStack()
        stg = [es_w.enter_context(nc.sbuf_tensor(pfx + "stg" + str(i), [128, 2048], F32)) for i in range(2)]
        B_stg = [P.buf("cstg0"), P.buf("cstg1")]
        P.dma("sp", lambda q: q.dma_start(out=gffnb[:, :], in_=I["norm_ffn_g"][l].partition_broadcast(128)), B_g, writes=[B_g])
        k = 0
        for c in range(8):
            b = k % 2
            P.dma("sp", lambda q, c=c, b=b: q.dma_start(out=stg[b][:, 0:D], in_=I["w_out"][l, c * 128:(c + 1) * 128, :]), B_stg[b], writes=[B_stg[b]])
            P.op("dve", lambda v, c=c, b=b: v.tensor_copy(out=wout[:, c, :], in_=stg[b][:, 0:D]), reads=[B_stg[b]], writes=[B_wout])
            k += 1
        for c in range(8):
            b = k % 2
            P.dma("sp", lambda q, c=c, b=b: q.dma_start(out=stg[b][:, :], in_=I["peer_w_q"][l, c * 128:(c + 1) * 128, :]), B_stg[b], writes=[B_stg[b]])
            P.op("act", lambda a, c=c, b=b: a.copy(out=wq[:, c, :], in_=stg[b][:, :]), reads=[B_stg[b]], writes=[B_wq])
            k += 1
        for j, nm in enumerate(("peer_key1T", "peer_key2T")):
            b = k % 2
            P.dma("sp", lambda q, nm=nm, b=b: q.dma_start(out=stg[b][:, 0:1024].rearrange("p (h i) -> p h i", i=128),
                                                         in_=I[nm][l].rearrange("h d i -> d h i")), B_stg[b], writes=[B_stg[b]])
            P.op("dve", lambda v, j=j, b=b: v.tensor_copy(out=keyT[:, :, j, :], in_=stg[b][:, 0:1024].rearrange("p (h i) -> p h i", i=128)),
                 reads=[B_stg[b]], writes=[B_key])
            k += 1
        P.barrier()
        es_w.close()

        mx = sb("mx", [128, D], BF16); B_mx = P.buf("mx")
        mixT = sb("mixT", [128, 8, 128], BF16); B_mixT = P.buf("mixT")
        st = sb("st", [128, 16]); B_st = P.buf("cst")
        xnb = sb("xnb", [128, D], BF16); B_xnb = P.buf("xnb")
        qT = sb("qT", [128, 16, 128], BF16); B_qT = P.buf("qT")
        sc = sb("sc", [128, 16, 128]); B_sc = P.buf("sc")
        sc2 = sb("sc2", [128, 16, 128]); B_sc2 = P.buf("sc2")
        v16 = sb("v16", [128, 16, 16]); B_v16 = P.buf("v16")
        i16 = sb("i16", [128, 16, 16], U32); B_i16 = P.buf("i16")
        i16f = sb("i16f", [128, 16, 16]); B_i16f = P.buf("i16f")
        junk = sc2[:, :, :].rearrange("p g i -> p (g i)")[:, 0:D]; B_junk = B_sc2
        cand = sc2[:, :, :].rearrange("p (h x) i -> p h (x i)", x=2); B_cand = B_sc2
        cand2 = sc[:, :, :].rearrange("p (h x) i -> p h (x i)", x=2); B_cand2 = B_sc
        oh = sc2[:, :, :].rearrange("p g (x a) -> p (g x) a", a=16); B_oh = B_sc2
        oh2 = sc[:, :, :].rearrange("p g (x a) -> p (g x) a", a=16); B_oh2 = B_sc
        tv = sb("tv", [128, 8, 16]); B_tv = P.buf("tv")
        tpos = sb("tpos", [128, 8, 16], U32); B_tpos = P.buf("tpos")
        ta = sb("ta", [128, 2, 128], U32); B_ta = P.buf("ta")
        taf = sb("taf", [128, 2, 128]); B_taf = P.buf("taf")
        gt = sb("gt", [128, 8, 16]); B_gt = P.buf("gt")
        gs = sb("gs", [128, 16]); B_gs = P.buf("gs")
        selT = sb("selT", [128, 3, 128]); B_selT = P.buf("selT")
        QN = 4
        Aoh = [sb("A%d" % i, [128, QN, 128], BF16) for i in range(2)]; B_A = [P.buf("A0"), P.buf("A1")]
        Boh = [sb("B%d" % i, [128, QN, 128], BF16) for i in range(2)]; B_B = [P.buf("B0"), P.buf("B1")]
        NBUF = 4
        utb = [sb("ut%d" % i, [128, 8, 128], BF16) for i in range(NBUF)]; B_ut = [P.buf("ut%d" % i) for i in range(NBUF)]
        vbb = [sb("vb%d" % i, [128, D], BF16) for i in range(NBUF)]; B_vb = [P.buf("vb%d" % i) for i in range(NBUF)]
        Hs = [sb("Hs%d" % i, [128, G], BF16) for i in range(2)]; B_Hs = [P.buf("Hs0"), P.buf("Hs1")]
        WH = [sb("WH%d" % i, [128, G], BF16) for i in range(2)]; B_WH = [P.buf("WH0"), P.buf("WH1")]
        tpq = ps("tpq", [128, 512]); B_tpq = P.buf("tpq")
        tpq2 = ps("tpq2", [128, 512]); B_tpq2 = P.buf("tpq2")
        tp = tpq[:, :].bitcast(BF16)
        hps = [ps("h%d" % i, [128, 512]) for i in range(2)]; B_h = [P.buf("hp0"), P.buf("hp1")]
        acc = [ps("acc%d" % i, [128, 512]) for i in range(4)]; B_acc = [P.buf("acc%d" % i) for i in range(4)]
        fps = [tpq, tpq2]; B_f = [B_tpq, B_tpq2]

        def tr8(pe, src):
            ins = None
            for c in range(8):
                ins = pe.transpose(out=tp[:, c * 128:(c + 1) * 128], in_=src[:, c * 128:(c + 1) * 128], identity=ident[:, :])
            return ins

        def front(grp):
            gb = grp % 2
            for tt in range(2):
                t = grp * 2 + tt
                x1 = x1g[gb][:, tt, :]
                BX1 = B_x1g[gb]
                P.dma("sp", lambda q: q.dma_start(out=x1, in_=x_src[t * 128:(t + 1) * 128, :]), BX1, writes=[BX1])
                P.dma("sp", lambda q: q.dma_start(out=mx[:, :], in_=sq["mix"][t * 128:(t + 1) * 128, :]), B_mx, writes=[B_mx])
                yield GAP
                yield GAP
                P.op("pe", lambda pe: tr8(pe, mx), reads=[B_mx, B_const], writes=[B_tpq])
                P.op("dve", lambda v: v.tensor_copy(out=mixT[:, :, :].rearrange("p c t -> p (c t)"), in_=tp), reads=[B_tpq], writes=[B_mixT])
                yield GAP
                for g in range(2):
                    def mm(pe, g=g):
                        ins = None
                        for c in range(8):
                            ins = pe.matmul(fps[g][:, :], lhsT=mixT[:, c, :], rhs=wout[:, c, g * 512:(g + 1) * 512], start=(c == 0), stop=(c == 7))
                        return ins
                    P.op("pe", mm, reads=[B_mixT, B_wout], writes=[B_f[g]])
                    P.op("dve", lambda v, g=g: v.tensor_tensor(out=x1[:, g * 512:(g + 1) * 512], in0=fps[g][:, :], in1=x1[:, g * 512:(g + 1) * 512], op=ALU.add),
                         reads=[B_f[g], BX1], writes=[BX1])
                    yield
                P.op("act", lambda a: a.activation(out=junk, in_=x1, func=AF.Square, accum_out=st[:, 0:1]), reads=[BX1], writes=[B_junk, B_st])
                P.op("dve", lambda v: v.tensor_scalar(out=st[:, 1:2], in0=st[:, 0:1], scalar1=1.0 / D, scalar2=EPS, op0=ALU.mult, op1=ALU.add),
                     reads=[B_st], writes=[B_st])
                rsqrt(st[:, 2:3], st[:, 1:2], 1, B_st, B_st)
                r1 = st[:, 2:3]
                yield
                P.op("dve", lambda v: v.scalar_tensor_tensor(out=xnb[:, :], in0=x1, scalar=r1, in1=gffnb[:, :], op0=ALU.mult, op1=ALU.mult),
                     reads=[BX1, B_st, B_g], writes=[B_xnb])
                yield GAP
                P.op("pe", lambda pe: tr8(pe, xnb), reads=[B_xnb, B_const], writes=[B_tpq])
                P.op("dve", lambda v: v.tensor_copy(out=xnT[gb][:, :, tt * 128:(tt + 1) * 128], in_=tp.rearrange("p (c t) -> p c t", t=128)),
                     reads=[B_tpq], writes=[B_xnT[gb]])
                yield GAP
                for half in range(4):
                    def mmq(pe, half=half):
                        ins = None
                        for cc in range(4):
                            col = half * 4 + cc
                            for c in range(8):
                                ins = pe.matmul(fps[half % 2][:, cc * 128:(cc + 1) * 128], lhsT=wq[:, c, col * 128:(col + 1) * 128],
                                                rhs=xnT[gb][:, c, tt * 128:(tt + 1) * 128], start=(c == 0), stop=(c == 7))
                        return ins
                    P.op("pe", mmq, reads=[B_xnT[gb], B_wq], writes=[B_f[half % 2]])
                    P.op("dve", lambda v, half=half: v.tensor_copy(out=qT[:, half * 4:(half + 1) * 4, :].rearrange("p c t -> p (c t)"), in_=fps[half % 2][:, :]),
                         reads=[B_f[half % 2]], writes=[B_qT])
                    yield GAP
                for half in range(4):
                    def mms(pe, half=half):
                        ins = None
                        for cc in range(4):
                            col = half * 4 + cc
                            ins = pe.matmul(fps[half % 2][:, cc * 128:(cc + 1) * 128], lhsT=qT[:, col, :], rhs=keyT[:, col // 2, col % 2, :],
                                            start=True, stop=True)
                        return ins
                    P.op("pe", mms, reads=[B_qT, B_key], writes=[B_f[half % 2]])
                    P.op("dve", lambda v, half=half: v.tensor_copy(out=sc[:, half * 4:(half + 1) * 4, :].rearrange("p c i -> p (c i)"), in_=fps[half % 2][:, :]),
                         reads=[B_f[half % 2]], writes=[B_sc])
                    yield GAP
                for g in range(16):
                    P.op("dve", lambda v, g=g: v.max(out=v16[:, g, 0:8], in_=sc[:, g, :]), reads=[B_sc], writes=[B_v16])
                    P.op("dve", lambda v, g=g: v.max_index(out=i16[:, g, 0:8], in_max=v16[:, g, 0:8], in_values=sc[:, g, :]), reads=[B_sc, B_v16], writes=[B_i16])
                    P.op("dve", lambda v, g=g: v.match_replace(out=sc2[:, g, :], in_to_replace=v16[:, g, 0:8], in_values=sc[:, g, :], imm_value=NEG),
                         reads=[B_sc, B_v16], writes=[B_sc2])
                    yield
                    P.op("dve", lambda v, g=g: v.max(out=v16[:, g, 8:16], in_=sc2[:, g, :]), reads=[B_sc2], writes=[B_v16])
                    P.op("dve", lambda v, g=g: v.max_index(out=i16[:, g, 8:16], in_max=v16[:, g, 8:16], in_values=sc2[:, g, :]), reads=[B_sc2, B_v16], writes=[B_i16])
                    yield
                P.op("dve", lambda v: v.tensor_copy(out=i16f[:, :, :], in_=i16[:, :, :]), reads=[B_i16], writes=[B_i16f])
                v16v = v16[:, :, :].rearrange("p (h j) k -> p h j k", j=2)
                P.op("dve", lambda v: v.tensor_tensor(out=cand.rearrange("p h (a b) -> p h a b", b=16),
                                                      in0=v16v[:, :, 0, :].unsqueeze(3).to_broadcast([128, 8, 16, 16]),
                                                      in1=v16v[:, :, 1, :].unsqueeze(2).to_broadcast([128, 8, 16, 16]), op=ALU.add),
                     reads=[B_v16], writes=[B_cand])
                yield
                for h in range(8):
                    P.op("dve", lambda v, h=h: v.max(out=tv[:, h, 0:8], in_=cand[:, h, :]), reads=[B_cand], writes=[B_tv])
                    P.op("dve", lambda v, h=h: v.max_index(out=tpos[:, h, 0:8], in_max=tv[:, h, 0:8], in_values=cand[:, h, :]), reads=[B_cand, B_tv], writes=[B_tpos])
                    P.op("dve", lambda v, h=h: v.match_replace(out=cand2[:, h, :], in_to_replace=tv[:, h, 0:8], in_values=cand[:, h, :], imm_value=NEG),
                         reads=[B_cand, B_tv], writes=[B_cand2])
                    yield
                    P.op("dve", lambda v, h=h: v.max(out=tv[:, h, 8:16], in_=cand2[:, h, :]), reads=[B_cand2], writes=[B_tv])
                    P.op("dve", lambda v, h=h: v.max_index(out=tpos[:, h, 8:16], in_max=tv[:, h, 8:16], in_values=cand2[:, h, :]), reads=[B_cand2, B_tv], writes=[B_tpos])
                    yield
                tposf = tpos[:, :, :].rearrange("p h k -> p (h k)")
                P.op("dve", lambda v: v.tensor_scalar(out=ta[:, 0, :], in0=tposf, scalar1=4, scalar2=None, op0=ALU.logical_shift_right), reads=[B_tpos], writes=[B_ta])
                P.op("dve", lambda v: v.tensor_scalar(out=ta[:, 1, :], in0=tposf, scalar1=15, scalar2=None, op0=ALU.bitwise_and), reads=[B_tpos], writes=[B_ta])
                P.op("dve", lambda v: v.tensor_copy(out=taf[:, :, :], in_=ta[:, :, :]), reads=[B_ta], writes=[B_taf])
                yield
                i16v = i16f[:, :, :].rearrange("p (h j) k -> p h j k", j=2)
                for j in range(2):
                    P.op("dve", lambda v, j=j: v.tensor_tensor(out=oh, in0=taf[:, j, :].unsqueeze(2).to_broadcast([128, 128, 16]),
                                                               in1=iota16[:, :].unsqueeze(1).to_broadcast([128, 128, 16]), op=ALU.is_equal),
                         reads=[B_taf, B_const], writes=[B_oh])
                    yield
                    P.op("dve", lambda v, j=j: v.tensor_tensor(out=oh2.rearrange("p (h k) a -> p h k a", k=16),
                                                               in0=oh.rearrange("p (h k) a -> p h k a", k=16),
                                                               in1=i16v[:, :, j, :].unsqueeze(2).to_broadcast([128, 8, 16, 16]), op=ALU.mult),
                         reads=[B_oh, B_i16f], writes=[B_oh2])
                    yield
                    P.op("dve", lambda v, j=j: v.tensor_reduce(out=selg[:, tt, j, :], in_=oh2, axis=AX.X, op=ALU.add), reads=[B_oh2], writes=[B_selg])
                    yield
                P.op("dve", lambda v: v.tensor_tensor(out=gt[:, :, :], in0=tv[:, :, :], in1=tv[:, :, 0:1].to_broadcast([128, 8, 16]), op=ALU.subtract),
                     reads=[B_tv], writes=[B_gt])
                P.op("act", lambda a: a.activation(out=gt[:, :, :].rearrange("p h k -> p (h k)"), in_=gt[:, :, :].rearrange("p h k -> p (h k)"), func=AF.Exp),
                     reads=[B_gt], writes=[B_gt])
                P.op("dve", lambda v: v.tensor_reduce(out=gs[:, 0:8], in_=gt[:, :, :], axis=AX.X, op=ALU.add), reads=[B_gt], writes=[B_gs])
                P.op("dve", lambda v: v.reciprocal(out=gs[:, 8:16], in_=gs[:, 0:8]), reads=[B_gs], writes=[B_gs])
                P.op("dve", lambda v: v.tensor_tensor(out=selg[:, tt, 2, :].rearrange("p (h k) -> p h k", k=16), in0=gt[:, :, :],
                                                      in1=gs[:, 8:16].unsqueeze(2).to_broadcast([128, 8, 16]), op=ALU.mult),
                     reads=[B_gt, B_gs], writes=[B_selg])
                yield

        def build(grp):
            ev = 0
            for tt in range(2):
                def trs(pe):
                    ins = None
                    for k3 in range(3):
                        ins = pe.transpose(out=tpq[:, k3 * 128:(k3 + 1) * 128], in_=selg[:, tt, k3, :], identity=identf[:, :])
                    return ins
                P.op("pe", trs, reads=[B_selg, B_const], writes=[B_tpq])
                P.op("act", lambda a: a.copy(out=selT[:, :, :].rearrange("p k c -> p (k c)"), in_=tpq[:, 0:384]), reads=[B_tpq], writes=[B_selT])
                for qi in range(128 // QN):
                    qb = qi % 2

                    def onehotA(v, qi=qi, qb=qb):
                        ins = None
                        for ci in range(QN):
                            c = qi * QN + ci
                            ins = v.tensor_scalar(out=Aoh[qb][:, ci, :], in0=iota128[:, :], scalar1=selT[:, 0, c:c + 1], scalar2=selT[:, 2, c:c + 1],
                                                  op0=ALU.is_equal, op1=ALU.mult)
                        return ins

                    def onehotB(v, qi=qi, qb=qb):
                        ins = None
                        for ci in range(QN):
                            c = qi * QN + ci
                            ins = v.tensor_scalar(out=Boh[qb][:, ci, :], in0=iota128[:, :], scalar1=selT[:, 1, c:c + 1], scalar2=None,
                                                  op0=ALU.is_equal)
                        return ins
                    P.op("dve", onehotA, reads=[B_selT, B_const], writes=[B_A[qb]])
                    P.op(ONEHOT_B_ENG, onehotB, reads=[B_selT, B_const], writes=[B_B[qb]])
                    for c4 in range(QN // 4):
                        wb = ev % 2
                        wp = hps[wb]
                        BW = B_h[wb]

                        def mmw(pe, c4=c4, wp=wp, qb=qb):
                            ins = None
                            for s4 in range(4):
                                ci = c4 * 4 + s4
                                ins = pe.matmul(wp[:, s4 * 128:(s4 + 1) * 128], lhsT=Aoh[qb][:, ci, :], rhs=Boh[qb][:, ci, :], start=True, stop=True)
                            return ins
                        P.op("pe", mmw, reads=[B_A[qb], B_B[qb]], writes=[BW])
                        c0 = tt * 128 + qi * QN + c4 * 4
                        dst = Wg[:, c0:c0 + 4, :].rearrange("p c i -> p (c i)")
                        P.op("act", lambda a, wp=wp, dst=dst: a.copy(out=dst, in_=wp[:, :]), reads=[BW], writes=[B_Wg])
                        ev += 1

        def main(grp, fgen):
            gb = grp % 2

            def load(i):
                b = i % NBUF
                P.dma("sp", lambda q: q.dma_start(out=utb[b][:, :, :].rearrange("p c i -> p (c i)"), in_=UT[i]), B_ut[b], writes=[B_ut[b]])
                P.dma("sp", lambda q: q.dma_start(out=vbb[b][:, :], in_=VB[i]), B_vb[b], writes=[B_vb[b]])

            def hmm(i):
                b = i % NBUF

                def f(pe):
                    ins = None
                    for c in range(8):
                        ins = pe.matmul(hps[i % 2][:, 0:G], lhsT=utb[b][:, c, :], rhs=xnT[gb][:, c, :], start=(c == 0), stop=(c == 7))
                    return ins
                P.op("pe", f, reads=[B_ut[b], B_xnT[gb]], writes=[B_h[i % 2]])

            for i0 in range(NBUF - 1):
                load(i0)
            hmm(0)
            for i in range(128):
                if i + NBUF - 1 < 128:
                    load(i + NBUF - 1)
                if i + 1 < 128:
                    hmm(i + 1)
                p2 = i % 2
                P.op("act", lambda a: a.activation(out=Hs[p2][:, :], in_=hps[p2][:, 0:G], func=AF.Gelu), reads=[B_h[p2]], writes=[B_Hs[p2]])
                P.op(MULT_ENG, lambda v: v.tensor_tensor(out=WH[p2][:, :], in0=Hs[p2][:, :], in1=Wg[:, :, i], op=ALU.mult),
                     reads=[B_Hs[p2], B_Wg], writes=[B_WH[p2]])

                def omm(pe, i=i, p2=p2):
                    ins = None
                    b = i % NBUF
                    for cs in range(2):
                        for dh in range(2):
                            ins = pe.matmul(acc[cs * 2 + dh][:, :], lhsT=WH[p2][:, cs * 128:(cs + 1) * 128], rhs=vbb[b][:, dh * 512:(dh + 1) * 512],
                                            start=(i == 0), stop=(i == 127))
                    return ins
                P.op("pe", omm, reads=[B_WH[p2], B_vb[i % NBUF]], writes=B_acc)
                if fgen is not None:
                    for _ in range(FRONT_PER_CHUNK):
                        if next(fgen, None) is GAP:
                            break
            for cs in range(2):
                for dh in range(2):
                    P.op("dve", lambda v, cs=cs, dh=dh: v.tensor_tensor(out=x1g[gb][:, cs, dh * 512:(dh + 1) * 512], in0=acc[cs * 2 + dh][:, :],
                                                                        in1=x1g[gb][:, cs, dh * 512:(dh + 1) * 512], op=ALU.add),
                         reads=[B_acc[cs * 2 + dh], B_x1g[gb]], writes=[B_x1g[gb]])
            P.dma("sp", lambda q: q.dma_start(out=x_dst[grp * G:(grp + 1) * G, :].rearrange("(c p) d -> p c d", p=128), in_=x1g[gb][:, :, :]),
                  B_x1g[gb], reads=[B_x1g[gb]])

        for _ in front(0):
            pass
        for grp in range(NG):
            build(grp)
            fgen = front(grp + 1) if grp + 1 < NG else None
            main(grp, fgen)
            if fgen is not None:
                for _ in fgen:
                    pass


def make_constants(SM):
    c = {}
    c["ident"] = np.eye(128, dtype=np.float32).astype(ml_dtypes.bfloat16)
    c["identf"] = np.eye(128, dtype=np.float32)
    c["iota16"] = np.tile(np.arange(16, dtype=np.float32)[None, :], (128, 1))
    c["iota128"] = np.tile(np.arange(128, dtype=np.float32)[None, :], (128, 1))
    half = 16
    inv = (np.float32(10000.0) ** (-np.arange(half, dtype=np.float32) / np.float32(half))).astype(np.float32)
    ang = np.arange(SM, dtype=np.float32)[:, None] * inv[None, :]
    c["ropecs"] = np.concatenate([np.cos(ang), np.sin(ang)], axis=1).astype(np.float32)
    pos = np.arange(SM)
    hi = (pos // 128 * 128).astype(np.float32)
    lo = (pos % 128).astype(np.float32)
    one = np.ones(SM, np.float32)
    augq = np.zeros((4, 8, SM), np.float32)
    augk = np.zeros((4, 4, SM), np.float32)
    dbias = np.zeros((4, 128, 4, 512), np.float32)
    srel = np.arange(128)[:, None, None]
    jj = np.arange(4)[None, :, None]
    qrel = np.arange(512)[None, None, :]
    dist = np.abs(qrel - srel - 128 * jj).astype(np.float32)
    for h in range(4):
        m = np.float32(2.0 ** (-8.0 * (h + 1) / 4))
        augq[h, 0] = -m * hi; augq[h, 1] = -m * lo; augq[h, 2] = one; augq[h, 3] = one
        augq[h, 4] = m * hi; augq[h, 5] = m * lo; augq[h, 6] = -one; augq[h, 7] = -one
        augk[h, 0] = one; augk[h, 1] = one; augk[h, 2] = m * hi; augk[h, 3] = m * lo
        dbias[h] = -m * dist
    c["augq"] = augq.astype(ml_dtypes.bfloat16)
    c["augk"] = augk.astype(ml_dtypes.bfloat16)
    c["dbias"] = dbias
    return c


WEIGHT_NAMES = ["norm_mix_g", "w_in", "diff_q_norm_g", "diff_k_norm_g", "lam_q1", "lam_k1", "lam_q2", "lam_k2",
                "diff_subln_g", "mla_q_latent_g", "mla_w_uq", "mla_kv_latent_g", "mla_w_ukv", "mla_q_norm_g",
                "mla_k_norm_g", "w_out", "norm_ffn_g", "peer_w_q", "peer_u", "peer_v"]


def make_in_maps(inputs, n_cores, SP, SS, depth):
    consts = make_constants(max(SP, SS))
    shared = {}
    for n in WEIGHT_NAMES:
        shared[n] = np.ascontiguousarray(np.asarray(inputs[n], dtype=np.float32)[:depth])
    shared["peer_key1T"] = np.ascontiguousarray(np.transpose(np.asarray(inputs["peer_key1"], dtype=np.float32)[:depth], (0, 1, 3, 2)))
    shared["peer_key2T"] = np.ascontiguousarray(np.transpose(np.asarray(inputs["peer_key2"], dtype=np.float32)[:depth], (0, 1, 3, 2)))
    shared.update(consts)
    xp = np.asarray(inputs["x_prompt"], dtype=np.float32)
    xs = np.asarray(inputs["x_sample"], dtype=np.float32)
    maps = []
    for i in range(n_cores):
        m = dict(shared)
        m["xp"] = np.ascontiguousarray(xp[i])
        m["xs"] = np.ascontiguousarray(xs[i])
        maps.append(m)
    return maps


def kernel(**inputs):
    n = 8
    nc = build_program(SP_FULL, SS_FULL, DEPTH)
    in_maps = make_in_maps(inputs, n, SP_FULL, SS_FULL, DEPTH)
    res = run_bass_kernel_spmd(nc, in_maps, core_ids=list(range(n)))
    yp = np.stack([np.asarray(r["yp"], dtype=np.float32) for r in res.results], axis=0)
    ys = np.stack([np.asarray(r["ys"], dtype=np.float32) for r in res.results], axis=0)
    return (yp, ys)
```

```python
import contextlib
import math

import ml_dtypes
import numpy as np

import concourse.bass as bass
import concourse.mybir as mybir
from concourse.bass_utils import run_bass_kernel_spmd

F32 = mybir.dt.float32
BF16 = mybir.dt.bfloat16
U32 = mybir.dt.uint32
I32 = mybir.dt.int32
AF = mybir.ActivationFunctionType
ALU = mybir.AluOpType
AX = mybir.AxisListType

D = 1024
DEPTH = 2
SP_FULL = 8192
SS_FULL = 2048
INW = 2080
EPS = 1e-6
NEG = -3.0e38
PEER_DENSE = True
KPAD = 96
BAND = 70.0
FRONT_PER_CHUNK = 2
MULT_ENG = "pool"
GAP = "gap"
ONEHOT_B_ENG = "dve"
PHASES = "abc"


class Buf:
    __slots__ = ("name", "w", "r", "dsem", "dcount", "slot", "gen")

    def __init__(self, name):
        self.name = name
        self.w = None
        self.r = {}
        self.dsem = None
        self.dcount = 0
        self.slot = None
        self.gen = -1


class Eng:
    def __init__(self, name, obj, sem):
        self.name = name
        self.obj = obj
        self.sem = sem
        self.count = 0
        self.seen = {}


class Prog:
    def __init__(self, nc, es):
        self.nc = nc
        self.es = es
        self.eng = {}
        for n, attr in (("pe", "tensor"), ("act", "scalar"), ("dve", "vector"),
                        ("pool", "gpsimd"), ("sp", "sync")):
            sem = es.enter_context(nc.semaphore("sem_" + n))
            self.eng[n] = Eng(n, getattr(nc, attr), sem)
        self.dma_events = {}
        self.nsem = 5
        self.ninstr = 0
        self.free_slots = []
        self.phase_slots = []
        self.gen = 0

    def buf(self, name):
        return Buf(name)

    def _wait(self, e, ev):
        sem, val = ev
        k = id(sem)
        if e.seen.get(k, 0) >= val:
            return
        e.obj.wait_ge(sem, val)
        e.seen[k] = val
        self.ninstr += 1

    def _deps(self, e, reads, writes, skip_sem=None):
        best = {}

        def add(ev):
            k = id(ev[0])
            if k not in best or best[k][1] < ev[1]:
                best[k] = ev

        for b in reads:
            if b.w is not None:
                add(b.w)
        for b in writes:
            if b.w is not None and not (skip_sem is not None and b.w[0] is skip_sem):
                add(b.w)
            for ev in b.r.values():
                add(ev)
        for k, ev in best.items():
            if e.name == "pe" and ev[0] is e.sem:
                continue
            self._wait(e, ev)

    def _record(self, ev, reads, writes):
        k = id(ev[0])
        for b in reads:
            b.r[k] = ev
        for b in writes:
            b.w = ev
            b.r = {}

    def op(self, en, emit, reads=(), writes=()):
        e = self.eng[en]
        self._deps(e, reads, writes)
        ins = emit(e.obj)
        e.count += 1
        ins.then_inc(e.sem, 1)
        self.ninstr += 1
        ev = (e.sem, e.count)
        self._record(ev, reads, writes)
        return ev

    def dma(self, qn, emit, sem_buf, reads=(), writes=()):
        e = self.eng[qn]
        if sem_buf.dsem is None or sem_buf.gen != self.gen:
            sem_buf.gen = self.gen
            if self.free_slots:
                slot = self.free_slots.pop()
            else:
                slot = [self.es.enter_context(self.nc.semaphore("dsem_%d" % self.nsem)), 0]
                self.nsem += 1
            self.phase_slots.append(slot)
            sem_buf.slot = slot
            sem_buf.dsem = slot[0]
            sem_buf.dcount = slot[1]
        self._deps(e, reads, writes, skip_sem=sem_buf.dsem)
        ins = emit(e.obj)
        sem_buf.dcount += 16
        ins.then_inc(sem_buf.dsem, 16)
        sem_buf.slot[1] = sem_buf.dcount
        self.ninstr += 1
        ev = (sem_buf.dsem, sem_buf.dcount)
        self.dma_events[id(sem_buf.dsem)] = ev
        self._record(ev, reads, writes)
        return ev

    def barrier(self, release=True):
        engs = list(self.eng.values())
        for e in engs:
            for f in engs:
                if f is not e and f.count > 0:
                    self._wait(e, (f.sem, f.count))
            for ev in self.dma_events.values():
                self._wait(e, ev)
        self.dma_events = {}
        self.free_slots.extend(self.phase_slots)
        self.phase_slots = []
        self.gen += 1


def build_program(SP, SS, depth=DEPTH, debug=False):
    nc = bass.Bass("TRN2", target_bir_lowering=False)
    SM = max(SP, SS)

    def din(name, shape, dt=F32):
        return nc.dram_tensor(name, list(shape), dt, kind="ExternalInput").ap()

    def dscr(name, shape, dt):
        kind = "ExternalOutput" if debug else "Internal"
        return nc.dram_tensor(name, list(shape), dt, kind=kind).ap()

    I = {}
    I["xp"] = din("xp", [SP, D])
    I["xs"] = din("xs", [SS, D])
    I["norm_mix_g"] = din("norm_mix_g", [depth, D])
    I["w_in"] = din("w_in", [depth, D, INW])
    I["diff_q_norm_g"] = din("diff_q_norm_g", [depth, 64])
    I["diff_k_norm_g"] = din("diff_k_norm_g", [depth, 64])
    for n in ("lam_q1", "lam_k1", "lam_q2", "lam_k2"):
        I[n] = din(n, [depth, 64])
    I["diff_subln_g"] = din("diff_subln_g", [depth, 128])
    I["mla_q_latent_g"] = din("mla_q_latent_g", [depth, 256])
    I["mla_w_uq"] = din("mla_w_uq", [depth, 256, 384])
    I["mla_kv_latent_g"] = din("mla_kv_latent_g", [depth, 256])
    I["mla_w_ukv"] = din("mla_w_ukv", [depth, 256, 768])
    I["mla_q_norm_g"] = din("mla_q_norm_g", [depth, 96])
    I["mla_k_norm_g"] = din("mla_k_norm_g", [depth, 96])
    I["w_out"] = din("w_out", [depth, D, D])
    I["norm_ffn_g"] = din("norm_ffn_g", [depth, D])
    I["peer_w_q"] = din("peer_w_q", [depth, D, 2048])
    I["peer_key1T"] = din("peer_key1T", [depth, 8, 128, 128])
    I["peer_key2T"] = din("peer_key2T", [depth, 8, 128, 128])
    I["peer_u"] = din("peer_u", [depth, 16384, D])
    I["peer_v"] = din("peer_v", [depth, 16384, D])
    I["ident"] = din("ident", [128, 128], BF16)
    I["identf"] = din("identf", [128, 128], F32)
    I["iota16"] = din("iota16", [128, 16], F32)
    I["ropecs"] = din("ropecs", [SM, 32], F32)
    I["augq"] = din("augq", [4, 8, SM], BF16)
    I["augk"] = din("augk", [4, 4, SM], BF16)
    I["dbias"] = din("dbias", [4, 128, 4, 512], F32)
    I["iota128"] = din("iota128", [128, 128], F32)
    UT = dscr("peerUT", [128, 128, D], BF16)
    VB = dscr("peerVB", [128, 128, D], BF16)

    yp = nc.dram_tensor("yp", [SP, D], F32, kind="ExternalOutput").ap()
    ys = nc.dram_tensor("ys", [SS, D], F32, kind="ExternalOutput").ap()

    seqs = []
    for nm, S, xin, yout in (("p", SP, I["xp"], yp), ("s", SS, I["xs"], ys)):
        seqs.append(dict(
            nm=nm, S=S, xin=xin, yout=yout,
            QdT=dscr("QdT" + nm, [4, 2, 64, S], BF16),
            KdT=dscr("KdT" + nm, [4, 2, 64, S], BF16),
            Vd=dscr("Vd" + nm, [S, 512], BF16),
            QmT=dscr("QmT" + nm, [384, S], BF16),
            KmT=dscr("KmT" + nm, [384, S], BF16),
            Vm=dscr("Vm" + nm, [S, 512], BF16),
            mix=dscr("mix" + nm, [S, D], BF16),
            xmid=dscr("xmid" + nm, [S, D], F32),
        ))

    es = contextlib.ExitStack()
    with es:
        P = Prog(nc, es)

        def sbp(name, shape, dt):
            return es.enter_context(nc.sbuf_tensor(name, list(shape), dt))

        ident = sbp("ident_sb", [128, 128], BF16)
        identf = sbp("identf_sb", [128, 128], F32)
        iota16 = sbp("iota16_sb", [128, 16], F32)
        iota128 = sbp("iota128_sb", [128, 128], F32)
        neghalf = sbp("neghalf", [128, 16], F32)
        B_const = P.buf("const")
        P.dma("sp", lambda q: q.dma_start(out=ident[:], in_=I["ident"][:, :]), B_const, writes=[B_const])
        P.dma("sp", lambda q: q.dma_start(out=identf[:], in_=I["identf"][:, :]), B_const, writes=[B_const])
        P.dma("sp", lambda q: q.dma_start(out=iota16[:], in_=I["iota16"][:, :]), B_const, writes=[B_const])
        P.dma("sp", lambda q: q.dma_start(out=iota128[:], in_=I["iota128"][:, :]), B_const, writes=[B_const])
        B_nh = P.buf("neghalf")
        P.op("pool", lambda g: g.memset(neghalf[:], -0.5), writes=[B_nh])

        def rsqrt(out_ap, in_ap, n, Bout, Bin):
            P.op("pool", lambda g: g.tensor_tensor(out=out_ap, in0=in_ap, in1=neghalf[:, 0:n], op=ALU.pow),
                 reads=[Bin, B_nh], writes=[Bout])

        for l in range(depth):
            if "c" in PHASES and PEER_DENSE:
                prepass_tables(nc, P, I, l, UT, VB, ident, B_const)
                P.barrier(release=True)
            for sq in seqs:
                x_src = sq["xin"] if l == 0 else sq["xmid"]
                x_dst = sq["yout"] if l == depth - 1 else sq["xmid"]
                if "a" in PHASES:
                    phase_a(nc, P, I, sq, l, x_src, ident, B_const, rsqrt)
                    P.barrier(release=True)
                if "b" in PHASES:
                    phase_b(nc, P, I, sq, l, rsqrt, neghalf, B_nh)
                    P.barrier(release=True)
                if "c" in PHASES:
                    if PEER_DENSE:
                        phase_c_dense(nc, P, I, sq, l, x_src, x_dst, ident, identf, iota16, iota128, B_const, rsqrt, UT, VB)
                    else:
                        phase_c(nc, P, I, sq, l, x_src, x_dst, ident, identf, iota16, B_const, rsqrt)
                    P.barrier(release=True)
        build_program.last_ninstr = P.ninstr
    return nc


def phase_a(nc, P, I, sq, l, x_src, ident, B_const, rsqrt):
    S = sq["S"]
    NT = S // 128
    with contextlib.ExitStack() as es:
        def sb(name, shape, dt=F32):
            return es.enter_context(nc.sbuf_tensor("a" + str(l) + sq["nm"] + "_" + name, list(shape), dt))

        def ps(name, shape, dt=F32):
            return es.enter_context(nc.psum_tensor("a" + str(l) + sq["nm"] + "_" + name, list(shape), dt))

        win = sb("win", [128, 8, INW], BF16)
        wuq = sb("wuq", [128, 2, 384], BF16)
        wukv = sb("wukv", [128, 2, 768], BF16)
        gmix = sb("gmix", [128, 8], F32)
        gql = sb("gql", [128, 2], F32)
        gkvl = sb("gkvl", [128, 2], F32)
        gain_qk = sb("gain_qk", [128, 2, 64], F32)
        gain_m = sb("gain_m", [128, 2, 96], F32)
        es_w = contextlib.ExitStack()
        stg = [es_w.enter_context(nc.sbuf_tensor("a" + str(l) + sq["nm"] + "_stg" + str(i), [128, INW], F32)) for i in range(2)]
        B_win, B_wuq, B_wukv = P.buf("win"), P.buf("wuq"), P.buf("wukv")
        B_stg = [P.buf("stg0"), P.buf("stg1")]
        B_g = P.buf("gains")

        with nc.allow_non_contiguous_dma(reason="tiny gain vectors"):
            P.dma("sp", lambda q: q.dma_start(out=gmix[:], in_=I["norm_mix_g"][l].rearrange("(c p) -> p c", p=128)), B_g, writes=[B_g])
            P.dma("sp", lambda q: q.dma_start(out=gql[:], in_=I["mla_q_latent_g"][l].rearrange("(c p) -> p c", p=128)), B_g, writes=[B_g])
            P.dma("sp", lambda q: q.dma_start(out=gkvl[:], in_=I["mla_kv_latent_g"][l].rearrange("(c p) -> p c", p=128)), B_g, writes=[B_g])
        P.dma("sp", lambda q: q.dma_start(out=gain_qk[:, 0, :], in_=I["diff_q_norm_g"][l].partition_broadcast(128)), B_g, writes=[B_g])
        P.dma("sp", lambda q: q.dma_start(out=gain_qk[:, 1, :], in_=I["diff_k_norm_g"][l].partition_broadcast(128)), B_g, writes=[B_g])
        P.dma("sp", lambda q: q.dma_start(out=gain_m[:, 0, :], in_=I["mla_q_norm_g"][l].partition_broadcast(128)), B_g, writes=[B_g])
        P.dma("sp", lambda q: q.dma_start(out=gain_m[:, 1, :], in_=I["mla_k_norm_g"][l].partition_broadcast(128)), B_g, writes=[B_g])

        k = 0
        for c in range(8):
            b = k % 2
            P.dma("sp", lambda q, c=c, b=b: q.dma_start(out=stg[b][:, :], in_=I["w_in"][l, c * 128:(c + 1) * 128, :]),
                  B_stg[b], writes=[B_stg[b]])
            P.op("dve", lambda v, c=c, b=b: v.tensor_scalar(out=win[:, c, :], in0=stg[b][:, :], scalar1=gmix[:, c:c + 1],
                                                            scalar2=None, op0=ALU.mult),
                 reads=[B_stg[b], B_g], writes=[B_win])
            k += 1
        for c in range(2):
            b = k % 2
            P.dma("sp", lambda q, c=c, b=b: q.dma_start(out=stg[b][:, 0:384], in_=I["mla_w_uq"][l, c * 128:(c + 1) * 128, :]),
                  B_stg[b], writes=[B_stg[b]])
            P.op("dve", lambda v, c=c, b=b: v.tensor_scalar(out=wuq[:, c, :], in0=stg[b][:, 0:384], scalar1=gql[:, c:c + 1],
                                                            scalar2=None, op0=ALU.mult),
                 reads=[B_stg[b], B_g], writes=[B_wuq])
            k += 1
        for c in range(2):
            b = k % 2
            P.dma("sp", lambda q, c=c, b=b: q.dma_start(out=stg[b][:, 0:768], in_=I["mla_w_ukv"][l, c * 128:(c + 1) * 128, :]),
                  B_stg[b], writes=[B_stg[b]])
            P.op("dve", lambda v, c=c, b=b: v.tensor_scalar(out=wukv[:, c, :], in0=stg[b][:, 0:768], scalar1=gkvl[:, c:c + 1],
                                                            scalar2=None, op0=ALU.mult),
                 reads=[B_stg[b], B_g], writes=[B_wukv])
            k += 1

        P.barrier()
        es_w.close()
        xt = [sb("xt%d" % i, [128, D]) for i in range(2)]
        B_xt = [P.buf("xt0"), P.buf("xt1")]
        cs = [sb("cs%d" % i, [128, 32]) for i in range(3)]
        B_cs = [P.buf("cs0"), P.buf("cs1"), P.buf("cs2")]
        junk = sb("junk", [128, D]); B_junk = P.buf("junk")
        st = sb("st", [128, 64]); B_st = P.buf("st")
        xb = sb("xb", [128, D], BF16); B_xb = P.buf("xb")
        xT = sb("xT", [128, 8, 128], BF16); B_xT = P.buf("xT")
        sqt = sb("sqt", [128, D]); B_sqt = P.buf("sqt")
        tqk = sb("tqk", [128, 16, 64]); B_tqk = P.buf("tqk")
        qkn = sb("qkn", [128, D], BF16); B_qkn = P.buf("qkn")
        qkT = sb("qkT", [128, 8, 128], BF16); B_qkT = P.buf("qkT")
        vdb = sb("vdb", [128, 512], BF16); B_vdb = P.buf("vdb")
        latb = sb("latb", [128, 512], BF16); B_latb = P.buf("latb")
        latT = sb("latT", [128, 4, 128], BF16); B_latT = P.buf("latT")
        kvs = sb("kvs", [128, 4, 192]); B_kvs = P.buf("kvs")
        krs = sb("krs", [128, 32]); B_krs = P.buf("krs")
        qkms = [sb("qkm%d" % i, [128, 8, 96]) for i in range(2)]; B_qkms = [P.buf("qkm0"), P.buf("qkm1")]
        junk2 = sb("junk2", [128, 768]); B_junk2 = P.buf("junk2")
        sm2 = sb("sm2", [128, 32]); B_sm2 = P.buf("sm2")
        qkm2 = sb("qkm2", [128, 8, 96]); B_qkm2 = P.buf("qkm2")
        rt = sb("rt", [128, 4, 8, 16]); B_rt = P.buf("rt")
        qkmb = sb("qkmb", [128, 8, 96], BF16); B_qkmb = P.buf("qkmb")
        qkmT = sb("qkmT", [128, 6, 128], BF16); B_qkmT = P.buf("qkmT")
        vmb = sb("vmb", [128, 4, 128], BF16); B_vmb = P.buf("vmb")

        tp = ps("tp", [128, 1024], BF16); B_tp = P.buf("tp")
        hps = [ps("h%d" % g, [128, 512]) for g in range(5)]
        B_h = [P.buf("h%d" % g) for g in range(5)]
        mqp = ps("mqp", [128, 512]); B_mqp = P.buf("mqp")
        kvp = ps("kvp", [128, 512]); B_kvp = P.buf("kvp")

        SSQ, T0, RX, RX2A, RX2B, RX2C = 0, 1, 2, 3, 4, 5
        SSG, TG, RG, SC = 8, 24, 40, 8
        sm = sb("sm", [128, 64]); B_sm = P.buf("sm")

        def load(t):
            b = t % 2
            P.dma("sp", lambda q: q.dma_start(out=xt[b][:, :], in_=x_src[t * 128:(t + 1) * 128, :]), B_xt[b], writes=[B_xt[b]])
            P.dma("sp", lambda q: q.dma_start(out=cs[t % 3][:, :], in_=I["ropecs"][t * 128:(t + 1) * 128, :]), B_cs[t % 3], writes=[B_cs[t % 3]])

        tail_gen = [None]

        def pull():
            if tail_gen[0] is not None:
                next(tail_gen[0], None)

        def bop(en, emit, reads=(), writes=()):
            r = P.op(en, emit, reads=reads, writes=writes)
            if not (en == "pe" and any(w is B_tp for w in writes)):
                pull()
            return r

        load(0)
        for t in range(NT):
            b = t % 2
            if t + 1 < NT:
                load(t + 1)
            tok = slice(t * 128, (t + 1) * 128)
            qkm, B_qkm = qkms[b], B_qkms[b]
            X, BX = xt[b], B_xt[b]
            bop("act", lambda a: a.activation(out=junk[:, :], in_=X[:, :], func=AF.Square, accum_out=st[:, SSQ:SSQ + 1]),
                 reads=[BX], writes=[B_junk, B_st])
            bop("dve", lambda v: v.tensor_scalar(out=st[:, T0:T0 + 1], in0=st[:, SSQ:SSQ + 1], scalar1=1.0 / D, scalar2=EPS,
                                                  op0=ALU.mult, op1=ALU.add), reads=[B_st], writes=[B_st])
            rsqrt(st[:, RX:RX + 1], st[:, T0:T0 + 1], 1, B_st, B_st)
            bop("dve", lambda v: v.tensor_scalar(out=st[:, RX2A:RX2A + 1], in0=st[:, RX:RX + 1], scalar1=st[:, RX:RX + 1],
                                                  scalar2=1.0 / 64, op0=ALU.mult, op1=ALU.mult), reads=[B_st], writes=[B_st])
            bop("dve", lambda v: v.tensor_scalar(out=st[:, RX2B:RX2B + 1], in0=st[:, RX:RX + 1], scalar1=st[:, RX:RX + 1],
                                                  scalar2=1.0 / 256, op0=ALU.mult, op1=ALU.mult), reads=[B_st], writes=[B_st])
            rx = st[:, RX:RX + 1]
            bop("dve", lambda v: v.tensor_copy(out=xb[:, :], in_=X[:, :]), reads=[BX], writes=[B_xb])

            def tr8(pe, src=xb, n=8):
                ins = None
                for c in range(n):
                    ins = pe.transpose(out=tp[:, c * 128:(c + 1) * 128], in_=src[:, c * 128:(c + 1) * 128], identity=ident[:, :])
                return ins
            bop("pe", tr8, reads=[B_xb, B_const], writes=[B_tp])
            bop("act", lambda a: a.copy(out=xT[:, :, :].rearrange("p c t -> p (c t)"), in_=tp[:, :]), reads=[B_tp], writes=[B_xT])
            for g in range(5):
                n = 512 if g < 4 else 32

                def mm(pe, g=g, n=n):
                    ins = None
                    for c in range(8):
                        ins = pe.matmul(hps[g][:, 0:n], lhsT=xT[:, c, :], rhs=win[:, c, g * 512:g * 512 + n],
                                        start=(c == 0), stop=(c == 7))
                    return ins
                bop("pe", mm, reads=[B_xT, B_win], writes=[B_h[g]])
            bop("act", lambda a: a.activation(out=sqt[:, 0:512], in_=hps[0][:, :], func=AF.Square), reads=[B_h[0]], writes=[B_sqt])
            bop("act", lambda a: a.activation(out=sqt[:, 512:1024], in_=hps[1][:, :], func=AF.Square), reads=[B_h[1]], writes=[B_sqt])
            bop("dve", lambda v: v.tensor_reduce(out=st[:, SSG:SSG + 16], in_=sqt[:, :].rearrange("p (g d) -> p g d", d=64),
                                                  axis=AX.X, op=ALU.add), reads=[B_sqt], writes=[B_st])
            bop("dve", lambda v: v.tensor_scalar(out=st[:, TG:TG + 16], in0=st[:, SSG:SSG + 16], scalar1=st[:, RX2A:RX2A + 1],
                                                  scalar2=EPS, op0=ALU.mult, op1=ALU.add), reads=[B_st], writes=[B_st])
            rsqrt(st[:, RG:RG + 16], st[:, TG:TG + 16], 16, B_st, B_st)
            bop("dve", lambda v: v.tensor_scalar(out=st[:, SC:SC + 8], in0=st[:, RG:RG + 8], scalar1=rx, scalar2=0.125,
                                                  op0=ALU.mult, op1=ALU.mult), reads=[B_st], writes=[B_st])
            bop("dve", lambda v: v.tensor_scalar(out=st[:, SC + 8:SC + 16], in0=st[:, RG + 8:RG + 16], scalar1=rx, scalar2=None,
                                                  op0=ALU.mult), reads=[B_st], writes=[B_st])
            for half in range(2):
                bop("dve", lambda v, half=half: v.tensor_tensor(
                    out=tqk[:, half * 8:(half + 1) * 8, :], in0=hps[half][:, :].rearrange("p (g d) -> p g d", d=64),
                    in1=st[:, SC + half * 8:SC + half * 8 + 8].unsqueeze(2).to_broadcast([128, 8, 64]), op=ALU.mult),
                    reads=[B_h[half], B_st], writes=[B_tqk])
                bop("dve", lambda v, half=half: v.tensor_tensor(
                    out=qkn[:, half * 512:(half + 1) * 512].rearrange("p (g d) -> p g d", d=64),
                    in0=tqk[:, half * 8:(half + 1) * 8, :],
                    in1=gain_qk[:, half:half + 1, :].to_broadcast([128, 8, 64]), op=ALU.mult),
                    reads=[B_tqk, B_g], writes=[B_qkn])
            bop("pe", lambda pe: tr8(pe, src=qkn), reads=[B_qkn, B_const], writes=[B_tp])
            bop("act", lambda a: a.copy(out=qkT[:, :, :].rearrange("p c t -> p (c t)"), in_=tp[:, :]), reads=[B_tp], writes=[B_qkT])
            P.dma("sp", lambda q: q.dma_start(out=sq["QdT"].rearrange("h j d s -> (j d) h s")[:, :, tok], in_=qkT[:, 0:4, :]),
                  B_qkT, reads=[B_qkT])
            P.dma("sp", lambda q: q.dma_start(out=sq["KdT"].rearrange("h j d s -> (j d) h s")[:, :, tok], in_=qkT[:, 4:8, :]),
                  B_qkT, reads=[B_qkT])
            bop("act", lambda a: a.activation(out=vdb[:, :], in_=hps[2][:, :], func=AF.Copy, scale=rx), reads=[B_h[2], B_st], writes=[B_vdb])
            P.dma("sp", lambda q: q.dma_start(out=sq["Vd"][tok, :], in_=vdb[:, :]), B_vdb, reads=[B_vdb])
            bop("act", lambda a: a.activation(out=junk[:, 0:256], in_=hps[3][:, 0:256], func=AF.Square, accum_out=sm[:, 0:1]),
                 reads=[B_h[3]], writes=[B_junk, B_sm])
            bop("act", lambda a: a.activation(out=junk[:, 256:512], in_=hps[3][:, 256:512], func=AF.Square, accum_out=sm[:, 1:2]),
                 reads=[B_h[3]], writes=[B_junk, B_sm])
            bop("act", lambda a: a.copy(out=latb[:, :], in_=hps[3][:, :]), reads=[B_h[3]], writes=[B_latb])
            bop("pe", lambda pe: tr8(pe, src=latb, n=4), reads=[B_latb, B_const], writes=[B_tp])
            bop("act", lambda a: a.copy(out=latT[:, :, :].rearrange("p c t -> p (c t)"), in_=tp[:, 0:512]), reads=[B_tp], writes=[B_latT])

            def mm_q(pe):
                ins = None
                for c in range(2):
                    ins = pe.matmul(mqp[:, 0:384], lhsT=latT[:, c, :], rhs=wuq[:, c, :], start=(c == 0), stop=(c == 1))
                return ins
            bop("pe", mm_q, reads=[B_latT, B_wuq], writes=[B_mqp])

            def mm_kv(pe, lo, n, dst):
                ins = None
                for c in range(2):
                    ins = pe.matmul(dst[:, 0:n], lhsT=latT[:, 2 + c, :], rhs=wukv[:, c, lo:lo + n], start=(c == 0), stop=(c == 1))
                return ins
            bop("pe", lambda pe: mm_kv(pe, 0, 512, kvp), reads=[B_latT, B_wukv], writes=[B_kvp])
            bop("dve", lambda v: v.tensor_scalar(out=sm[:, 2:4], in0=sm[:, 0:2], scalar1=st[:, RX2B:RX2B + 1], scalar2=EPS,
                                                  op0=ALU.mult, op1=ALU.add), reads=[B_sm, B_st], writes=[B_sm])
            rsqrt(sm[:, 4:6], sm[:, 2:4], 2, B_sm, B_sm)
            bop("dve", lambda v: v.tensor_scalar(out=sm[:, 6:8], in0=sm[:, 4:6], scalar1=rx, scalar2=None, op0=ALU.mult),
                 reads=[B_sm, B_st], writes=[B_sm])
            aq = sm[:, 6:7]
            akv = sm[:, 7:8]
            bop("act", lambda a: a.activation(out=qkm[:, 0:4, :].rearrange("p h d -> p (h d)"), in_=mqp[:, 0:384], func=AF.Copy, scale=aq),
                 reads=[B_mqp, B_sm], writes=[B_qkm])
            bop("act", lambda a: a.activation(out=kvs[:, :, :].rearrange("p h d -> p (h d)")[:, 0:512], in_=kvp[:, 0:512], func=AF.Copy, scale=akv),
                 reads=[B_kvp, B_sm], writes=[B_kvs])
            bop("pe", lambda pe: mm_kv(pe, 512, 256, kvp), reads=[B_latT, B_wukv], writes=[B_kvp])
            bop("act", lambda a: a.activation(out=kvs[:, :, :].rearrange("p h d -> p (h d)")[:, 512:768], in_=kvp[:, 0:256], func=AF.Copy, scale=akv),
                 reads=[B_kvp, B_sm], writes=[B_kvs])
            bop("act", lambda a: a.activation(out=krs[:, :], in_=hps[4][:, 0:32], func=AF.Copy, scale=rx), reads=[B_h[4], B_st], writes=[B_krs])
            bop("dve", lambda v: v.tensor_copy(out=qkm[:, 4:8, 0:64], in_=kvs[:, :, 0:64]), reads=[B_kvs], writes=[B_qkm])
            bop("dve", lambda v: v.tensor_copy(out=qkm[:, 4:8, 64:96], in_=krs[:, :].unsqueeze(1).to_broadcast([128, 4, 32])),
                 reads=[B_krs], writes=[B_qkm])
            bop("dve", lambda v: v.tensor_copy(out=vmb[:, :, :], in_=kvs[:, :, 64:192]), reads=[B_kvs], writes=[B_vmb])
            P.dma("sp", lambda q: q.dma_start(out=sq["Vm"][tok, :], in_=vmb[:, :, :].rearrange("p h e -> p (h e)")), B_vmb, reads=[B_vmb])
            def tail(t=t, b=b, tok=tok, qkm=qkm, B_qkm=B_qkm):
                P.op("act", lambda a: a.activation(out=junk2[:, 0:768], in_=qkm[:, :, :].rearrange("p h d -> p (h d)"), func=AF.Square),
                     reads=[B_qkm], writes=[B_junk2])
                yield
                P.op("dve", lambda v: v.tensor_reduce(out=sm2[:, 8:16], in_=junk2[:, 0:768].rearrange("p (h d) -> p h d", d=96), axis=AX.X, op=ALU.add),
                     reads=[B_junk2], writes=[B_sm2])
                yield
                P.op("dve", lambda v: v.tensor_scalar(out=sm2[:, 16:24], in0=sm2[:, 8:16], scalar1=1.0 / 96, scalar2=EPS, op0=ALU.mult, op1=ALU.add),
                     reads=[B_sm2], writes=[B_sm2])
                yield
                rsqrt(sm2[:, 24:32], sm2[:, 16:24], 8, B_sm2, B_sm2)
                yield
                P.op("dve", lambda v: v.tensor_scalar(out=sm2[:, 24:28], in0=sm2[:, 24:28], scalar1=96.0 ** -0.5, scalar2=None, op0=ALU.mult),
                     reads=[B_sm2], writes=[B_sm2])
                yield
                P.op("dve", lambda v: v.tensor_tensor(out=qkm2[:, :, :], in0=qkm[:, :, :], in1=sm2[:, 24:32].unsqueeze(2).to_broadcast([128, 8, 96]), op=ALU.mult),
                     reads=[B_qkm, B_sm2], writes=[B_qkm2])
                yield
                for half in range(2):
                    P.op("dve", lambda v, half=half: v.tensor_tensor(
                        out=qkm[:, half * 4:(half + 1) * 4, :], in0=qkm2[:, half * 4:(half + 1) * 4, :],
                        in1=gain_m[:, half:half + 1, :].to_broadcast([128, 4, 96]), op=ALU.mult),
                        reads=[B_qkm2, B_g], writes=[B_qkm])
                C = cs[t % 3]
                cosb = C[:, 0:16].unsqueeze(1).to_broadcast([128, 8, 16])
                sinb = C[:, 16:32].unsqueeze(1).to_broadcast([128, 8, 16])
                x1 = qkm[:, :, 64:80]
                x2 = qkm[:, :, 80:96]
                P.op("dve", lambda v: v.tensor_tensor(out=rt[:, 0, :, :], in0=x1, in1=cosb, op=ALU.mult), reads=[B_qkm, B_cs[t % 3]], writes=[B_rt])
                yield
                P.op("dve", lambda v: v.tensor_tensor(out=rt[:, 1, :, :], in0=x2, in1=sinb, op=ALU.mult), reads=[B_qkm, B_cs[t % 3]], writes=[B_rt])
                yield
                P.op("dve", lambda v: v.tensor_tensor(out=rt[:, 2, :, :], in0=x2, in1=cosb, op=ALU.mult), reads=[B_qkm, B_cs[t % 3]], writes=[B_rt])
                yield
                P.op("dve", lambda v: v.tensor_tensor(out=rt[:, 3, :, :], in0=x1, in1=sinb, op=ALU.mult), reads=[B_qkm, B_cs[t % 3]], writes=[B_rt])
                yield
                P.op("dve", lambda v: v.tensor_copy(out=qkmb[:, :, 0:64], in_=qkm[:, :, 0:64]), reads=[B_qkm], writes=[B_qkmb])
                yield
                P.op("dve", lambda v: v.tensor_tensor(out=qkmb[:, :, 64:80], in0=rt[:, 0, :, :], in1=rt[:, 1, :, :], op=ALU.subtract),
                     reads=[B_rt], writes=[B_qkmb])
                yield
                P.op("dve", lambda v: v.tensor_tensor(out=qkmb[:, :, 80:96], in0=rt[:, 2, :, :], in1=rt[:, 3, :, :], op=ALU.add),
                     reads=[B_rt], writes=[B_qkmb])
                yield

                def tr6(pe):
                    ins = None
                    src = qkmb[:, :, :].rearrange("p h d -> p (h d)")
                    for c in range(6):
                        ins = pe.transpose(out=tp[:, c * 128:(c + 1) * 128], in_=src[:, c * 128:(c + 1) * 128], identity=ident[:, :])
                    return ins
                P.op("pe", tr6, reads=[B_qkmb, B_const], writes=[B_tp])
                P.op("act", lambda a: a.copy(out=qkmT[:, :, :].rearrange("p c t -> p (c t)"), in_=tp[:, 0:768]), reads=[B_tp], writes=[B_qkmT])
                yield
                P.dma("sp", lambda q: q.dma_start(out=sq["QmT"].rearrange("(c p) s -> p c s", p=128)[:, :, tok], in_=qkmT[:, 0:3, :]),
                      B_qkmT, reads=[B_qkmT])
                yield
                P.dma("sp", lambda q: q.dma_start(out=sq["KmT"].rearrange("(c p) s -> p c s", p=128)[:, :, tok], in_=qkmT[:, 3:6, :]),
                      B_qkmT, reads=[B_qkmT])
                yield

            if tail_gen[0] is not None:
                for _ in tail_gen[0]:
                    pass
            tail_gen[0] = tail()
        if tail_gen[0] is not None:
            for _ in tail_gen[0]:
                pass


def phase_b(nc, P, I, sq, l, rsqrt, neghalf, B_nh):
    S = sq["S"]
    NB = S // 128
    NQ = S // 512
    lam_init = 0.8 - 0.6 * math.exp(-0.3 * l)
    with contextlib.ExitStack() as es:
        def sb(name, shape, dt=F32):
            return es.enter_context(nc.sbuf_tensor("b" + str(l) + sq["nm"] + "_" + name, list(shape), dt))

        def ps(name, shape, dt=F32):
            return es.enter_context(nc.psum_tensor("b" + str(l) + sq["nm"] + "_" + name, list(shape), dt))

        NSETS = 2 if S <= 2048 else 1
        KTs = [sb("KT%d" % j, [96, S], BF16) for j in range(2 * NSETS)]
        QAs = [sb("QA%d" % j, [96, S], BF16) for j in range(2 * NSETS)]
        QBs = [sb("QB%d" % j, [96, S], BF16) for j in range(2 * NSETS)]
        Vs = [sb("V%d" % j, [128, NB, 132], BF16) for j in range(NSETS)]
        dbiass = [sb("dbias%d" % j, [128, 4, 512], F32) for j in range(NSETS)]
        B_KTs = [P.buf("KT") for j in range(2 * NSETS)]
        B_QAs = [P.buf("QA") for j in range(2 * NSETS)]
        B_QBs = [P.buf("QB") for j in range(2 * NSETS)]
        B_Vs = [P.buf("V") for j in range(NSETS)]
        B_dbs = [P.buf("dbias") for j in range(NSETS)]
        PT = [sb("PT%d" % i, [128, 512], BF16) for i in range(3)]
        B_PT = [P.buf("PT%d" % i) for i in range(3)]
        Sf = sb("Sf", [128, 512], F32); B_Sf = P.buf("Sf")
        on = [sb("on%d" % j, [128, 4, 128], F32) for j in range(2)]
        B_on = [P.buf("on0"), P.buf("on1")]
        rz = sb("rz", [128, 8], F32); B_rz = P.buf("rz")
        od = sb("od", [128, 4, 128], F32); B_od = P.buf("od")
        od2 = sb("od2", [128, 4, 128], F32); B_od2 = P.buf("od2")
        jk = sb("jk", [128, 512], F32); B_jk = P.buf("jk")
        ob = [sb("ob%d" % i, [128, 4, 128], BF16) for i in range(2)]
        B_ob = [P.buf("ob0"), P.buf("ob1")]
        st_ = sb("st", [128, 16], F32); B_st = P.buf("bst")
        lamv = sb("lamv", [128, 4, 64], F32)
        lamt = sb("lamt", [128, 8], F32)
        subg = sb("subg", [128, 128], F32)
        B_lam = P.buf("lam")
        Sps = [ps("S%d" % i, [128, 512]) for i in range(3)]
        B_S = [P.buf("S0"), P.buf("S1"), P.buf("S2")]
        O4 = ps("O4", [128, 4, 512])
        Ops = [O4[:, i, :] for i in range(4)]
        B_O = [P.buf("O%d" % i) for i in range(4)]

        for i, n in enumerate(("lam_q1", "lam_k1", "lam_q2", "lam_k2")):
            P.dma("sp", lambda q, i=i, n=n: q.dma_start(out=lamv[:, i, :], in_=I[n][l].partition_broadcast(128)), B_lam, writes=[B_lam])
        P.dma("sp", lambda q: q.dma_start(out=subg[:, :], in_=I["diff_subln_g"][l].partition_broadcast(128)), B_lam, writes=[B_lam])
        P.op("dve", lambda v: v.tensor_tensor(out=lamv[:, 0, :], in0=lamv[:, 0, :], in1=lamv[:, 1, :], op=ALU.mult), reads=[B_lam], writes=[B_lam])
        P.op("dve", lambda v: v.tensor_tensor(out=lamv[:, 2, :], in0=lamv[:, 2, :], in1=lamv[:, 3, :], op=ALU.mult), reads=[B_lam], writes=[B_lam])
        P.op("dve", lambda v: v.tensor_reduce(out=lamt[:, 0:1], in_=lamv[:, 0, :], axis=AX.X, op=ALU.add), reads=[B_lam], writes=[B_lam])
        P.op("dve", lambda v: v.tensor_reduce(out=lamt[:, 1:2], in_=lamv[:, 2, :], axis=AX.X, op=ALU.add), reads=[B_lam], writes=[B_lam])
        P.op("act", lambda a: a.activation(out=lamt[:, 2:4], in_=lamt[:, 0:2], func=AF.Exp), reads=[B_lam], writes=[B_lam])
        P.op("dve", lambda v: v.tensor_tensor(out=lamt[:, 4:5], in0=lamt[:, 3:4], in1=lamt[:, 2:3], op=ALU.subtract), reads=[B_lam], writes=[B_lam])
        P.op("dve", lambda v: v.tensor_scalar(out=lamt[:, 5:6], in0=lamt[:, 4:5], scalar1=-lam_init, scalar2=None, op0=ALU.add), reads=[B_lam], writes=[B_lam])
        neglam = lamt[:, 5:6]
        for V_, B_V_ in zip(Vs, B_Vs):
            P.op("pool", lambda g, V_=V_: g.memset(V_[:, :, 128:132], 1.0), writes=[B_V_])
        for j in range(2 * NSETS):
            P.op("pool", lambda g, j=j: g.memset(KTs[j][64:96, :], 0.0), writes=[B_KTs[j]])
            P.op("pool", lambda g, j=j: g.memset(QAs[j][64:96, :], 0.0), writes=[B_QAs[j]])
            P.op("pool", lambda g, j=j: g.memset(QBs[j][64:96, :], 0.0), writes=[B_QBs[j]])

        pt_i = [0]
        s_i = [0]

        def attention(maps, finish, V, B_V, dbias, B_db):
            units = []
            for qt in range(NQ):
                for j, m in enumerate(maps):
                    blks = []
                    for blk in range(NB):
                        if m["alibi"]:
                            q0, s0 = qt * 512, blk * 128
                            dmin = max(0, s0 - (q0 + 511), q0 - (s0 + 127))
                            if m["slope"] * dmin > BAND:
                                continue
                        blks.append(blk)
                    for n_, blk in enumerate(blks):
                        units.append((qt, j, blk, n_ == 0, n_ == len(blks) - 1))

            def qk(u):
                qt, j, blk, first, last = units[u]
                m = maps[j]
                sbuf_i = u % 3
                rel = blk - 4 * qt
                bs = slice(blk * 128, (blk + 1) * 128)
                qs = slice(qt * 512, (qt + 1) * 512)
                if m["alibi"]:
                    if rel < 0:
                        Kr, Q, BQ = KPAD, m["QA"], m["B_QA"]
                    elif rel >= 4:
                        Kr, Q, BQ = KPAD, m["QB"], m["B_QB"]
                    else:
                        Kr, Q, BQ = 64, m["QA"], m["B_QA"]
                else:
                    Kr, Q, BQ = 96, m["QA"], m["B_QA"]
                P.op("pe", lambda pe: pe.matmul(Sps[sbuf_i][:, :], lhsT=m["KT"][0:Kr, bs], rhs=Q[0:Kr, qs], start=True, stop=True),
                     reads=[m["B_KT"], BQ], writes=[B_S[sbuf_i]])

            def ex(u):
                qt, j, blk, first, last = units[u]
                m = maps[j]
                sbuf_i = u % 3
                pi = u % 3
                rel = blk - 4 * qt
                if m["alibi"] and 0 <= rel < 4:
                    P.op("dve", lambda v: v.tensor_tensor(out=Sf[:, :], in0=Sps[sbuf_i][:, :], in1=dbias[:, rel, :], op=ALU.add),
                         reads=[B_S[sbuf_i], B_db], writes=[B_Sf])
                    P.op("act", lambda a: a.activation(out=PT[pi][:, :], in_=Sf[:, :], func=AF.Exp), reads=[B_Sf], writes=[B_PT[pi]])
                else:
                    P.op("act", lambda a: a.activation(out=PT[pi][:, :], in_=Sps[sbuf_i][:, :], func=AF.Exp),
                         reads=[B_S[sbuf_i]], writes=[B_PT[pi]])

            def av(u):
                qt, j, blk, first, last = units[u]
                pi = u % 3

                def f(pe):
                    ins = None
                    for i in range(4):
                        ins = pe.matmul(Ops[i][:, 0:129], lhsT=PT[pi][:, i * 128:(i + 1) * 128], rhs=V[:, blk, 0:129],
                                        start=first, stop=last)
                    return ins
                P.op("pe", f, reads=[B_PT[pi], B_V], writes=B_O)
                if last:
                    P.op("dve", lambda v: v.reciprocal(out=rz[:, 0:4].unsqueeze(2), in_=O4[:, :, 128:129]), reads=B_O, writes=[B_rz])
                    P.op("dve", lambda v: v.tensor_tensor(out=on[j][:, :, :], in0=O4[:, :, 0:128],
                                                          in1=rz[:, 0:4].unsqueeze(2).to_broadcast([128, 4, 128]), op=ALU.mult),
                         reads=B_O + [B_rz], writes=[B_on[j]])
                    if j == len(maps) - 1:
                        finish(qt)

            n = len(units)
            qk(0)
            if n > 1:
                qk(1)
            for u in range(n):
                if u + 2 < n:
                    qk(u + 2)
                ex(u)
                av(u)

        ob_i = [0]

        def load_diff(h, st):
            KT, QA, QB = KTs[2 * st:2 * st + 2], QAs[2 * st:2 * st + 2], QBs[2 * st:2 * st + 2]
            B_KT, B_QA, B_QB = B_KTs[2 * st:2 * st + 2], B_QAs[2 * st:2 * st + 2], B_QBs[2 * st:2 * st + 2]
            V, B_V, dbias, B_db = Vs[st], B_Vs[st], dbiass[st], B_dbs[st]
            for j in range(2):
                P.dma("sp", lambda q, j=j: q.dma_start(out=KT[j][0:64, :], in_=sq["KdT"][h, j, :, :]), B_KT[j], writes=[B_KT[j]])
                P.dma("sp", lambda q, j=j: q.dma_start(out=KT[j][64:68, :], in_=I["augk"][h, :, 0:S]), B_KT[j], writes=[B_KT[j]])
                P.dma("sp", lambda q, j=j: q.dma_start(out=QA[j][0:64, :], in_=sq["QdT"][h, j, :, :]), B_QA[j], writes=[B_QA[j]])
                P.dma("sp", lambda q, j=j: q.dma_start(out=QA[j][64:68, :], in_=I["augq"][h, 0:4, 0:S]), B_QA[j], writes=[B_QA[j]])
                P.dma("sp", lambda q, j=j: q.dma_start(out=QB[j][0:64, :], in_=sq["QdT"][h, j, :, :]), B_QB[j], writes=[B_QB[j]])
                P.dma("sp", lambda q, j=j: q.dma_start(out=QB[j][64:68, :], in_=I["augq"][h, 4:8, 0:S]), B_QB[j], writes=[B_QB[j]])
            P.dma("sp", lambda q: q.dma_start(out=V[:, :, 0:128], in_=sq["Vd"].rearrange("(b p) e -> p b e", p=128)[:, :, h * 128:(h + 1) * 128]),
                  B_V, writes=[B_V])
            P.dma("sp", lambda q: q.dma_start(out=dbias[:, :, :], in_=I["dbias"][h]), B_db, writes=[B_db])

        def run_diff(h, st):
            KT, QA, QB = KTs[2 * st:2 * st + 2], QAs[2 * st:2 * st + 2], QBs[2 * st:2 * st + 2]
            B_KT, B_QA, B_QB = B_KTs[2 * st:2 * st + 2], B_QAs[2 * st:2 * st + 2], B_QBs[2 * st:2 * st + 2]

            def finish_diff(qt, h=h):
                oi = ob_i[0] % 2
                ob_i[0] += 1
                flat = lambda t: t[:, :, :].rearrange("p i e -> p (i e)")
                P.op("dve", lambda v: v.scalar_tensor_tensor(out=flat(od), in0=flat(on[1]), scalar=neglam, in1=flat(on[0]),
                                                             op0=ALU.mult, op1=ALU.add),
                     reads=[B_on[0], B_on[1], B_lam], writes=[B_od])
                P.op("dve", lambda v: v.tensor_tensor(out=jk[:, :], in0=flat(od), in1=flat(od), op=ALU.mult), reads=[B_od], writes=[B_jk])
                P.op("dve", lambda v: v.tensor_reduce(out=st_[:, 0:4], in_=jk[:, :].rearrange("p (i e) -> p i e", e=128), axis=AX.X, op=ALU.add),
                     reads=[B_jk], writes=[B_st])
                P.op("dve", lambda v: v.tensor_scalar(out=st_[:, 4:8], in0=st_[:, 0:4], scalar1=1.0 / 128, scalar2=EPS, op0=ALU.mult, op1=ALU.add),
                     reads=[B_st], writes=[B_st])
                rsqrt(st_[:, 8:12], st_[:, 4:8], 4, B_st, B_st)
                P.op("dve", lambda v: v.tensor_scalar(out=st_[:, 12:16], in0=st_[:, 8:12], scalar1=1.0 - lam_init, scalar2=None, op0=ALU.mult),
                     reads=[B_st], writes=[B_st])
                P.op("dve", lambda v: v.tensor_tensor(out=od2[:, :, :], in0=od[:, :, :], in1=st_[:, 12:16].unsqueeze(2).to_broadcast([128, 4, 128]), op=ALU.mult),
                     reads=[B_od, B_st], writes=[B_od2])
                P.op("dve", lambda v: v.tensor_tensor(out=ob[oi][:, :, :], in0=od2[:, :, :], in1=subg[:, :].unsqueeze(1).to_broadcast([128, 4, 128]), op=ALU.mult),
                     reads=[B_od2, B_lam], writes=[B_ob[oi]])
                P.dma("sp", lambda q: q.dma_start(
                    out=sq["mix"][qt * 512:(qt + 1) * 512, h * 128:(h + 1) * 128].rearrange("(i p) e -> p i e", p=128),
                    in_=ob[oi][:, :, :]), B_ob[oi], reads=[B_ob[oi]])

            maps = [dict(KT=KT[j], QA=QA[j], QB=QB[j], B_KT=B_KT[j], B_QA=B_QA[j], B_QB=B_QB[j], alibi=True,
                         slope=2.0 ** (-8.0 * (h + 1) / 4)) for j in range(2)]
            attention(maps, finish_diff, Vs[st], B_Vs[st], dbiass[st], B_dbs[st])

        def load_mla(h, st):
            P.dma("sp", lambda q: q.dma_start(out=KTs[2 * st][0:96, :], in_=sq["KmT"][h * 96:(h + 1) * 96, :]), B_KTs[2 * st], writes=[B_KTs[2 * st]])
            P.dma("sp", lambda q: q.dma_start(out=QAs[2 * st][0:96, :], in_=sq["QmT"][h * 96:(h + 1) * 96, :]), B_QAs[2 * st], writes=[B_QAs[2 * st]])
            P.dma("sp", lambda q: q.dma_start(out=Vs[st][:, :, 0:128], in_=sq["Vm"].rearrange("(b p) e -> p b e", p=128)[:, :, h * 128:(h + 1) * 128]),
                  B_Vs[st], writes=[B_Vs[st]])

        def run_mla(h, st):
            def finish_mla(qt, h=h):
                oi = ob_i[0] % 2
                ob_i[0] += 1
                P.op("dve", lambda v: v.tensor_copy(out=ob[oi][:, :, :], in_=on[0][:, :, :]), reads=[B_on[0]], writes=[B_ob[oi]])
                P.dma("sp", lambda q: q.dma_start(
                    out=sq["mix"][qt * 512:(qt + 1) * 512, 512 + h * 128:512 + (h + 1) * 128].rearrange("(i p) e -> p i e", p=128),
                    in_=ob[oi][:, :, :]), B_ob[oi], reads=[B_ob[oi]])

            maps = [dict(KT=KTs[2 * st], QA=QAs[2 * st], QB=None, B_KT=B_KTs[2 * st], B_QA=B_QAs[2 * st], B_QB=None, alibi=False)]
            attention(maps, finish_mla, Vs[st], B_Vs[st], dbiass[st], B_dbs[st])

        jobs = [(load_diff, run_diff, h) for h in range(4)] + [(load_mla, run_mla, h) for h in range(4)]
        if NSETS == 2:
            jobs[0][0](jobs[0][2], 0)
            for i, (ld, rn, h) in enumerate(jobs):
                if i + 1 < len(jobs):
                    jobs[i + 1][0](jobs[i + 1][2], (i + 1) % 2)
                rn(h, i % 2)
        else:
            for ld, rn, h in jobs:
                ld(h, 0)
                rn(h, 0)


def phase_c(nc, P, I, sq, l, x_src, x_dst, ident, identf, iota16, B_const, rsqrt):
    S = sq["S"]
    NT = S // 128
    NSLOT = 4
    with contextlib.ExitStack() as es:
        def sb(name, shape, dt=F32):
            return es.enter_context(nc.sbuf_tensor("c" + str(l) + sq["nm"] + "_" + name, list(shape), dt))

        def ps(name, shape, dt=F32):
            return es.enter_context(nc.psum_tensor("c" + str(l) + sq["nm"] + "_" + name, list(shape), dt))

        wout = sb("wout", [128, 8, D], BF16); B_wout = P.buf("wout")
        wq = sb("wq", [128, 8, 2048], BF16); B_wq = P.buf("wq")
        keyT = sb("keyT", [128, 8, 2, 128], BF16); B_key = P.buf("keyT")
        gffn = sb("gffn", [128, 8], F32)
        gffnb = sb("gffnb", [128, D], F32)
        B_g = P.buf("cg")
        es_w = contextlib.ExitStack()
        stg = [es_w.enter_context(nc.sbuf_tensor("c" + str(l) + sq["nm"] + "_stg" + str(i), [128, 2048], F32)) for i in range(2)]
        B_stg = [P.buf("cstg0"), P.buf("cstg1")]
        with nc.allow_non_contiguous_dma(reason="tiny gain vector"):
            P.dma("sp", lambda q: q.dma_start(out=gffn[:], in_=I["norm_ffn_g"][l].rearrange("(c p) -> p c", p=128)), B_g, writes=[B_g])
        P.dma("sp", lambda q: q.dma_start(out=gffnb[:, :], in_=I["norm_ffn_g"][l].partition_broadcast(128)), B_g, writes=[B_g])
        k = 0
        for c in range(8):
            b = k % 2
            P.dma("sp", lambda q, c=c, b=b: q.dma_start(out=stg[b][:, 0:D], in_=I["w_out"][l, c * 128:(c + 1) * 128, :]), B_stg[b], writes=[B_stg[b]])
            P.op("dve", lambda v, c=c, b=b: v.tensor_copy(out=wout[:, c, :], in_=stg[b][:, 0:D]), reads=[B_stg[b]], writes=[B_wout])
            k += 1
        for c in range(8):
            b = k % 2
            P.dma("sp", lambda q, c=c, b=b: q.dma_start(out=stg[b][:, :], in_=I["peer_w_q"][l, c * 128:(c + 1) * 128, :]), B_stg[b], writes=[B_stg[b]])
            P.op("dve", lambda v, c=c, b=b: v.tensor_scalar(out=wq[:, c, :], in0=stg[b][:, :], scalar1=gffn[:, c:c + 1], scalar2=None, op0=ALU.mult),
                 reads=[B_stg[b], B_g], writes=[B_wq])
            k += 1
        for j, nm in enumerate(("peer_key1T", "peer_key2T")):
            b = k % 2
            P.dma("sp", lambda q, nm=nm, b=b: q.dma_start(out=stg[b][:, 0:1024].rearrange("p (h i) -> p h i", i=128),
                                                         in_=I[nm][l].rearrange("h d i -> d h i")), B_stg[b], writes=[B_stg[b]])
            P.op("dve", lambda v, j=j, b=b: v.tensor_copy(out=keyT[:, :, j, :], in_=stg[b][:, 0:1024].rearrange("p (h i) -> p h i", i=128)),
                 reads=[B_stg[b]], writes=[B_key])
            k += 1

        P.barrier()
        es_w.close()
        xt = [sb("xt%d" % i, [128, D]) for i in range(2)]
        B_xt = [P.buf("cxt0"), P.buf("cxt1")]
        mx = [sb("mx%d" % i, [128, D], BF16) for i in range(2)]
        B_mx = [P.buf("mx0"), P.buf("mx1")]
        mixT = sb("mixT", [128, 8, 128], BF16); B_mixT = P.buf("mixT")
        x1 = sb("x1", [128, D]); B_x1 = P.buf("x1")
        junk = sb("junk", [128, D]); B_junk = P.buf("cjunk")
        st = sb("st", [128, 16]); B_st = P.buf("cst")
        x1b = sb("x1b", [128, D], BF16); B_x1b = P.buf("x1b")
        x1T = sb("x1T", [128, 8, 128], BF16); B_x1T = P.buf("x1T")
        xn = sb("xn", [128, D]); B_xn = P.buf("xn")
        qT = sb("qT", [128, 16, 128], BF16); B_qT = P.buf("qT")
        sc = sb("sc", [128, 16, 128]); B_sc = P.buf("sc")
        sc2 = sb("sc2", [128, 16, 128]); B_sc2 = P.buf("sc2")
        v16 = sb("v16", [128, 16, 16]); B_v16 = P.buf("v16")
        i16 = sb("i16", [128, 16, 16], U32); B_i16 = P.buf("i16")
        i16f = sb("i16f", [128, 16, 16]); B_i16f = P.buf("i16f")
        cand = sb("cand", [128, 8, 256]); B_cand = P.buf("cand")
        cand2 = sb("cand2", [128, 8, 256]); B_cand2 = P.buf("cand2")
        tv = sb("tv", [128, 8, 16]); B_tv = P.buf("tv")
        tpos = sb("tpos", [128, 8, 16], U32); B_tpos = P.buf("tpos")
        ta = sb("ta", [128, 2, 128], U32); B_ta = P.buf("ta")
        taf = sb("taf", [128, 2, 128]); B_taf = P.buf("taf")
        oh = sc2[:, :, :].rearrange("p g (x a) -> p (g x) a", a=16); B_oh = B_sc2
        oh2 = cand2[:, :, :].rearrange("p h (k a) -> p (h k) a", a=16); B_oh2 = B_cand2
        isel = sb("isel", [128, 2, 128]); B_isel = P.buf("isel")
        eidf = sb("eidf", [128, 128]); B_eidf = P.buf("eidf")
        eid = sb("eid", [128, 128], I32); B_eid = P.buf("eid")
        gt = sb("gt", [128, 8, 16]); B_gt = P.buf("gt")
        gs = sb("gs", [128, 16]); B_gs = P.buf("gs")
        hraw = sb("hraw", [128, 128]); B_hraw = P.buf("hraw")
        hg = sb("hg", [128, 128]); B_hg = P.buf("hg")
        wgt = sb("wgt", [128, 128]); B_wgt = P.buf("wgt")
        ug = [sb("ug%d" % i, [128, D]) for i in range(NSLOT)]
        B_ug = [P.buf("ug%d" % i) for i in range(NSLOT)]
        vg = ug; B_vg = B_ug
        vgb = [sb("vgb%d" % i, [128, D], BF16) for i in range(2)]
        B_vgb = [P.buf("vgb0"), P.buf("vgb1")]
        dg = [sb("dg%d" % i, [128, 128], BF16) for i in range(2)]
        B_dg = [P.buf("dg0"), P.buf("dg1")]
        prod = junk; B_prod = B_junk
        xo = [sb("xo%d" % i, [128, D]) for i in range(2)]
        B_xo = [P.buf("xo0"), P.buf("xo1")]

        tp = ps("tp", [128, 1024], BF16); B_tp = P.buf("ctp")
        yps = [ps("y%d" % i, [128, 512]) for i in range(2)]; B_y = [P.buf("y0"), P.buf("y1")]
        qps = [ps("q%d" % i, [128, 512]) for i in range(2)]; B_q = [P.buf("q0"), P.buf("q1")]
        aps = [ps("acc%d" % i, [128, 512]) for i in range(2)]; B_acc = [P.buf("acc0"), P.buf("acc1")]

        def load(t):
            b = t % 2
            P.dma("sp", lambda q: q.dma_start(out=xt[b][:, :], in_=x_src[t * 128:(t + 1) * 128, :]), B_xt[b], writes=[B_xt[b]])
            P.dma("sp", lambda q: q.dma_start(out=mx[b][:, :], in_=sq["mix"][t * 128:(t + 1) * 128, :]), B_mx[b], writes=[B_mx[b]])

        def tr8(pe, src):
            ins = None
            for c in range(8):
                ins = pe.transpose(out=tp[:, c * 128:(c + 1) * 128], in_=src[:, c * 128:(c + 1) * 128], identity=ident[:, :])
            return ins

        load(0)
        for t in range(NT):
            b = t % 2
            if t + 1 < NT:
                load(t + 1)
            tok = slice(t * 128, (t + 1) * 128)
            P.op("pe", lambda pe: tr8(pe, mx[b]), reads=[B_mx[b], B_const], writes=[B_tp])
            P.op("act", lambda a: a.copy(out=mixT[:, :, :].rearrange("p c t -> p (c t)"), in_=tp[:, :]), reads=[B_tp], writes=[B_mixT])
            for g in range(2):
                def mm(pe, g=g):
                    ins = None
                    for c in range(8):
                        ins = pe.matmul(yps[g][:, :], lhsT=mixT[:, c, :], rhs=wout[:, c, g * 512:(g + 1) * 512], start=(c == 0), stop=(c == 7))
                    return ins
                P.op("pe", mm, reads=[B_mixT, B_wout], writes=[B_y[g]])
                P.op("dve", lambda v, g=g: v.tensor_tensor(out=x1[:, g * 512:(g + 1) * 512], in0=yps[g][:, :], in1=xt[b][:, g * 512:(g + 1) * 512], op=ALU.add),
                     reads=[B_y[g], B_xt[b]], writes=[B_x1])
            P.op("act", lambda a: a.activation(out=junk[:, 0:D], in_=x1[:, :], func=AF.Square, accum_out=st[:, 0:1]), reads=[B_x1], writes=[B_junk, B_st])
            P.op("dve", lambda v: v.tensor_scalar(out=st[:, 1:2], in0=st[:, 0:1], scalar1=1.0 / D, scalar2=EPS, op0=ALU.mult, op1=ALU.add),
                 reads=[B_st], writes=[B_st])
            rsqrt(st[:, 2:3], st[:, 1:2], 1, B_st, B_st)
            r1 = st[:, 2:3]
            P.op("dve", lambda v: v.tensor_copy(out=x1b[:, :], in_=x1[:, :]), reads=[B_x1], writes=[B_x1b])
            P.op("pe", lambda pe: tr8(pe, x1b), reads=[B_x1b, B_const], writes=[B_tp])
            P.op("act", lambda a: a.copy(out=x1T[:, :, :].rearrange("p c t -> p (c t)"), in_=tp[:, :]), reads=[B_tp], writes=[B_x1T])
            P.op("dve", lambda v: v.scalar_tensor_tensor(out=xn[:, :], in0=x1[:, :], scalar=r1, in1=gffnb[:, :], op0=ALU.mult, op1=ALU.mult),
                 reads=[B_x1, B_st, B_g], writes=[B_xn])
            for half in range(4):
                def mmq(pe, half=half):
                    ins = None
                    for cc in range(4):
                        col = half * 4 + cc
                        for c in range(8):
                            ins = pe.matmul(qps[half % 2][:, cc * 128:(cc + 1) * 128], lhsT=wq[:, c, col * 128:(col + 1) * 128], rhs=x1T[:, c, :],
                                            start=(c == 0), stop=(c == 7))
                    return ins
                P.op("pe", mmq, reads=[B_x1T, B_wq], writes=[B_q[half % 2]])
                P.op("act", lambda a, half=half: a.copy(out=qT[:, half * 4:(half + 1) * 4, :].rearrange("p c t -> p (c t)"), in_=qps[half % 2][:, :]),
                     reads=[B_q[half % 2]], writes=[B_qT])
            for half in range(4):
                def mms(pe, half=half):
                    ins = None
                    for cc in range(4):
                        col = half * 4 + cc
                        ins = pe.matmul(qps[half % 2][:, cc * 128:(cc + 1) * 128], lhsT=qT[:, col, :], rhs=keyT[:, col // 2, col % 2, :],
                                        start=True, stop=True)
                    return ins
                P.op("pe", mms, reads=[B_qT, B_key], writes=[B_q[half % 2]])
                P.op("act", lambda a, half=half: a.activation(out=sc[:, half * 4:(half + 1) * 4, :].rearrange("p c i -> p (c i)"), in_=qps[half % 2][:, :],
                                                              func=AF.Copy, scale=r1),
                     reads=[B_q[half % 2], B_st], writes=[B_sc])
            for g in range(16):
                P.op("dve", lambda v, g=g: v.max(out=v16[:, g, 0:8], in_=sc[:, g, :]), reads=[B_sc], writes=[B_v16])
                P.op("dve", lambda v, g=g: v.max_index(out=i16[:, g, 0:8], in_max=v16[:, g, 0:8], in_values=sc[:, g, :]), reads=[B_sc, B_v16], writes=[B_i16])
                P.op("dve", lambda v, g=g: v.match_replace(out=sc2[:, g, :], in_to_replace=v16[:, g, 0:8], in_values=sc[:, g, :], imm_value=NEG),
                     reads=[B_sc, B_v16], writes=[B_sc2])
                P.op("dve", lambda v, g=g: v.max(out=v16[:, g, 8:16], in_=sc2[:, g, :]), reads=[B_sc2], writes=[B_v16])
                P.op("dve", lambda v, g=g: v.max_index(out=i16[:, g, 8:16], in_max=v16[:, g, 8:16], in_values=sc2[:, g, :]), reads=[B_sc2, B_v16], writes=[B_i16])
            P.op("dve", lambda v: v.tensor_copy(out=i16f[:, :, :], in_=i16[:, :, :]), reads=[B_i16], writes=[B_i16f])
            v16v = v16[:, :, :].rearrange("p (h j) k -> p h j k", j=2)
            P.op("dve", lambda v: v.tensor_tensor(out=cand[:, :, :].rearrange("p h (a b) -> p h a b", b=16),
                                                  in0=v16v[:, :, 0, :].unsqueeze(3).to_broadcast([128, 8, 16, 16]),
                                                  in1=v16v[:, :, 1, :].unsqueeze(2).to_broadcast([128, 8, 16, 16]), op=ALU.add),
                 reads=[B_v16], writes=[B_cand])
            for h in range(8):
                P.op("dve", lambda v, h=h: v.max(out=tv[:, h, 0:8], in_=cand[:, h, :]), reads=[B_cand], writes=[B_tv])
                P.op("dve", lambda v, h=h: v.max_index(out=tpos[:, h, 0:8], in_max=tv[:, h, 0:8], in_values=cand[:, h, :]), reads=[B_cand, B_tv], writes=[B_tpos])
                P.op("dve", lambda v, h=h: v.match_replace(out=cand2[:, h, :], in_to_replace=tv[:, h, 0:8], in_values=cand[:, h, :], imm_value=NEG),
                     reads=[B_cand, B_tv], writes=[B_cand2])
                P.op("dve", lambda v, h=h: v.max(out=tv[:, h, 8:16], in_=cand2[:, h, :]), reads=[B_cand2], writes=[B_tv])
                P.op("dve", lambda v, h=h: v.max_index(out=tpos[:, h, 8:16], in_max=tv[:, h, 8:16], in_values=cand2[:, h, :]), reads=[B_cand2, B_tv], writes=[B_tpos])
            tposf = tpos[:, :, :].rearrange("p h k -> p (h k)")
            P.op("dve", lambda v: v.tensor_scalar(out=ta[:, 0, :], in0=tposf, scalar1=4, scalar2=None, op0=ALU.logical_shift_right), reads=[B_tpos], writes=[B_ta])
            P.op("dve", lambda v: v.tensor_scalar(out=ta[:, 1, :], in0=tposf, scalar1=15, scalar2=None, op0=ALU.bitwise_and), reads=[B_tpos], writes=[B_ta])
            P.op("dve", lambda v: v.tensor_copy(out=taf[:, :, :], in_=ta[:, :, :]), reads=[B_ta], writes=[B_taf])
            i16v = i16f[:, :, :].rearrange("p (h j) k -> p h j k", j=2)
            for j in range(2):
                P.op("dve", lambda v, j=j: v.tensor_tensor(out=oh[:, :, :], in0=taf[:, j, :].unsqueeze(2).to_broadcast([128, 128, 16]),
                                                           in1=iota16[:, :].unsqueeze(1).to_broadcast([128, 128, 16]), op=ALU.is_equal),
                     reads=[B_taf, B_const], writes=[B_oh])
                P.op("dve", lambda v, j=j: v.tensor_tensor(out=oh2[:, :, :].rearrange("p (h k) a -> p h k a", k=16),
                                                           in0=oh[:, :, :].rearrange("p (h k) a -> p h k a", k=16),
                                                           in1=i16v[:, :, j, :].unsqueeze(2).to_broadcast([128, 8, 16, 16]), op=ALU.mult),
                     reads=[B_oh, B_i16f], writes=[B_oh2])
                P.op("dve", lambda v, j=j: v.tensor_reduce(out=isel[:, j, :], in_=oh2[:, :, :], axis=AX.X, op=ALU.add), reads=[B_oh2], writes=[B_isel])
            P.op("dve", lambda v: v.scalar_tensor_tensor(out=eidf[:, :], in0=isel[:, 0, :], scalar=128.0, in1=isel[:, 1, :], op0=ALU.mult, op1=ALU.add),
                 reads=[B_isel], writes=[B_eidf])
            P.op("dve", lambda v: v.tensor_scalar(out=eid[:, :], in0=eidf[:, :], scalar1=float(l * 16384), scalar2=None, op0=ALU.add),
                 reads=[B_eidf], writes=[B_eid])
            P.op("dve", lambda v: v.tensor_tensor(out=gt[:, :, :], in0=tv[:, :, :], in1=tv[:, :, 0:1].to_broadcast([128, 8, 16]), op=ALU.subtract),
                 reads=[B_tv], writes=[B_gt])
            P.op("act", lambda a: a.activation(out=gt[:, :, :].rearrange("p h k -> p (h k)"), in_=gt[:, :, :].rearrange("p h k -> p (h k)"), func=AF.Exp),
                 reads=[B_gt], writes=[B_gt])
            P.op("dve", lambda v: v.tensor_reduce(out=gs[:, 0:8], in_=gt[:, :, :], axis=AX.X, op=ALU.add), reads=[B_gt], writes=[B_gs])
            P.op("dve", lambda v: v.reciprocal(out=gs[:, 8:16], in_=gs[:, 0:8]), reads=[B_gs], writes=[B_gs])
            P.op("dve", lambda v: v.tensor_tensor(out=gt[:, :, :], in0=gt[:, :, :], in1=gs[:, 8:16].unsqueeze(2).to_broadcast([128, 8, 16]), op=ALU.mult),
                 reads=[B_gt, B_gs], writes=[B_gt])
            for hk in range(128):
                s_ = hk % NSLOT
                P.dma("pool", lambda g, hk=hk, s_=s_: g.indirect_dma_start(
                    out=ug[s_][:, :], out_offset=None, in_=I["peer_u"].rearrange("l e d -> (l e) d"),
                    in_offset=bass.IndirectOffsetOnAxis(ap=eid[:, hk:hk + 1], axis=0)),
                    B_ug[s_], reads=[B_eid], writes=[B_ug[s_]])
                P.op("dve", lambda v, hk=hk, s_=s_: v.scalar_tensor_tensor(
                    out=prod[:, :], in0=ug[s_][:, :], scalar=1.0, in1=xn[:, :], op0=ALU.mult, op1=ALU.mult,
                    accum_out=hraw[:, hk:hk + 1]),
                    reads=[B_ug[s_], B_xn], writes=[B_prod, B_hraw])
            P.op("act", lambda a: a.activation(out=hg[:, :], in_=hraw[:, :], func=AF.Gelu), reads=[B_hraw], writes=[B_hg])
            P.op("dve", lambda v: v.tensor_tensor(out=wgt[:, :], in0=hg[:, :], in1=gt[:, :, :].rearrange("p h k -> p (h k)"), op=ALU.mult),
                 reads=[B_hg, B_gt], writes=[B_wgt])
            for hk in range(128):
                s_ = hk % NSLOT
                d_ = hk % 2
                P.dma("pool", lambda g, hk=hk, s_=s_: g.indirect_dma_start(
                    out=vg[s_][:, :], out_offset=None, in_=I["peer_v"].rearrange("l e d -> (l e) d"),
                    in_offset=bass.IndirectOffsetOnAxis(ap=eid[:, hk:hk + 1], axis=0)),
                    B_vg[s_], reads=[B_eid], writes=[B_vg[s_]])
                P.op("act", lambda a, s_=s_, d_=d_: a.copy(out=vgb[d_][:, :], in_=vg[s_][:, :]), reads=[B_vg[s_]], writes=[B_vgb[d_]])
                P.op("dve", lambda v, hk=hk, d_=d_: v.tensor_scalar(out=dg[d_][:, :], in0=identf[:, :], scalar1=wgt[:, hk:hk + 1], scalar2=None, op0=ALU.mult),
                     reads=[B_wgt, B_const], writes=[B_dg[d_]])

                def mmv(pe, hk=hk, d_=d_):
                    ins = None
                    for g in range(2):
                        ins = pe.matmul(aps[g][:, :], lhsT=dg[d_][:, :], rhs=vgb[d_][:, g * 512:(g + 1) * 512], start=(hk == 0), stop=(hk == 127))
                    return ins
                P.op("pe", mmv, reads=[B_dg[d_], B_vgb[d_]], writes=B_acc)
            o = t % 2
            for g in range(2):
                P.op("dve", lambda v, g=g: v.tensor_tensor(out=xo[o][:, g * 512:(g + 1) * 512], in0=aps[g][:, :], in1=x1[:, g * 512:(g + 1) * 512], op=ALU.add),
                     reads=[B_acc[g], B_x1], writes=[B_xo[o]])
            P.dma("sp", lambda q: q.dma_start(out=x_dst[tok, :], in_=xo[o][:, :]), B_xo[o], reads=[B_xo[o]])


def prepass_tables(nc, P, I, l, UT, VB, ident, B_const):
    with contextlib.ExitStack() as es:
        def sb(name, shape, dt=F32):
            return es.enter_context(nc.sbuf_tensor("t" + str(l) + "_" + name, list(shape), dt))

        def ps(name, shape, dt=F32):
            return es.enter_context(nc.psum_tensor("t" + str(l) + "_" + name, list(shape), dt))

        ru = [sb("ru%d" % i, [128, D]) for i in range(3)]; B_ru = [P.buf("ru%d" % i) for i in range(3)]
        rv = [sb("rv%d" % i, [128, D]) for i in range(3)]; B_rv = [P.buf("rv%d" % i) for i in range(3)]
        rub = [sb("rub%d" % i, [128, D], BF16) for i in range(2)]; B_rub = [P.buf("rub%d" % i) for i in range(2)]
        utb = [sb("utb%d" % i, [128, D], BF16) for i in range(2)]; B_utb = [P.buf("utb%d" % i) for i in range(2)]
        vbb = [sb("vbb%d" % i, [128, D], BF16) for i in range(2)]; B_vbb = [P.buf("vbb%d" % i) for i in range(2)]
        tp = [ps("tp%d" % i, [128, 1024], BF16) for i in range(2)]; B_tp = [P.buf("ttp%d" % i) for i in range(2)]
        Usrc = I["peer_u"][l].rearrange("(i1 i2) d -> i2 i1 d", i2=128)
        Vsrc = I["peer_v"][l].rearrange("(i1 i2) d -> i2 i1 d", i2=128)

        def load(i2):
            b = i2 % 3
            P.dma("sp", lambda q: q.dma_start(out=ru[b][:, :], in_=Usrc[i2]), B_ru[b], writes=[B_ru[b]])
            P.dma("sp", lambda q: q.dma_start(out=rv[b][:, :], in_=Vsrc[i2]), B_rv[b], writes=[B_rv[b]])

        load(0)
        load(1)
        for i2 in range(128):
            if i2 + 2 < 128:
                load(i2 + 2)
            b3 = i2 % 3
            b2 = i2 % 2
            e1, e2 = ("act", "dve") if i2 % 2 == 0 else ("dve", "act")

            def cast(eng, out, in_):
                if eng == "act":
                    return lambda a: a.copy(out=out, in_=in_)
                return lambda v: v.tensor_copy(out=out, in_=in_)
            P.op(e1, cast(e1, rub[b2][:, :], ru[b3][:, :]), reads=[B_ru[b3]], writes=[B_rub[b2]])

            def tr(pe):
                ins = None
                for c in range(8):
                    ins = pe.transpose(out=tp[b2][:, c * 128:(c + 1) * 128], in_=rub[b2][:, c * 128:(c + 1) * 128], identity=ident[:, :])
                return ins
            P.op("pe", tr, reads=[B_rub[b2], B_const], writes=[B_tp[b2]])
            P.op(e2, cast(e2, utb[b2][:, :], tp[b2][:, :]), reads=[B_tp[b2]], writes=[B_utb[b2]])
            P.dma("sp", lambda q: q.dma_start(out=UT[i2], in_=utb[b2][:, :]), B_utb[b2], reads=[B_utb[b2]])
            P.op(e1, cast(e1, vbb[b2][:, :], rv[b3][:, :]), reads=[B_rv[b3]], writes=[B_vbb[b2]])
            P.dma("sp", lambda q: q.dma_start(out=VB[i2], in_=vbb[b2][:, :]), B_vbb[b2], reads=[B_vbb[b2]])


def phase_c_dense(nc, P, I, sq, l, x_src, x_dst, ident, identf, iota16, iota128, B_const, rsqrt, UT, VB):
    S = sq["S"]
    NT = S // 128
    G = 256
    NG = NT // 2
    pfx = "c" + str(l) + sq["nm"] + "_"
    with contextlib.ExitStack() as es:
        def sb(name, shape, dt=F32):
            return es.enter_context(nc.sbuf_tensor(pfx + name, list(shape), dt))

        def ps(name, shape, dt=F32):
            return es.enter_context(nc.psum_tensor(pfx + name, list(shape), dt))

        wout = sb("wout", [128, 8, D], BF16); B_wout = P.buf("wout")
        wq = sb("wq", [128, 8, 2048], BF16); B_wq = P.buf("wq")
        keyT = sb("keyT", [128, 8, 2, 128], BF16); B_key = P.buf("keyT")
        gffnb = sb("gffnb", [128, D], F32)
        B_g = P.buf("cg")
        Wg = sb("Wg", [128, G, 128], BF16); B_Wg = P.buf("Wg")
        xnT = [sb("xnT%d" % i, [128, 8, G], BF16) for i in range(2)]; B_xnT = [P.buf("xnT0"), P.buf("xnT1")]
        x1g = [sb("x1g%d" % i, [128, 2, D]) for i in range(2)]; B_x1g = [P.buf("x1g0"), P.buf("x1g1")]
        selg = sb("selg", [128, 2, 3, 128]); B_selg = P.buf("selg")

        es_w = contextlib.ExitStack()
        stg = [es_w.enter_context(nc.sbuf_tensor(pfx + "stg" + str(i), [128, 2048], F32)) for i in range(2)]
        B_stg = [P.buf("cstg0"), P.buf("cstg1")]
        P.dma("sp", lambda q: q.dma_start(out=gffnb[:, :], in_=I["norm_ffn_g"][l].partition_broadcast(128)), B_g, writes=[B_g])
        k = 0
        for c in range(8):
            b = k % 2
            P.dma("sp", lambda q, c=c, b=b: q.dma_start(out=stg[b][:, 0:D], in_=I["w_out"][l, c * 128:(c + 1) * 128, :]), B_stg[b], writes=[B_stg[b]])
            P.op("dve", lambda v, c=c, b=b: v.tensor_copy(out=wout[:, c, :], in_=stg[b][:, 0:D]), reads=[B_stg[b]], writes=[B_wout])
            k += 1
        for c in range(8):
            b = k % 2
            P.dma("sp", lambda q, c=c, b=b: q.dma_start(out=stg[b][:, :], in_=I["peer_w_q"][l, c * 128:(c + 1) * 128, :]), B_stg[b], writes=[B_stg[b]])
            P.op("act", lambda a, c=c, b=b: a.copy(out=wq[:, c, :], in_=stg[b][:, :]), reads=[B_stg[b]], writes=[B_wq])
            k += 1
        for j, nm in enumerate(("peer_key1T", "peer_key2T")):
            b = k % 2
            P.dma("sp", lambda q, nm=nm, b=b: q.dma_start(out=stg[b][:, 0:1024].rearrange("p (h i) -> p h i", i=128),
                                                         in_=I[nm][l].rearrange("h d i -> d h i")), B_stg[b], writes=[B_stg[b]])
            P.op("dve", lambda v, j=j, b=b: v.tensor_copy(out=keyT[:, :, j, :], in_=stg[b][:, 0:1024].rearrange("p (h i) -> p h i", i=128)),
                 reads=[B_stg[b]], writes=[B_key])
            k += 1
        P.barrier()
        es_w.close()

        mx = sb("mx", [128, D], BF16); B_mx = P.buf("mx")
        mixT = sb("mixT", [128, 8, 128], BF16); B_mixT = P.buf("mixT")
        st = sb("st", [128, 16]); B_st = P.buf("cst")
        xnb = sb("xnb", [128, D], BF16); B_xnb = P.buf("xnb")
        qT = sb("qT", [128, 16, 128], BF16); B_qT = P.buf("qT")
        sc = sb("sc", [128, 16, 128]); B_sc = P.buf("sc")
        sc2 = sb("sc2", [128, 16, 128]); B_sc2 = P.buf("sc2")
        v16 = sb("v16", [128, 16, 16]); B_v16 = P.buf("v16")
        i16 = sb("i16", [128, 16, 16], U32); B_i16 = P.buf("i16")
        i16f = sb("i16f", [128, 16, 16]); B_i16f = P.buf("i16f")
        junk = sc2[:, :, :].rearrange("p g i -> p (g i)")[:, 0:D]; B_junk = B_sc2
        cand = sc2[:, :, :].rearrange("p (h x) i -> p h (x i)", x=2); B_cand = B_sc2
        cand2 = sc[:, :, :].rearrange("p (h x) i -> p h (x i)", x=2); B_cand2 = B_sc
        oh = sc2[:, :, :].rearrange("p g (x a) -> p (g x) a", a=16); B_oh = B_sc2
        oh2 = sc[:, :, :].rearrange("p g (x a) -> p (g x) a", a=16); B_oh2 = B_sc
        tv = sb("tv", [128, 8, 16]); B_tv = P.buf("tv")
        tpos = sb("tpos", [128, 8, 16], U32); B_tpos = P.buf("tpos")
        ta = sb("ta", [128, 2, 128], U32); B_ta = P.buf("ta")
        taf = sb("taf", [128, 2, 128]); B_taf = P.buf("taf")
        gt = sb("gt", [128, 8, 16]); B_gt = P.buf("gt")
        gs = sb("gs", [128, 16]); B_gs = P.buf("gs")
        selT = sb("selT", [128, 3, 128]); B_selT = P.buf("selT")
        QN = 4
        Aoh = [sb("A%d" % i, [128, QN, 128], BF16) for i in range(2)]; B_A = [P.buf("A0"), P.buf("A1")]
        Boh = [sb("B%d" % i, [128, QN, 128], BF16) for i in range(2)]; B_B = [P.buf("B0"), P.buf("B1")]
        NBUF = 4
        utb = [sb("ut%d" % i, [128, 8, 128], BF16) for i in range(NBUF)]; B_ut = [P.buf("ut%d" % i) for i in range(NBUF)]
        vbb = [sb("vb%d" % i, [128, D], BF16) for i in range(NBUF)]; B_vb = [P.buf("vb%d" % i) for i in range(NBUF)]
        Hs = [sb("Hs%d" % i, [128, G], BF16) for i in range(2)]; B_Hs = [P.buf("Hs0"), P.buf("Hs1")]
        WH = [sb("WH%d" % i, [128, G], BF16) for i in range(2)]; B_WH = [P.buf("WH0"), P.buf("WH1")]
        tpq = ps("tpq", [128, 512]); B_tpq = P.buf("tpq")
        tpq2 = ps("tpq2", [128, 512]); B_tpq2 = P.buf("tpq2")
        tp = tpq[:, :].bitcast(BF16)
        hps = [ps("h%d" % i, [128, 512]) for i in range(2)]; B_h = [P.buf("hp0"), P.buf("hp1")]
        acc = [ps("acc%d" % i, [128, 512]) for i in range(4)]; B_acc = [P.buf("acc%d" % i) for i in range(4)]
        fps = [tpq, tpq2]; B_f = [B_tpq, B_tpq2]

        def tr8(pe, src):
            ins = None
            for c in range(8):
                ins = pe.transpose(out=tp[:, c * 128:(c + 1) * 128], in_=src[:, c * 128:(c + 1) * 128], identity=ident[:, :])
            return ins

        def front(grp):
            gb = grp % 2
            for tt in range(2):
                t = grp * 2 + tt
                x1 = x1g[gb][:, tt, :]
                BX1 = B_x1g[gb]
                P.dma("sp", lambda q: q.dma_start(out=x1, in_=x_src[t * 128:(t + 1) * 128, :]), BX1, writes=[BX1])
                P.dma("sp", lambda q: q.dma_start(out=mx[:, :], in_=sq["mix"][t * 128:(t + 1) * 128, :]), B_mx, writes=[B_mx])
                yield GAP
                yield GAP
                P.op("pe", lambda pe: tr8(pe, mx), reads=[B_mx, B_const], writes=[B_tpq])
                P.op("dve", lambda v: v.tensor_copy(out=mixT[:, :, :].rearrange("p c t -> p (c t)"), in_=tp), reads=[B_tpq], writes=[B_mixT])
                yield GAP
                for g in range(2):
                    def mm(pe, g=g):
                        ins = None
                        for c in range(8):
                            ins = pe.matmul(fps[g][:, :], lhsT=mixT[:, c, :], rhs=wout[:, c, g * 512:(g + 1) * 512], start=(c == 0), stop=(c == 7))
                        return ins
                    P.op("pe", mm, reads=[B_mixT, B_wout], writes=[B_f[g]])
                    P.op("dve", lambda v, g=g: v.tensor_tensor(out=x1[:, g * 512:(g + 1) * 512], in0=fps[g][:, :], in1=x1[:, g * 512:(g + 1) * 512], op=ALU.add),
                         reads=[B_f[g], BX1], writes=[BX1])
                    yield
                P.op("act", lambda a: a.activation(out=junk, in_=x1, func=AF.Square, accum_out=st[:, 0:1]), reads=[BX1], writes=[B_junk, B_st])
                P.op("dve", lambda v: v.tensor_scalar(out=st[:, 1:2], in0=st[:, 0:1], scalar1=1.0 / D, scalar2=EPS, op0=ALU.mult, op1=ALU.add),
                     reads=[B_st], writes=[B_st])
                rsqrt(st[:, 2:3], st[:, 1:2], 1, B_st, B_st)
                r1 = st[:, 2:3]
                yield
                P.op("dve", lambda v: v.scalar_tensor_tensor(out=xnb[:, :], in0=x1, scalar=r1, in1=gffnb[:, :], op0=ALU.mult, op1=ALU.mult),
                     reads=[BX1, B_st, B_g], writes=[B_xnb])
                yield GAP
                P.op("pe", lambda pe: tr8(pe, xnb), reads=[B_xnb, B_const], writes=[B_tpq])
                P.op("dve", lambda v: v.tensor_copy(out=xnT[gb][:, :, tt * 128:(tt + 1) * 128], in_=tp.rearrange("p (c t) -> p c t", t=128)),
                     reads=[B_tpq], writes=[B_xnT[gb]])
                yield GAP
                for half in range(4):
                    def mmq(pe, half=half):
                        ins = None
                        for cc in range(4):
                            col = half * 4 + cc
                            for c in range(8):
                                ins = pe.matmul(fps[half % 2][:, cc * 128:(cc + 1) * 128], lhsT=wq[:, c, col * 128:(col + 1) * 128],
                                                rhs=xnT[gb][:, c, tt * 128:(tt + 1) * 128], start=(c == 0), stop=(c == 7))
                        return ins
                    P.op("pe", mmq, reads=[B_xnT[gb], B_wq], writes=[B_f[half % 2]])
                    P.op("dve", lambda v, half=half: v.tensor_copy(out=qT[:, half * 4:(half + 1) * 4, :].rearrange("p c t -> p (c t)"), in_=fps[half % 2][:, :]),
                         reads=[B_f[half % 2]], writes=[B_qT])
                    yield GAP
                for half in range(4):
                    def mms(pe, half=half):
                        ins = None
                        for cc in range(4):
                            col = half * 4 + cc
                            ins = pe.matmul(fps[half % 2][:, cc * 128:(cc + 1) * 128], lhsT=qT[:, col, :], rhs=keyT[:, col // 2, col % 2, :],
                                            start=True, stop=True)
                        return ins
                    P.op("pe", mms, reads=[B_qT, B_key], writes=[B_f[half % 2]])
                    P.op("dve", lambda v, half=half: v.tensor_copy(out=sc[:, half * 4:(half + 1) * 4, :].rearrange("p c i -> p (c i)"), in_=fps[half % 2][:, :]),
                         reads=[B_f[half % 2]], writes=[B_sc])
                    yield GAP
                for g in range(16):
                    P.op("dve", lambda v, g=g: v.max(out=v16[:, g, 0:8], in_=sc[:, g, :]), reads=[B_sc], writes=[B_v16])
                    P.op("dve", lambda v, g=g: v.max_index(out=i16[:, g, 0:8], in_max=v16[:, g, 0:8], in_values=sc[:, g, :]), reads=[B_sc, B_v16], writes=[B_i16])
                    P.op("dve", lambda v, g=g: v.match_replace(out=sc2[:, g, :], in_to_replace=v16[:, g, 0:8], in_values=sc[:, g, :], imm_value=NEG),
                         reads=[B_sc, B_v16], writes=[B_sc2])
                    yield
                    P.op("dve", lambda v, g=g: v.max(out=v16[:, g, 8:16], in_=sc2[:, g, :]), reads=[B_sc2], writes=[B_v16])
                    P.op("dve", lambda v, g=g: v.max_index(out=i16[:, g, 8:16], in_max=v16[:, g, 8:16], in_values=sc2[:, g, :]), reads=[B_sc2, B_v16], writes=[B_i16])
                    yield
                P.op("dve", lambda v: v.tensor_copy(out=i16f[:, :, :], in_=i16[:, :, :]), reads=[B_i16], writes=[B_i16f])
                v16v = v16[:, :, :].rearrange("p (h j) k -> p h j k", j=2)
                P.op("dve", lambda v: v.tensor_tensor(out=cand.rearrange("p h (a b) -> p h a b", b=16),
                                                      in0=v16v[:, :, 0, :].unsqueeze(3).to_broadcast([128, 8, 16, 16]),
                                                      in1=v16v[:, :, 1, :].unsqueeze(2).to_broadcast([128, 8, 16, 16]), op=ALU.add),
                     reads=[B_v16], writes=[B_cand])
                yield
                for h in range(8):
                    P.op("dve", lambda v, h=h: v.max(out=tv[:, h, 0:8], in_=cand[:, h, :]), reads=[B_cand], writes=[B_tv])
                    P.op("dve", lambda v, h=h: v.max_index(out=tpos[:, h, 0:8], in_max=tv[:, h, 0:8], in_values=cand[:, h, :]), reads=[B_cand, B_tv], writes=[B_tpos])
                    P.op("dve", lambda v, h=h: v.match_replace(out=cand2[:, h, :], in_to_replace=tv[:, h, 0:8], in_values=cand[:, h, :], imm_value=NEG),
                         reads=[B_cand, B_tv], writes=[B_cand2])
                    yield
                    P.op("dve", lambda v, h=h: v.max(out=tv[:, h, 8:16], in_=cand2[:, h, :]), reads=[B_cand2], writes=[B_tv])
                    P.op("dve", lambda v, h=h: v.max_index(out=tpos[:, h, 8:16], in_max=tv[:, h, 8:16], in_values=cand2[:, h, :]), reads=[B_cand2, B_tv], writes=[B_tpos])
                    yield
                tposf = tpos[:, :, :].rearrange("p h k -> p (h k)")
                P.op("dve", lambda v: v.tensor_scalar(out=ta[:, 0, :], in0=tposf, scalar1=4, scalar2=None, op0=ALU.logical_shift_right), reads=[B_tpos], writes=[B_ta])
                P.op("dve", lambda v: v.tensor_scalar(out=ta[:, 1, :], in0=tposf, scalar1=15, scalar2=None, op0=ALU.bitwise_and), reads=[B_tpos], writes=[B_ta])
                P.op("dve", lambda v: v.tensor_copy(out=taf[:, :, :], in_=ta[:, :, :]), reads=[B_ta], writes=[B_taf])
                yield
                i16v = i16f[:, :, :].rearrange("p (h j) k -> p h j k", j=2)
                for j in range(2):
                    P.op("dve", lambda v, j=j: v.tensor_tensor(out=oh, in0=taf[:, j, :].unsqueeze(2).to_broadcast([128, 128, 16]),
                                                               in1=iota16[:, :].unsqueeze(1).to_broadcast([128, 128, 16]), op=ALU.is_equal),
                         reads=[B_taf, B_const], writes=[B_oh])
                    yield
                    P.op("dve", lambda v, j=j: v.tensor_tensor(out=oh2.rearrange("p (h k) a -> p h k a", k=16),
                                                               in0=oh.rearrange("p (h k) a -> p h k a", k=16),
                                                               in1=i16v[:, :, j, :].unsqueeze(2).to_broadcast([128, 8, 16, 16]), op=ALU.mult),
                         reads=[B_oh, B_i16f], writes=[B_oh2])
                    yield
                    P.op("dve", lambda v, j=j: v.tensor_reduce(out=selg[:, tt, j, :], in_=oh2, axis=AX.X, op=ALU.add), reads=[B_oh2], writes=[B_selg])
                    yield
                P.op("dve", lambda v: v.tensor_tensor(out=gt[:, :, :], in0=tv[:, :, :], in1=tv[:, :, 0:1].to_broadcast([128, 8, 16]), op=ALU.subtract),
                     reads=[B_tv], writes=[B_gt])
                P.op("act", lambda a: a.activation(out=gt[:, :, :].rearrange("p h k -> p (h k)"), in_=gt[:, :, :].rearrange("p h k -> p (h k)"), func=AF.Exp),
                     reads=[B_gt], writes=[B_gt])
                P.op("dve", lambda v: v.tensor_reduce(out=gs[:, 0:8], in_=gt[:, :, :], axis=AX.X, op=ALU.add), reads=[B_gt], writes=[B_gs])
                P.op("dve", lambda v: v.reciprocal(out=gs[:, 8:16], in_=gs[:, 0:8]), reads=[B_gs], writes=[B_gs])
                P.op("dve", lambda v: v.tensor_tensor(out=selg[:, tt, 2, :].rearrange("p (h k) -> p h k", k=16), in0=gt[:, :, :],
                                                      in1=gs[:, 8:16].unsqueeze(2).to_broadcast([128, 8, 16]), op=ALU.mult),
                     reads=[B_gt, B_gs], writes=[B_selg])
                yield

        def build(grp):
            ev = 0
            for tt in range(2):
                def trs(pe):
                    ins = None
                    for k3 in range(3):
                        ins = pe.transpose(out=tpq[:, k3 * 128:(k3 + 1) * 128], in_=selg[:, tt, k3, :], identity=identf[:, :])
                    return ins
                P.op("pe", trs, reads=[B_selg, B_const], writes=[B_tpq])
                P.op("act", lambda a: a.copy(out=selT[:, :, :].rearrange("p k c -> p (k c)"), in_=tpq[:, 0:384]), reads=[B_tpq], writes=[B_selT])
                for qi in range(128 // QN):
                    qb = qi % 2

                    def onehotA(v, qi=qi, qb=qb):
                        ins = None
                        for ci in range(QN):
                            c = qi * QN + ci
                            ins = v.tensor_scalar(out=Aoh[qb][:, ci, :], in0=iota128[:, :], scalar1=selT[:, 0, c:c + 1], scalar2=selT[:, 2, c:c + 1],
                                                  op0=ALU.is_equal, op1=ALU.mult)
                        return ins

                    def onehotB(v, qi=qi, qb=qb):
                        c0 = qi * QN
                        return v.tensor_tensor(out=Boh[qb][:, :, :], in0=iota128[:, :].unsqueeze(1).to_broadcast([128, QN, 128]),
                                               in1=selT[:, 1, c0:c0 + QN].unsqueeze(2).to_broadcast([128, QN, 128]), op=ALU.is_equal)
                    P.op("dve", onehotA, reads=[B_selT, B_const], writes=[B_A[qb]])
                    P.op(ONEHOT_B_ENG, onehotB, reads=[B_selT, B_const], writes=[B_B[qb]])
                    for c4 in range(QN // 4):
                        wb = ev % 2
                        wp = hps[wb]
                        BW = B_h[wb]

                        def mmw(pe, c4=c4, wp=wp, qb=qb):
                            ins = None
                            for s4 in range(4):
                                ci = c4 * 4 + s4
                                ins = pe.matmul(wp[:, s4 * 128:(s4 + 1) * 128], lhsT=Aoh[qb][:, ci, :], rhs=Boh[qb][:, ci, :], start=True, stop=True)
                            return ins
                        P.op("pe", mmw, reads=[B_A[qb], B_B[qb]], writes=[BW])
                        c0 = tt * 128 + qi * QN + c4 * 4
                        dst = Wg[:, c0:c0 + 4, :].rearrange("p c i -> p (c i)")
                        P.op("act", lambda a, wp=wp, dst=dst: a.copy(out=dst, in_=wp[:, :]), reads=[BW], writes=[B_Wg])
                        ev += 1

        def main(grp, fgen):
            gb = grp % 2

            def load(i):
                b = i % NBUF
                P.dma("sp", lambda q: q.dma_start(out=utb[b][:, :, :].rearrange("p c i -> p (c i)"), in_=UT[i]), B_ut[b], writes=[B_ut[b]])
                P.dma("sp", lambda q: q.dma_start(out=vbb[b][:, :], in_=VB[i]), B_vb[b], writes=[B_vb[b]])

            def hmm(i):
                b = i % NBUF

                def f(pe):
                    ins = None
                    for c in range(8):
                        ins = pe.matmul(hps[i % 2][:, 0:G], lhsT=utb[b][:, c, :], rhs=xnT[gb][:, c, :], start=(c == 0), stop=(c == 7))
                    return ins
                P.op("pe", f, reads=[B_ut[b], B_xnT[gb]], writes=[B_h[i % 2]])

            for i0 in range(NBUF - 1):
                load(i0)
            hmm(0)
            for i in range(128):
                if i + NBUF - 1 < 128:
                    load(i + NBUF - 1)
                if i + 1 < 128:
                    hmm(i + 1)
                p2 = i % 2
                P.op("act", lambda a: a.activation(out=Hs[p2][:, :], in_=hps[p2][:, 0:G], func=AF.Gelu), reads=[B_h[p2]], writes=[B_Hs[p2]])
                P.op(MULT_ENG, lambda v: v.tensor_tensor(out=WH[p2][:, :], in0=Hs[p2][:, :], in1=Wg[:, :, i], op=ALU.mult),
                     reads=[B_Hs[p2], B_Wg], writes=[B_WH[p2]])

                def omm(pe, i=i, p2=p2):
                    ins = None
                    b = i % NBUF
                    for cs in range(2):
                        for dh in range(2):
                            ins = pe.matmul(acc[cs * 2 + dh][:, :], lhsT=WH[p2][:, cs * 128:(cs + 1) * 128], rhs=vbb[b][:, dh * 512:(dh + 1) * 512],
                                            start=(i == 0), stop=(i == 127))
                    return ins
                P.op("pe", omm, reads=[B_WH[p2], B_vb[i % NBUF]], writes=B_acc)
                if fgen is not None:
                    for _ in range(FRONT_PER_CHUNK):
                        if next(fgen, None) is GAP:
                            break
            for cs in range(2):
                for dh in range(2):
                    P.op("dve", lambda v, cs=cs, dh=dh: v.tensor_tensor(out=x1g[gb][:, cs, dh * 512:(dh + 1) * 512], in0=acc[cs * 2 + dh][:, :],
                                                                        in1=x1g[gb][:, cs, dh * 512:(dh + 1) * 512], op=ALU.add),
                         reads=[B_acc[cs * 2 + dh], B_x1g[gb]], writes=[B_x1g[gb]])
            P.dma("sp", lambda q: q.dma_start(out=x_dst[grp * G:(grp + 1) * G, :].rearrange("(c p) d -> p c d", p=128), in_=x1g[gb][:, :, :]),
                  B_x1g[gb], reads=[B_x1g[gb]])

        for _ in front(0):
            pass
        for grp in range(NG):
            build(grp)
            fgen = front(grp + 1) if grp + 1 < NG else None
            main(grp, fgen)
            if fgen is not None:
                for _ in fgen:
                    pass


def make_constants(SM):
    c = {}
    c["ident"] = np.eye(128, dtype=np.float32).astype(ml_dtypes.bfloat16)
    c["identf"] = np.eye(128, dtype=np.float32)
    c["iota16"] = np.tile(np.arange(16, dtype=np.float32)[None, :], (128, 1))
    c["iota128"] = np.tile(np.arange(128, dtype=np.float32)[None, :], (128, 1))
    half = 16
    inv = (np.float32(10000.0) ** (-np.arange(half, dtype=np.float32) / np.float32(half))).astype(np.float32)
    ang = np.arange(SM, dtype=np.float32)[:, None] * inv[None, :]
    c["ropecs"] = np.concatenate([np.cos(ang), np.sin(ang)], axis=1).astype(np.float32)
    pos = np.arange(SM)
    hi = (pos // 128 * 128).astype(np.float32)
    lo = (pos % 128).astype(np.float32)
    one = np.ones(SM, np.float32)
    augq = np.zeros((4, 8, SM), np.float32)
    augk = np.zeros((4, 4, SM), np.float32)
    dbias = np.zeros((4, 128, 4, 512), np.float32)
    srel = np.arange(128)[:, None, None]
    jj = np.arange(4)[None, :, None]
    qrel = np.arange(512)[None, None, :]
    dist = np.abs(qrel - srel - 128 * jj).astype(np.float32)
    for h in range(4):
        m = np.float32(2.0 ** (-8.0 * (h + 1) / 4))
        augq[h, 0] = -m * hi; augq[h, 1] = -m * lo; augq[h, 2] = one; augq[h, 3] = one
        augq[h, 4] = m * hi; augq[h, 5] = m * lo; augq[h, 6] = -one; augq[h, 7] = -one
        augk[h, 0] = one; augk[h, 1] = one; augk[h, 2] = m * hi; augk[h, 3] = m * lo
        dbias[h] = -m * dist
    c["augq"] = augq.astype(ml_dtypes.bfloat16)
    c["augk"] = augk.astype(ml_dtypes.bfloat16)
    c["dbias"] = dbias
    return c


WEIGHT_NAMES = ["norm_mix_g", "w_in", "diff_q_norm_g", "diff_k_norm_g", "lam_q1", "lam_k1", "lam_q2", "lam_k2",
                "diff_subln_g", "mla_q_latent_g", "mla_w_uq", "mla_kv_latent_g", "mla_w_ukv", "mla_q_norm_g",
                "mla_k_norm_g", "w_out", "norm_ffn_g", "peer_w_q", "peer_u", "peer_v"]


def make_in_maps(inputs, n_cores, SP, SS, depth):
    consts = make_constants(max(SP, SS))
    shared = {}
    for n in WEIGHT_NAMES:
        shared[n] = np.ascontiguousarray(np.asarray(inputs[n], dtype=np.float32)[:depth])
    shared["peer_key1T"] = np.ascontiguousarray(np.transpose(np.asarray(inputs["peer_key1"], dtype=np.float32)[:depth], (0, 1, 3, 2)))
    shared["peer_key2T"] = np.ascontiguousarray(np.transpose(np.asarray(inputs["peer_key2"], dtype=np.float32)[:depth], (0, 1, 3, 2)))
    shared.update(consts)
    xp = np.asarray(inputs["x_prompt"], dtype=np.float32)
    xs = np.asarray(inputs["x_sample"], dtype=np.float32)
    maps = []
    for i in range(n_cores):
        m = dict(shared)
        m["xp"] = np.ascontiguousarray(xp[i])
        m["xs"] = np.ascontiguousarray(xs[i])
        maps.append(m)
    return maps


def kernel(**inputs):
    n = 8
    nc = build_program(SP_FULL, SS_FULL, DEPTH)
    in_maps = make_in_maps(inputs, n, SP_FULL, SS_FULL, DEPTH)
    res = run_bass_kernel_spmd(nc, in_maps, core_ids=list(range(n)))
    yp = np.stack([np.asarray(r["yp"], dtype=np.float32) for r in res.results], axis=0)
    ys = np.stack([np.asarray(r["ys"], dtype=np.float32) for r in res.results], axis=0)
    return (yp, ys)
```
